# Optimizing a Trainium2 kernel written in Bass

```python
import math
import jax
import jax.numpy as jnp
from jax import lax
import numpy as np

D_MODEL = 4096
BATCH = 2
SEQ = 8192
DEPTH = 2

GRID_W = 64
CTX_LEN = 256
N_MIXERS = 2
D_FF = 11008
FFN_CONV_K = 3
HY_ORDER = 2
HY_EMB_DIM = 33
HY_FILTER_HIDDEN = 64
HY_SHORT_K = 3
HY_DECAY_TARGET = 1e-2
HY_FAST_DECAY_PCT = 0.3
HY_SLOW_DECAY_PCT = 1.5
GDN_HEADS = 32
GDN_HEAD_DIM = D_MODEL // GDN_HEADS
GDN_CONV_K = 3
GDN_CHUNK = 64
LN_EPS = 1e-5
RMS_EPS = 1e-6
L2_EPS = 1e-6
DN_ALPHA = (2 * DEPTH) ** 0.25
DN_BETA = (8 * DEPTH) ** -0.25

kernel_name = 'hyena_gdn_convglu_hybrid_dit'


def layer_norm(x, g, b):
    xf = x.astype(jnp.float32)
    mu = jnp.mean(xf, axis=-1, keepdims=True)
    var = jnp.mean(jnp.square(xf - mu), axis=-1, keepdims=True)
    return ((xf - mu) * lax.rsqrt(var + LN_EPS) * g + b).astype(x.dtype)


def l2norm(t):
    return t * lax.rsqrt(jnp.sum(t * t, axis=-1, keepdims=True) + L2_EPS)


def adaln(cond, w, b):
    m = (jax.nn.silu(cond) @ w + b)[..., None, :]
    return jnp.split(m, 6, axis=-1)


def modulate(x, shift, scale):
    return x * (1 + scale) + shift


def post_norm(x, y, gate, g, b):
    return layer_norm(DN_ALPHA * x + gate * y, g, b)


def dwconv1d(x, w):
    return lax.conv_general_dilated(x, w[:, None, :].astype(x.dtype), window_strides=(1,), padding='SAME',
                                    dimension_numbers=('NWC', 'WIO', 'NWC'), feature_group_count=x.shape[-1])


def dwconv2d_grid(x, w, b, rows, cols):
    bsz, n_tok, ch = x.shape
    img = x.reshape(bsz, rows, cols, ch)
    y = lax.conv_general_dilated(img, w[:, :, None, :].astype(x.dtype), window_strides=(1, 1), padding='SAME',
                                 dimension_numbers=('NHWC', 'HWIO', 'NHWC'), feature_group_count=ch)
    return y.reshape(bsz, n_tok, ch) + b


def hyena_filters(L, w1, b1, w2, b2, w3, b3, w_out, freq):
    t = jnp.linspace(0.0, 1.0, L, dtype=jnp.float32)[:, None]
    bands = (HY_EMB_DIM - 1) // 2
    ang = 2.0 * math.pi * jnp.arange(L, dtype=jnp.float32)[:, None] / L
    f = jnp.linspace(1e-4, bands - 1, bands, dtype=jnp.float32)[None, :]
    z = jnp.concatenate([t, jnp.cos(f * ang), -jnp.sin(f * ang)], axis=-1)
    h = jnp.sin(freq * (z @ w1 + b1))
    h = jnp.sin(freq * (h @ w2 + b2))
    h = jnp.sin(freq * (h @ w3 + b3))
    h = (h @ w_out).astype(jnp.float32).reshape(L, 2 * HY_ORDER, D_MODEL)
    max_decay = math.log(HY_DECAY_TARGET) / HY_FAST_DECAY_PCT
    min_decay = math.log(HY_DECAY_TARGET) / HY_SLOW_DECAY_PCT
    deltas = jnp.abs(jnp.linspace(min_decay, max_decay, D_MODEL, dtype=jnp.float32))
    return h * jnp.exp(-t[:, :, None] * deltas)


def two_sided_fftconv(z, h_fwd, h_bwd, skip):
    L = z.shape[1]
    k = jnp.concatenate([h_fwd[:1] + h_bwd[:1], h_fwd[1:], jnp.zeros_like(h_fwd[:1]), h_bwd[:0:-1]], axis=0)
    zf = z.astype(jnp.float32)
    zk = jnp.fft.rfft(zf, n=2 * L, axis=1) * jnp.fft.rfft(k, n=2 * L, axis=0)[None]
    y = jnp.fft.irfft(zk, n=2 * L, axis=1)[:, :L]
    return (y + zf * skip).astype(z.dtype)


def hyena_mixer(h, p):
    L = h.shape[1]
    u = dwconv1d(h @ p['hy_w_in'] + p['hy_b_in'], p['hy_w_short']) + p['hy_b_short']
    parts = jnp.split(u, HY_ORDER + 1, axis=-1)
    filt = hyena_filters(L, p['hy_f_w1'], p['hy_f_b1'], p['hy_f_w2'], p['hy_f_b2'], p['hy_f_w3'],
                         p['hy_f_b3'], p['hy_f_wout'], p['hy_f_freq'])
    z = parts[-1]
    for o in range(HY_ORDER):
        z = parts[o] * two_sided_fftconv(z, filt[:, 2 * o], filt[:, 2 * o + 1], p['hy_skip'][o])
    return z @ p['hy_w_out'] + p['hy_b_out']


def gdn_chunked(q, k, v, g, beta, s0):
    bsz, nh, L, dk = q.shape
    dv = v.shape[-1]
    C = GDN_CHUNK
    n = L // C
    q, k, v = (t.reshape(bsz, nh, n, C, t.shape[-1]) for t in (q, k, v))
    g = jnp.cumsum(g.reshape(bsz, nh, n, C), axis=-1)
    beta = beta.reshape(bsz, nh, n, C)
    idx = jnp.arange(C)
    lower_incl = idx[:, None] >= idx[None, :]
    lower_strict = idx[:, None] > idx[None, :]
    decay = jnp.exp(jnp.where(lower_incl, g[..., :, None] - g[..., None, :], -jnp.inf))
    kb = k * beta[..., None]
    a_mat = jnp.where(lower_strict, jnp.einsum('bhnik,bhnjk->bhnij', kb, k) * decay, 0.0)
    rhs = jnp.concatenate([v * beta[..., None], kb * jnp.exp(g)[..., None]], axis=-1)
    sol = lax.linalg.triangular_solve(a_mat + jnp.eye(C, dtype=a_mat.dtype), rhs, left_side=True,
                                      lower=True, unit_diagonal=True)
    u, w = sol[..., :dv], sol[..., dv:]
    attn = jnp.where(lower_incl, jnp.einsum('bhnik,bhnjk->bhnij', q, k) * decay, 0.0)
    qg = q * jnp.exp(g)[..., None]
    kg = k * jnp.exp(g[..., -1:] - g)[..., None]
    g_last = jnp.exp(g[..., -1])

    def step(S, xs):
        qg_i, kg_i, u_i, w_i, attn_i, gl_i = xs
        v_new = u_i - jnp.einsum('bhck,bhkv->bhcv', w_i, S)
        o_i = jnp.einsum('bhck,bhkv->bhcv', qg_i, S) + jnp.einsum('bhij,bhjv->bhiv', attn_i, v_new)
        S = S * gl_i[..., None, None] + jnp.einsum('bhck,bhcv->bhkv', kg_i, v_new)
        return S, o_i

    xs = tuple(jnp.moveaxis(t, 2, 0) for t in (qg, kg, u, w, attn, g_last))
    s_last, o = lax.scan(step, s0, xs)
    return jnp.moveaxis(o, 0, 2).reshape(bsz, nh, L, dv), s_last


def gdn_mixer(h, p, s0_fwd, s0_bwd):
    bsz, L, _ = h.shape
    nh, dh = GDN_HEADS, GDN_HEAD_DIM
    proj = h @ p['gdn_w_in']
    qkv = jax.nn.silu(dwconv1d(proj[..., :3 * D_MODEL], p['gdn_w_conv']))
    z_gate = proj[..., 3 * D_MODEL:4 * D_MODEL].reshape(bsz, L, nh, dh).astype(jnp.float32)
    ab = proj[..., 4 * D_MODEL:].astype(jnp.float32).reshape(bsz, L, 2, 2, nh)
    g = -jnp.exp(p['gdn_a_log'].astype(jnp.float32)) * jax.nn.softplus(ab[:, :, 0] + p['gdn_dt_bias'])
    beta = jax.nn.sigmoid(ab[:, :, 1])
    g = g.transpose(2, 0, 3, 1)
    beta = beta.transpose(2, 0, 3, 1)

    def heads(t):
        return t.reshape(bsz, L, nh, dh).transpose(0, 2, 1, 3).astype(jnp.float32)

    q, k, v = (heads(t) for t in jnp.split(qkv, 3, axis=-1))
    q = l2norm(q) * dh ** -0.5
    k = l2norm(k)
    o_f, s_f = gdn_chunked(q, k, v, g[0], beta[0], s0_fwd)

    def rev(t):
        return jnp.flip(t, axis=2)

    o_b, s_b = gdn_chunked(rev(q), rev(k), rev(v), rev(g[1]), rev(beta[1]), s0_bwd)
    o = (o_f + rev(o_b)).transpose(0, 2, 1, 3)
    o = o * lax.rsqrt(jnp.mean(o * o, axis=-1, keepdims=True) + RMS_EPS) * p['gdn_norm_w'] * jax.nn.silu(z_gate)
    return o.reshape(bsz, L, D_MODEL).astype(h.dtype) @ p['gdn_w_out'], s_f, s_b


def conv_glu(h, p, rows, cols):
    gate, val = jnp.split(h @ p['ffn_w_up'], 2, axis=-1)
    gate = dwconv2d_grid(gate, p['ffn_w_dw'], p['ffn_b_dw'], rows, cols)
    return (jax.nn.gelu(gate, approximate=False) * val) @ p['ffn_w_down']


def setup_inputs(seed: int = 0) -> dict:
    key = jax.random.key(seed)
    ks = iter(jax.random.split(key, 64))
    D, F, H, FH = D_MODEL, D_FF, GDN_HEADS, HY_FILTER_HIDDEN

    def nrm(shape, std):
        return jax.random.normal(next(ks), shape, jnp.float32) * std

    inp = {}
    inp['x'] = nrm((BATCH, SEQ, D), 1.0)
    inp['c'] = nrm((BATCH, D), 1.0)
    inp['ctx'] = nrm((BATCH, CTX_LEN, D), 1.0)
    inp['c_ctx'] = nrm((D,), 1.0)
    for i in range(DEPTH):
        pre = 'l%d_' % i
        inp[pre + 'w_ada'] = nrm((D, 6 * D), 0.5 * D ** -0.5)
        inp[pre + 'b_ada'] = nrm((6 * D,), 0.01)
        inp[pre + 'ln1_g'] = 1.0 + nrm((D,), 0.02)
        inp[pre + 'ln1_b'] = nrm((D,), 0.02)
        inp[pre + 'ln2_g'] = 1.0 + nrm((D,), 0.02)
        inp[pre + 'ln2_b'] = nrm((D,), 0.02)
        if i % N_MIXERS == 0:
            inp[pre + 'hy_w_in'] = nrm((D, (HY_ORDER + 1) * D), D ** -0.5)
            inp[pre + 'hy_b_in'] = nrm(((HY_ORDER + 1) * D,), 0.02)
            inp[pre + 'hy_w_short'] = nrm((HY_SHORT_K, (HY_ORDER + 1) * D), HY_SHORT_K ** -0.5)
            inp[pre + 'hy_b_short'] = nrm(((HY_ORDER + 1) * D,), 0.02)
            inp[pre + 'hy_f_w1'] = nrm((HY_EMB_DIM, FH), HY_EMB_DIM ** -0.5)
            inp[pre + 'hy_f_b1'] = nrm((FH,), 0.02)
            inp[pre + 'hy_f_w2'] = nrm((FH, FH), FH ** -0.5)
            inp[pre + 'hy_f_b2'] = nrm((FH,), 0.02)
            inp[pre + 'hy_f_w3'] = nrm((FH, FH), FH ** -0.5)
            inp[pre + 'hy_f_b3'] = nrm((FH,), 0.02)
            inp[pre + 'hy_f_wout'] = nrm((FH, 2 * HY_ORDER * D), 0.02)
            inp[pre + 'hy_f_freq'] = 1.0 + nrm((FH,), 0.02)
            inp[pre + 'hy_skip'] = nrm((HY_ORDER, D), 1.0)
            inp[pre + 'hy_w_out'] = nrm((D, D), DN_BETA * D ** -0.5)
            inp[pre + 'hy_b_out'] = nrm((D,), 0.02)
        else:
            inp[pre + 'gdn_w_in'] = nrm((D, 4 * D + 4 * H), D ** -0.5)
            inp[pre + 'gdn_w_conv'] = nrm((GDN_CONV_K, 3 * D), GDN_CONV_K ** -0.5)
            inp[pre + 'gdn_a_log'] = jnp.log(jax.random.uniform(next(ks), (2, H), jnp.float32, 1.0, 16.0))
            dt = jnp.exp(jax.random.uniform(next(ks), (2, H), jnp.float32, math.log(1e-3), math.log(1e-1)))
            inp[pre + 'gdn_dt_bias'] = dt + jnp.log(-jnp.expm1(-dt))
            inp[pre + 'gdn_norm_w'] = 1.0 + nrm((GDN_HEAD_DIM,), 0.02)
            inp[pre + 'gdn_w_out'] = nrm((D, D), DN_BETA * D ** -0.5)
        inp[pre + 'ffn_w_up'] = nrm((D, 2 * F), D ** -0.5)
        inp[pre + 'ffn_w_dw'] = nrm((FFN_CONV_K, FFN_CONV_K, F), 1.0 / FFN_CONV_K)
        inp[pre + 'ffn_b_dw'] = nrm((F,), 0.02)
        inp[pre + 'ffn_w_down'] = nrm((F, D), DN_BETA * F ** -0.5)
    return inp


def reference(x, c, ctx, c_ctx,
              l0_w_ada, l0_b_ada, l0_ln1_g, l0_ln1_b, l0_ln2_g, l0_ln2_b,
              l0_hy_w_in, l0_hy_b_in, l0_hy_w_short, l0_hy_b_short,
              l0_hy_f_w1, l0_hy_f_b1, l0_hy_f_w2, l0_hy_f_b2, l0_hy_f_w3, l0_hy_f_b3,
              l0_hy_f_wout, l0_hy_f_freq, l0_hy_skip, l0_hy_w_out, l0_hy_b_out,
              l0_ffn_w_up, l0_ffn_w_dw, l0_ffn_b_dw, l0_ffn_w_down,
              l1_w_ada, l1_b_ada, l1_ln1_g, l1_ln1_b, l1_ln2_g, l1_ln2_b,
              l1_gdn_w_in, l1_gdn_w_conv, l1_gdn_a_log, l1_gdn_dt_bias, l1_gdn_norm_w, l1_gdn_w_out,
              l1_ffn_w_up, l1_ffn_w_dw, l1_ffn_b_dw, l1_ffn_w_down):
    layers = (
        dict(w_ada=l0_w_ada, b_ada=l0_b_ada, ln1_g=l0_ln1_g, ln1_b=l0_ln1_b, ln2_g=l0_ln2_g, ln2_b=l0_ln2_b,
             hy_w_in=l0_hy_w_in, hy_b_in=l0_hy_b_in, hy_w_short=l0_hy_w_short, hy_b_short=l0_hy_b_short,
             hy_f_w1=l0_hy_f_w1, hy_f_b1=l0_hy_f_b1, hy_f_w2=l0_hy_f_w2, hy_f_b2=l0_hy_f_b2,
             hy_f_w3=l0_hy_f_w3, hy_f_b3=l0_hy_f_b3, hy_f_wout=l0_hy_f_wout, hy_f_freq=l0_hy_f_freq,
             hy_skip=l0_hy_skip, hy_w_out=l0_hy_w_out, hy_b_out=l0_hy_b_out,
             ffn_w_up=l0_ffn_w_up, ffn_w_dw=l0_ffn_w_dw, ffn_b_dw=l0_ffn_b_dw, ffn_w_down=l0_ffn_w_down),
        dict(w_ada=l1_w_ada, b_ada=l1_b_ada, ln1_g=l1_ln1_g, ln1_b=l1_ln1_b, ln2_g=l1_ln2_g, ln2_b=l1_ln2_b,
             gdn_w_in=l1_gdn_w_in, gdn_w_conv=l1_gdn_w_conv, gdn_a_log=l1_gdn_a_log,
             gdn_dt_bias=l1_gdn_dt_bias, gdn_norm_w=l1_gdn_norm_w, gdn_w_out=l1_gdn_w_out,
             ffn_w_up=l1_ffn_w_up, ffn_w_dw=l1_ffn_w_dw, ffn_b_dw=l1_ffn_b_dw, ffn_w_down=l1_ffn_w_down),
    )
    rows = x.shape[1] // GRID_W
    ctx_len = ctx.shape[1]
    for i in range(DEPTH):
        p = layers[i]
        last = i == DEPTH - 1
        sh1, sc1, gt1, sh2, sc2, gt2 = adaln(c, p['w_ada'], p['b_ada'])
        csh1, csc1, cgt1, csh2, csc2, cgt2 = adaln(c_ctx, p['w_ada'], p['b_ada'])
        h_lat = modulate(x, sh1, sc1)
        h_ctx = modulate(ctx, csh1, csc1)
        if i % N_MIXERS == 0:
            y_lat = hyena_mixer(h_lat, p)
            y_ctx = None if last else hyena_mixer(h_ctx, p)
        else:
            s0 = jnp.zeros((ctx.shape[0], GDN_HEADS, GDN_HEAD_DIM, GDN_HEAD_DIM), jnp.float32)
            y_ctx, s_fwd, s_bwd = gdn_mixer(h_ctx, p, s0, s0)
            y_lat, _, _ = gdn_mixer(h_lat, p, s_fwd, s_bwd)
        x = post_norm(x, y_lat, gt1, p['ln1_g'], p['ln1_b'])
        x = post_norm(x, conv_glu(modulate(x, sh2, sc2), p, rows, GRID_W), gt2, p['ln2_g'], p['ln2_b'])
        if not last:
            ctx = post_norm(ctx, y_ctx, cgt1, p['ln1_g'], p['ln1_b'])
            ctx = post_norm(ctx, conv_glu(modulate(ctx, csh2, csc2), p, 1, ctx_len), cgt2,
                            p['ln2_g'], p['ln2_b'])
    return x
```

```python
import contextlib
import numpy as np
import concourse.bass as bass
import concourse.mybir as mybir
from concourse.bass_utils import run_bass_kernel_spmd

F32 = mybir.dt.float32
BF16 = mybir.dt.bfloat16
ALU = mybir.AluOpType
AF = mybir.ActivationFunctionType
AX = mybir.AxisListType

ENGS = ("tensor", "vector", "scalar", "gpsimd", "sync")
N_DMA_SEMS = 8


class Buf:
    __slots__ = ("t", "name", "last_w", "readers", "excl")

    def __init__(self, t, name="", excl=False):
        self.t = t
        self.name = name
        self.excl = excl
        self.last_w = None
        self.readers = {}

    def __getitem__(self, idx):
        return self.t[idx]

    def ap(self):
        return self.t.ap()


class Prog:
    def __init__(self, nc, stack, same_engine_sync=True):
        self.nc = nc
        self.ops = {e: [] for e in ENGS}
        self.cnt = {}
        self.known = {e: {} for e in ENGS}
        self.dma_i = {e: 0 for e in ENGS}
        self.same_engine_sync = same_engine_sync
        self.stack = stack
        self.sems = {}
        self.uid = 0

    def sbuf(self, name, shape, dt):
        self.uid += 1
        t = self.stack.enter_context(self.nc.sbuf_tensor("%s_%d" % (name, self.uid), list(shape), dt))
        return Buf(t, name)

    def psum(self, name, shape, dt=F32):
        self.uid += 1
        t = self.stack.enter_context(self.nc.psum_tensor("%s_%d" % (name, self.uid), list(shape), dt))
        return Buf(t, name, excl=True)

    def dram(self, name, shape, dt, kind="Internal"):
        t = self.nc.dram_tensor(name, list(shape), dt, kind=kind)
        return Buf(t, name)

    def _deps(self, eng, reads, writes):
        need = {}

        def add(k, v):
            if need.get(k, 0) < v:
                need[k] = v
        for b in reads:
            if b.last_w is not None:
                add(*b.last_w)
        for b in writes:
            if b.last_w is not None:
                add(*b.last_w)
            for k, v in b.readers.items():
                add(k, v)
        waits = []
        kn = self.known[eng]
        for k, v in need.items():
            if k == eng and (eng == "tensor" or not self.same_engine_sync):
                continue
            if kn.get(k, 0) >= v:
                continue
            kn[k] = v
            waits.append((k, v))
        return waits

    def _commit(self, key, val, reads, writes):
        for b in reads:
            if b.readers.get(key, 0) < val:
                b.readers[key] = val
        for b in writes:
            b.last_w = (key, val)
            b.readers = {}

    def op(self, eng, fn, reads=(), writes=()):
        if any(b.excl for b in reads):
            writes = list(writes) + [b for b in reads if b.excl and b not in writes]
            reads = [b for b in reads if not b.excl]
        waits = self._deps(eng, reads, writes)
        val = self.cnt.get(eng, 0) + 1
        self.cnt[eng] = val
        self.ops[eng].append((waits, fn, eng, 1))
        self._commit(eng, val, reads, writes)

    def dma(self, eng, out_ap, in_ap, reads=(), writes=()):
        i = self.dma_i[eng]
        self.dma_i[eng] = i + 1
        key = "dma_%s_%d" % (eng, i % N_DMA_SEMS)
        waits = self._deps(eng, reads, writes)
        prev = self.cnt.get(key, 0)
        kn = self.known[eng]
        if prev > 0 and kn.get(key, 0) < prev:
            kn[key] = prev
            waits.append((key, prev))
        val = prev + 16
        self.cnt[key] = val
        self.ops[eng].append((waits, lambda e: e.dma_start(out=out_ap, in_=in_ap), key, 16))
        self._commit(key, val, reads, writes)

    def barrier(self):
        snap = dict(self.cnt)
        for eng in ENGS:
            waits = []
            kn = self.known[eng]
            for k, v in snap.items():
                if k == eng:
                    continue
                if kn.get(k, 0) < v:
                    kn[k] = v
                    waits.append((k, v))
            if waits:
                self.ops[eng].append((waits, None, None, 0))

    def emit(self):
        nc = self.nc
        keys = list(self.cnt.keys())
        for k in keys:
            self.sems[k] = self.stack.enter_context(nc.semaphore("s_" + k))
        fin = [(k, self.cnt[k]) for k in keys]
        self.ops["sync"].append((fin, None, None, 0))
        with nc.Block() as block:
            def mk(engname):
                def body(e):
                    for waits, fn, key, inc in self.ops[engname]:
                        for (k, v) in waits:
                            e.wait_ge(self.sems[k], v)
                        if fn is not None:
                            fn(e).then_inc(self.sems[key], inc)
                return body
            block.tensor(mk("tensor"))
            block.vector(mk("vector"))
            block.scalar(mk("scalar"))
            block.gpsimd(mk("gpsimd"))
            block.sync(mk("sync"))


class Rot:
    def __init__(self, bufs):
        self.bufs = bufs
        self.i = 0

    def next(self):
        b = self.bufs[self.i % len(self.bufs)]
        self.i += 1
        return b


def emit_cast(p, dst, src, rows, cols, rb=None):
    if rb is None:
        rb = max(1, min(rows, (4 << 20) // (cols * 4)))
    r = 0
    while r < rows:
        n = min(rb, rows - r)
        p.dma("gpsimd", dst.ap()[r:r + n, :], src.ap()[r:r + n, :], reads=[src], writes=[dst])
        r += n


def emit_linear(p, inT, W, Ci, Co, T, epilogue, psums, TT=512, cb=256, kg=None, tag="lin",
                in_col0=0, w_col0=0):
    nk = Ci // 128
    if kg is None:
        kg = nk
        while kg * cb * 2 > 16384:
            kg = (kg + 1) // 2
    ngrp = (nk + kg - 1) // kg
    xslots = 2 if nk * TT * 2 <= 40000 else 1
    xin = Rot([p.sbuf(tag + "_x", [128, nk, TT], BF16) for _ in range(xslots)])
    wts = Rot([p.sbuf(tag + "_w", [128, kg, cb], BF16) for _ in range(3)])
    inv = inT.ap().rearrange("(k p) t -> p k t", p=128)
    wv = W.ap().rearrange("(k p) c -> p k c", p=128)
    nchunk = cb // 128
    t0 = 0
    while t0 < T:
        n = min(TT, T - t0)
        xb = xin.next()
        p.dma("sync", xb[:, :, 0:n], inv[:, :, in_col0 + t0:in_col0 + t0 + n], reads=[inT], writes=[xb])
        for c0 in range(0, Co, cb):
            ncc = min(nchunk, (Co - c0) // 128)
            pss = [psums.next() for _ in range(ncc)]
            for g in range(ngrp):
                k0 = g * kg
                nkk = min(kg, nk - k0)
                wb = wts.next()
                p.dma("sync", wb[:, 0:nkk, 0:ncc * 128], wv[:, k0:k0 + nkk, w_col0 + c0:w_col0 + c0 + ncc * 128],
                      reads=[W], writes=[wb])
                for c in range(ncc):
                    for k in range(nkk):
                        first = (g == 0 and k == 0)
                        last = (g == ngrp - 1 and k == nkk - 1)
                        p.op("tensor",
                             (lambda e, ps=pss[c], wb=wb, xb=xb, k=k, kk=k0 + k, c=c, n=n, first=first, last=last:
                              e.matmul(ps[:, 0:n], wb[:, k, c * 128:(c + 1) * 128], xb[:, kk, 0:n],
                                       start=first, stop=last)),
                             reads=[wb, xb], writes=[pss[c]])
                if g == ngrp - 1:
                    for c in range(ncc):
                        epilogue(p, pss[c], (c0 // 128) + c, t0, n)
        t0 += n


@contextlib.contextmanager
def phase(p):
    p.barrier()
    old = p.stack
    with contextlib.ExitStack() as st:
        p.stack = st
        yield
        p.barrier()
    p.stack = old


def segs(t0, n, tl):
    out = []
    if t0 < tl:
        out.append((0, min(n, tl - t0), 0))
    if t0 + n > tl:
        out.append((max(0, tl - t0), n, 1))
    return out


def emit_resid_linear(p, inT, Wb, Ci, D, Tc, TL, resid, bias, gates, V, meanB, rstdB, ones, eps, tag, alpha):
    nd = D // 128
    with phase(p):
        psums = Rot([p.psum(tag + "ps", [128, 512]) for _ in range(4)])
        st_s = Rot([p.psum(tag + "ss", [128, 512]) for _ in range(2)])
        st_q = Rot([p.psum(tag + "sq", [128, 512]) for _ in range(2)])
        xr = Rot([p.sbuf(tag + "xr", [128, 512], F32) for _ in range(3)])
        vb = Rot([p.sbuf(tag + "vb", [128, 512], F32) for _ in range(3)])
        vq = Rot([p.sbuf(tag + "vq", [128, 512], F32) for _ in range(3)])
        tmp = Rot([p.sbuf(tag + "tm", [128, 512], F32) for _ in range(2)])
        state = {}

        def epi(p, ps, cc, t0, n):
            x_t = xr.next()
            p.dma("sync", x_t[:, 0:n], resid.ap()[cc * 128:(cc + 1) * 128, t0:t0 + n], reads=[resid], writes=[x_t])
            v_t = vb.next()
            q_t = vq.next()
            for (a, b, ic) in segs(t0, n, TL):
                g = gates[ic]
                if bias is not None:
                    p.op("vector", lambda e, a=a, b=b, g=g: e.tensor_scalar(v_t[:, a:b], ps[:, a:b], bias[:, cc:cc + 1], g[:, cc:cc + 1], ALU.add, ALU.mult),
                         reads=[ps, bias, g], writes=[v_t])
                else:
                    p.op("vector", lambda e, a=a, b=b, g=g: e.tensor_scalar(v_t[:, a:b], ps[:, a:b], g[:, cc:cc + 1], None, ALU.mult),
                         reads=[ps, g], writes=[v_t])
            p.op("vector", lambda e: e.scalar_tensor_tensor(v_t[:, 0:n], x_t[:, 0:n], float(alpha), v_t[:, 0:n], ALU.mult, ALU.add),
                 reads=[x_t, v_t], writes=[v_t])
            p.op("scalar", lambda e: e.activation(out=q_t[:, 0:n], in_=v_t[:, 0:n], func=AF.Square), reads=[v_t], writes=[q_t])
            p.dma("gpsimd", V.ap()[cc * 128:(cc + 1) * 128, t0:t0 + n], v_t[:, 0:n], reads=[v_t], writes=[V])
            if cc == 0:
                state["s"] = st_s.next()
                state["q"] = st_q.next()
            s_ps, q_ps = state["s"], state["q"]
            p.op("tensor", lambda e: e.matmul(s_ps[:, 0:n], ones[:, :], v_t[:, 0:n], start=(cc == 0), stop=(cc == nd - 1)),
                 reads=[ones, v_t], writes=[s_ps])
            p.op("tensor", lambda e: e.matmul(q_ps[:, 0:n], ones[:, :], q_t[:, 0:n], start=(cc == 0), stop=(cc == nd - 1)),
                 reads=[ones, q_t], writes=[q_ps])
            if cc == nd - 1:
                m = meanB
                r = rstdB
                t_ = tmp.next()
                p.op("scalar", lambda e: e.activation(out=m[:, t0:t0 + n], in_=s_ps[:, 0:n], func=AF.Identity, scale=1.0 / D, bias=0.0),
                     reads=[s_ps], writes=[m])
                p.op("vector", lambda e: e.tensor_tensor(t_[:, 0:n], m[:, t0:t0 + n], m[:, t0:t0 + n], ALU.mult), reads=[m], writes=[t_])
                p.op("vector", lambda e: e.scalar_tensor_tensor(t_[:, 0:n], q_ps[:, 0:n], 1.0 / D, t_[:, 0:n], ALU.mult, ALU.subtract),
                     reads=[q_ps, t_], writes=[t_])
                p.op("vector", lambda e: e.tensor_scalar(t_[:, 0:n], t_[:, 0:n], float(eps), None, ALU.add), reads=[t_], writes=[t_])
                p.op("scalar", lambda e: e.activation(out=t_[:, 0:n], in_=t_[:, 0:n], func=AF.Sqrt), reads=[t_], writes=[t_])
                p.op("vector", lambda e: e.reciprocal(r[:, t0:t0 + n], t_[:, 0:n]), reads=[t_], writes=[r])
        emit_linear(p, inT, Wb, Ci, D, Tc, epi, psums, TT=512, cb=256, tag=tag)


def emit_ln_apply(p, V, D, Tc, TL, meanB, rstdB, g, b, outX, mods, outH, mask, tag):
    nd = D // 128
    with phase(p):
        vt = Rot([p.sbuf(tag + "v", [128, 512], F32) for _ in range(3)])
        xt = Rot([p.sbuf(tag + "x", [128, 512], F32) for _ in range(3)])
        ht = Rot([p.sbuf(tag + "h", [128, 512], BF16) for _ in range(3)])
        t0 = 0
        while t0 < Tc:
            n = min(512, Tc - t0)
            for cc in range(nd):
                v_t = vt.next()
                x_t = xt.next()
                p.dma("sync", v_t[:, 0:n], V.ap()[cc * 128:(cc + 1) * 128, t0:t0 + n], reads=[V], writes=[v_t])
                p.op("vector", lambda e, v_t=v_t, t0=t0, n=n: e.tensor_tensor(v_t[:, 0:n], v_t[:, 0:n], meanB[:, t0:t0 + n], ALU.subtract),
                     reads=[v_t, meanB], writes=[v_t])
                p.op("vector", lambda e, v_t=v_t, t0=t0, n=n: e.tensor_tensor(v_t[:, 0:n], v_t[:, 0:n], rstdB[:, t0:t0 + n], ALU.mult),
                     reads=[v_t, rstdB], writes=[v_t])
                p.op("scalar", lambda e, v_t=v_t, x_t=x_t, n=n, cc=cc: e.activation(out=x_t[:, 0:n], in_=v_t[:, 0:n], func=AF.Identity,
                                                                                   scale=g[:, cc:cc + 1], bias=b[:, cc:cc + 1]),
                     reads=[v_t, g, b], writes=[x_t])
                p.dma("gpsimd", outX.ap()[cc * 128:(cc + 1) * 128, t0:t0 + n], x_t[:, 0:n], reads=[x_t], writes=[outX])
                if mods is not None:
                    h_t = ht.next()
                    for (a, bb, ic) in segs(t0, n, TL):
                        sc1p, sh = mods[ic]
                        p.op("scalar", lambda e, a=a, bb=bb, sc1p=sc1p, sh=sh, x_t=x_t, v_t=v_t, cc=cc:
                             e.activation(out=v_t[:, a:bb], in_=x_t[:, a:bb], func=AF.Identity, scale=sc1p[:, cc:cc + 1], bias=sh[:, cc:cc + 1]),
                             reads=[x_t, sc1p, sh], writes=[v_t])
                    p.op("vector", lambda e, v_t=v_t, h_t=h_t, t0=t0, n=n: e.tensor_tensor(h_t[:, 0:n], v_t[:, 0:n], mask[:, t0:t0 + n], ALU.mult),
                         reads=[v_t, mask], writes=[h_t])
                    p.dma("gpsimd", outH.ap()[cc * 128:(cc + 1) * 128, t0:t0 + n], h_t[:, 0:n], reads=[h_t], writes=[outH])
            t0 += n


def emit_convglu(p, G, A, F, TO, CO, GW, wdw, bdw, mcl, mcr, has_ctx, tag="cg"):
    nf = F // 128
    HL = GW + 1
    TL = TO + 2 * HL
    Tc = TL + (CO + 2 if has_ctx else 0)
    with phase(p):
        z = p.sbuf(tag + "z", [128, nf, HL], BF16)
        p.op("vector", lambda e: e.memset(z[:, :, :], 0.0), writes=[z])
        Av = A.ap().rearrange("(c q) t -> q c t", q=128)
        p.dma("gpsimd", Av[:, :, 0:HL], z[:, :, :], reads=[z], writes=[A])
        p.dma("gpsimd", Av[:, :, HL + TO:TL], z[:, :, :], reads=[z], writes=[A])
        gts = Rot([p.sbuf(tag + "g", [128, Tc], F32) for _ in range(2)])
        vts = Rot([p.sbuf(tag + "v", [128, Tc], F32) for _ in range(2)])
        gls = Rot([p.sbuf(tag + "l", [128, TL], F32) for _ in range(2)])
        grs = Rot([p.sbuf(tag + "r", [128, TL], F32) for _ in range(2)])
        accs = Rot([p.sbuf(tag + "a", [128, TO + CO], F32) for _ in range(2)])
        acc2s = Rot([p.sbuf(tag + "b", [128, TO + CO], F32) for _ in range(2)])
        outs = Rot([p.sbuf(tag + "o", [128, TO + CO + 2], BF16) for _ in range(2)])
        for ob_ in outs.bufs:
            p.op("vector", lambda e, ob_=ob_: e.memset(ob_[:, :], 0.0), writes=[ob_])
        for f in range(nf):
            gt, vt, gl, gr, acc, acc2, ob = gts.next(), vts.next(), gls.next(), grs.next(), accs.next(), acc2s.next(), outs.next()
            p.dma("sync", gt[:, :], G.ap()[f * 128:(f + 1) * 128, :], reads=[G], writes=[gt])
            p.dma("sync", vt[:, :], G.ap()[F + f * 128:F + (f + 1) * 128, :], reads=[G], writes=[vt])
            p.op("gpsimd", lambda e, gl=gl, gt=gt: e.tensor_tensor(gl[:, :], gt[:, 0:TL], mcl[:, :], ALU.mult), reads=[gt, mcl], writes=[gl])
            p.op("gpsimd", lambda e, gr=gr, gt=gt: e.tensor_tensor(gr[:, :], gt[:, 0:TL], mcr[:, :], ALU.mult), reads=[gt, mcr], writes=[gr])
            first = True
            for dr in (0, -1, 1):
                for dc in (0, -1, 1):
                    src = gt if dc == 0 else (gl if dc == -1 else gr)
                    o = HL + GW * dr + dc
                    tap = (dr + 1) * 3 + (dc + 1)
                    if first:
                        p.op("vector", lambda e, src=src, o=o, tap=tap, acc=acc, f=f:
                             e.tensor_scalar(acc[:, 0:TO], src[:, o:o + TO], wdw[:, f, tap:tap + 1], None, ALU.mult),
                             reads=[src, wdw], writes=[acc])
                        first = False
                    else:
                        p.op("vector", lambda e, src=src, o=o, tap=tap, acc=acc, f=f:
                             e.scalar_tensor_tensor(acc[:, 0:TO], src[:, o:o + TO], wdw[:, f, tap:tap + 1], acc[:, 0:TO], ALU.mult, ALU.add),
                             reads=[src, wdw, acc], writes=[acc])
            if has_ctx:
                for dc in (0, -1, 1):
                    o = TL + 1 + dc
                    tap = 3 + (dc + 1)
                    if dc == 0:
                        p.op("vector", lambda e, o=o, tap=tap, acc=acc, gt=gt, f=f:
                             e.tensor_scalar(acc[:, TO:TO + CO], gt[:, o:o + CO], wdw[:, f, tap:tap + 1], None, ALU.mult),
                             reads=[gt, wdw, acc], writes=[acc])
                    else:
                        p.op("vector", lambda e, o=o, tap=tap, acc=acc, gt=gt, f=f:
                             e.scalar_tensor_tensor(acc[:, TO:TO + CO], gt[:, o:o + CO], wdw[:, f, tap:tap + 1], acc[:, TO:TO + CO], ALU.mult, ALU.add),
                             reads=[gt, wdw, acc], writes=[acc])
            nn = TO + (CO if has_ctx else 0)
            p.op("scalar", lambda e, acc=acc, acc2=acc2, f=f, nn=nn: e.activation(out=acc2[:, 0:nn], in_=acc[:, 0:nn], func=AF.Gelu,
                                                                                 bias=bdw[:, f:f + 1], scale=1.0),
                 reads=[acc, bdw], writes=[acc2])
            p.op("vector", lambda e, acc2=acc2, vt=vt, ob=ob: e.tensor_tensor(ob[:, 0:TO], acc2[:, 0:TO], vt[:, HL:HL + TO], ALU.mult),
                 reads=[acc2, vt], writes=[ob])
            if has_ctx:
                p.op("vector", lambda e, acc2=acc2, vt=vt, ob=ob: e.tensor_tensor(ob[:, TO + 1:TO + 1 + CO], acc2[:, TO:TO + CO], vt[:, TL + 1:TL + 1 + CO], ALU.mult),
                     reads=[acc2, vt], writes=[ob])
            p.dma("gpsimd", A.ap()[f * 128:(f + 1) * 128, HL:HL + TO], ob[:, 0:TO], reads=[ob], writes=[A])
            if has_ctx:
                p.dma("gpsimd", A.ap()[f * 128:(f + 1) * 128, TL:TL + CO + 2], ob[:, TO:TO + CO + 2], reads=[ob], writes=[A])


def load_tile(p, name, src, shape, dt=F32):
    t = p.sbuf(name, shape, dt)
    if len(shape) == 2:
        p.dma("sync", t[:, :], src.ap()[:, :], reads=[src], writes=[t])
    else:
        p.dma("sync", t[:, :, :], src.ap()[:, :, :], reads=[src], writes=[t])
    return t


def build_back(cfg, has_ctx, has_bias):
    D, F, TO, CO, GW = cfg["D"], cfg["F"], cfg["TO"], cfg["CO"], cfg["GW"]
    alpha, eps = cfg["alpha"], cfg["ln_eps"]
    nd, nf = D // 128, F // 128
    HL = GW + 1
    TL = TO + 2 * HL
    Tc = TL + (CO + 2 if has_ctx else 0)
    nc = bass.Bass("TRN2", target_bir_lowering=False)
    with contextlib.ExitStack() as st:
        p = Prog(nc, st)
        ext = lambda n, s, d: p.dram(n, s, d, kind="ExternalInput")
        zT = ext("zT", [D, Tc], BF16)
        xT = ext("xT", [D, Tc], F32)
        cmask = ext("cmask", [128, Tc], F32)
        mcl_d = ext("mcl", [128, TL], F32)
        mcr_d = ext("mcr", [128, TL], F32)
        modl_d = ext("modl", [128, 4, nd], F32)
        modc_d = ext("modc", [128, 4, nd], F32)
        lnp_d = ext("lnp", [128, 4, nd], F32)
        bo_d = ext("bo", [128, nd], F32)
        wmix = ext("wmix", [D, D], F32)
        wup = ext("wup", [D, 2 * F], F32)
        wdw_d = ext("wdw", [128, nf, 9], F32)
        bdw_d = ext("bdw", [128, nf], F32)
        wdown = ext("wdown", [F, D], F32)
        out = p.dram("out", [D, Tc], F32, kind="ExternalOutput")
        wmix_b = p.dram("wmix_b", [D, D], BF16)
        wup_b = p.dram("wup_b", [D, 2 * F], BF16)
        wdown_b = p.dram("wdown_b", [F, D], BF16)
        V = p.dram("V", [D, Tc], F32)
        X1 = p.dram("X1", [D, Tc], F32)
        H2 = p.dram("H2", [D, Tc], BF16)
        G = p.dram("G", [2 * F, Tc], F32)
        A = p.dram("A", [F, Tc], BF16)
        emit_cast(p, wmix_b, wmix, D, D)
        emit_cast(p, wup_b, wup, D, 2 * F)
        emit_cast(p, wdown_b, wdown, F, D)
        modl = load_tile(p, "modl", modl_d, [128, 4, nd])
        modc = load_tile(p, "modc", modc_d, [128, 4, nd])
        lnp = load_tile(p, "lnp", lnp_d, [128, 4, nd])
        bo = load_tile(p, "bo", bo_d, [128, nd])
        mask = load_tile(p, "mask", cmask, [128, Tc])
        ones = p.sbuf("ones", [128, 128], F32)
        p.op("vector", lambda e: e.memset(ones[:, :], 1.0), writes=[ones])
        sc1p_l = p.sbuf("sc1pl", [128, nd], F32)
        sc1p_c = p.sbuf("sc1pc", [128, nd], F32)
        p.op("vector", lambda e: e.tensor_scalar(sc1p_l[:, :], modl[:, 1, :], 1.0, None, ALU.add), reads=[modl], writes=[sc1p_l])
        p.op("vector", lambda e: e.tensor_scalar(sc1p_c[:, :], modc[:, 1, :], 1.0, None, ALU.add), reads=[modc], writes=[sc1p_c])
        meanB = p.sbuf("meanB", [128, Tc], F32)
        rstdB = p.sbuf("rstdB", [128, Tc], F32)

        def plane_tile(name, src, i):
            t = p.sbuf(name, [128, nd], F32)
            p.op("vector", lambda e: e.tensor_copy(t[:, :], src[:, i, :]), reads=[src], writes=[t])
            return t
        g1l, sh2l, g2l = plane_tile("g1l", modl, 0), plane_tile("sh2l", modl, 2), plane_tile("g2l", modl, 3)
        g1c, sh2c, g2c = plane_tile("g1c", modc, 0), plane_tile("sh2c", modc, 2), plane_tile("g2c", modc, 3)
        ln1g, ln1b, ln2g, ln2b = (plane_tile("ln%d" % i, lnp, i) for i in range(4))

        emit_resid_linear(p, zT, wmix_b, D, D, Tc, TL, xT, bo if has_bias else None, (g1l, g1c), V, meanB, rstdB, ones, eps, "r1", alpha)
        emit_ln_apply(p, V, D, Tc, TL, meanB, rstdB, ln1g, ln1b, X1, ((sc1p_l, sh2l), (sc1p_c, sh2c)), H2, mask, "l1")
        with phase(p):
            psums = Rot([p.psum("ups", [128, 512]) for _ in range(6)])
            gouts = Rot([p.sbuf("gout", [128, 512], F32) for _ in range(4)])
            cnt = [0]

            def epi_up(p, ps, cc, t0, n):
                ob = gouts.next()
                cnt[0] += 1
                if cnt[0] % 2 == 0:
                    p.op("scalar", lambda e: e.copy(ob[:, 0:n], ps[:, 0:n]), reads=[ps], writes=[ob])
                else:
                    p.op("vector", lambda e: e.tensor_copy(ob[:, 0:n], ps[:, 0:n]), reads=[ps], writes=[ob])
                p.dma("gpsimd", G.ap()[cc * 128:(cc + 1) * 128, t0:t0 + n], ob[:, 0:n], reads=[ob], writes=[G])
            emit_linear(p, H2, wup_b, D, 2 * F, Tc, epi_up, psums, TT=512, cb=256, tag="up")
        with phase(p):
            wdw = load_tile(p, "wdw", wdw_d, [128, nf, 9])
            bdw = load_tile(p, "bdw", bdw_d, [128, nf])
            mcl = load_tile(p, "mcl", mcl_d, [128, TL])
            mcr = load_tile(p, "mcr", mcr_d, [128, TL])
            emit_convglu(p, G, A, F, TO, CO, GW, wdw, bdw, mcl, mcr, has_ctx)
        emit_resid_linear(p, A, wdown_b, F, D, Tc, TL, X1, None, (g2l, g2c), V, meanB, rstdB, ones, eps, "r2", alpha)
        emit_ln_apply(p, V, D, Tc, TL, meanB, rstdB, ln2g, ln2b, out, None, None, None, "l2")
        p.emit()
    return nc


def build_ada(D, ncols, nlayers=2):
    nd = D // 128
    ncc = ncols // 128
    nc = bass.Bass("TRN2", target_bir_lowering=False)
    with contextlib.ExitStack() as st:
        p = Prog(nc, st)
        cond_d = p.dram("cond", [128, nd, 3], F32, kind="ExternalInput")
        ws = [p.dram("w%d" % l, [D, ncols], F32, kind="ExternalInput") for l in range(nlayers)]
        bs_d = p.dram("b", [128, nlayers, ncc], F32, kind="ExternalInput")
        out = p.dram("out", [128, nlayers, ncc, 3], F32, kind="ExternalOutput")
        cond = load_tile(p, "cond", cond_d, [128, nd, 3])
        bs = load_tile(p, "bs", bs_d, [128, nlayers, ncc])
        sT = p.sbuf("sT", [128, nd, 3], F32)
        p.op("scalar", lambda e: e.activation(out=sT[:, :, :], in_=cond[:, :, :], func=AF.Silu), reads=[cond], writes=[sT])
        ot = p.sbuf("ot", [128, nlayers, ncc, 3], F32)
        wts = Rot([p.sbuf("wt", [128, nd, 128], F32) for _ in range(3)])
        pss = Rot([p.psum("ps", [128, 512]) for _ in range(4)])
        for l in range(nlayers):
            wv = ws[l].ap().rearrange("(k p) c -> p k c", p=128)
            for cc in range(ncc):
                wt = wts.next()
                ps = pss.next()
                p.dma("sync", wt[:, :, :], wv[:, :, cc * 128:(cc + 1) * 128], reads=[ws[l]], writes=[wt])
                for k in range(nd):
                    p.op("tensor", lambda e, wt=wt, ps=ps, k=k: e.matmul(ps[:, 0:3], wt[:, k, :], sT[:, k, :], start=(k == 0), stop=(k == nd - 1)),
                         reads=[wt, sT], writes=[ps])
                p.op("vector", lambda e, ps=ps, l=l, cc=cc: e.tensor_scalar(ot[:, l, cc, :], ps[:, 0:3], bs[:, l, cc:cc + 1], None, ALU.add),
                     reads=[ps, bs], writes=[ot])
        p.dma("sync", out.ap()[:, :, :, :], ot[:, :, :, :], reads=[ot], writes=[out])
        p.emit()
    return nc


def dram_ap(buf, offset, pattern):
    return bass.AP(buf.t, offset, [list(x) for x in pattern])


def emit_sin_layer(p, ps, n, fp, li, out, tmp1, tmp2, pi2, K=64):
    hf = fp[0:K, li:li + 1]
    hfb = fp[0:K, 3 + li:4 + li]
    p.op("scalar", lambda e: e.activation(out=tmp1[0:K, 0:n], in_=ps[0:K, 0:n], func=AF.Sin, scale=hf, bias=hfb), reads=[ps, fp], writes=[tmp1])
    p.op("scalar", lambda e: e.activation(out=tmp2[0:K, 0:n], in_=ps[0:K, 0:n], func=AF.Abs, scale=hf, bias=hfb), reads=[ps, fp], writes=[tmp2])
    p.op("scalar", lambda e: e.activation(out=tmp2[0:K, 0:n], in_=tmp2[0:K, 0:n], func=AF.Sin, scale=-1.0, bias=pi2[0:K, 0:1]), reads=[tmp2, pi2], writes=[tmp2])
    p.op("vector", lambda e: e.scalar_tensor_tensor(out[0:K, 0:n], tmp1[0:K, 0:n], 2.0, tmp2[0:K, 0:n], ALU.mult, ALU.mult),
         reads=[tmp1, tmp2], writes=[out])


DBG = {}


def emit_hy_filters(p, Lx, zext, text, w1, w2, w3, fp, wout, negdelta, skip, Kf, CS, tag):
    W = 2 * Lx - 1
    ctr = Lx - 1
    ncc = CS // 128
    with phase(p):
        pi2 = p.sbuf(tag + "pi2", [128, 1], F32)
        p.op("vector", lambda e: e.memset(pi2[:, :], float(np.pi / 2)), writes=[pi2])
        zts = Rot([p.sbuf(tag + "z", [33, 512], F32) for _ in range(2)])
        tts = Rot([p.sbuf(tag + "t", [128, 512], F32) for _ in range(2)])
        hs = Rot([p.sbuf(tag + "h", [64, 512], F32) for _ in range(6)])
        t1s = Rot([p.sbuf(tag + "a", [64, 512], F32) for _ in range(2)])
        t2s = Rot([p.sbuf(tag + "b", [64, 512], F32) for _ in range(2)])
        pss = Rot([p.psum(tag + "ps", [128, 512]) for _ in range(4)])
        decs = Rot([p.sbuf(tag + "d", [128, 512], F32) for _ in range(2)])
        f32s = Rot([p.sbuf(tag + "f", [128, 512], F32) for _ in range(3)])
        fbs = Rot([p.sbuf(tag + "fb", [128, 512], BF16) for _ in range(3)])
        one1 = p.sbuf(tag + "one1", [1, 1], F32)
        p.op("vector", lambda e: e.memset(one1[:, :], 1.0), writes=[one1])
        c0 = 0
        while c0 < W:
            n = min(512, W - c0)
            zt, tt = zts.next(), tts.next()
            p.dma("sync", zt[:, 0:n], zext.ap()[:, c0:c0 + n], reads=[zext], writes=[zt])
            p.dma("sync", tt[:, 0:n], text.ap()[:, c0:c0 + n], reads=[text], writes=[tt])
            cur = zt
            Kdim = 33
            for li, wl in enumerate((w1, w2, w3)):
                ps = pss.next()
                p.op("tensor", lambda e, ps=ps, wl=wl, cur=cur, Kdim=Kdim, n=n: e.matmul(ps[0:64, 0:n], wl[0:Kdim, :], cur[0:Kdim, 0:n], start=True, stop=True),
                     reads=[wl, cur], writes=[ps])
                h = hs.next()
                emit_sin_layer(p, ps, n, fp, li, h, t1s.next(), t2s.next(), pi2)
                cur = h
                Kdim = 64
            h3 = cur
            has_c = (c0 <= ctr < c0 + n)
            for o in range(2):
                gl, gr = (0, 1) if o == 0 else (3, 2)
                for cc in range(ncc):
                    ps = pss.next()
                    a_end = min(n, max(0, ctr + 1 - c0))
                    if a_end > 0:
                        p.op("tensor", lambda e, ps=ps, gl=gl, cc=cc, a_end=a_end, h3=h3: e.matmul(ps[:, 0:a_end], wout[0:64, gl, cc * 128:(cc + 1) * 128], h3[0:64, 0:a_end], start=True, stop=True),
                             reads=[wout, h3], writes=[ps])
                    if a_end < n:
                        p.op("tensor", lambda e, ps=ps, gr=gr, cc=cc, a_end=a_end, h3=h3, n=n: e.matmul(ps[:, a_end:n], wout[0:64, gr, cc * 128:(cc + 1) * 128], h3[0:64, a_end:n], start=(a_end == 0), stop=True, skip_group_check=True),
                             reads=[wout, h3], writes=[ps])
                    if has_c:
                        ci = ctr - c0
                        p.op("tensor", lambda e, ps=ps, gr=gr, cc=cc, ci=ci, h3=h3: e.matmul(ps[:, ci:ci + 1], wout[0:64, gr, cc * 128:(cc + 1) * 128], h3[0:64, ci:ci + 1], start=False, stop=False, skip_group_check=True),
                             reads=[wout, h3], writes=[ps])
                        p.op("tensor", lambda e, ps=ps, o=o, cc=cc, ci=ci: e.matmul(ps[:, ci:ci + 1], skip[0:1, o, cc * 128:(cc + 1) * 128], one1[0:1, 0:1], start=False, stop=True, skip_group_check=True),
                             reads=[skip, one1], writes=[ps])
                    dec = decs.next()
                    p.op("scalar", lambda e, dec=dec, tt=tt, cc=cc, n=n: e.activation(out=dec[:, 0:n], in_=tt[:, 0:n], func=AF.Exp, scale=negdelta[:, cc:cc + 1]),
                         reads=[tt, negdelta], writes=[dec])
                    f32 = f32s.next()
                    p.op("vector", lambda e, f32=f32, ps=ps, dec=dec, n=n: e.tensor_tensor(f32[:, 0:n], ps[:, 0:n], dec[:, 0:n], ALU.mult), reads=[ps, dec], writes=[f32])
                    if DBG and tag == "fl" and c0 == 0 and o == 0 and cc == 0:
                        p.dma("sync", DBG["h3"].ap()[:, :], h3[:, :], reads=[h3], writes=[DBG["h3"]])
                        p.dma("sync", DBG["dec"].ap()[:, :], dec[:, :], reads=[dec], writes=[DBG["dec"]])
                        p.dma("sync", DBG["f32"].ap()[:, :], tt[:, :], reads=[tt], writes=[DBG["f32"]])
                    fb = fbs.next()
                    p.op("scalar", lambda e, fb=fb, f32=f32, n=n: e.copy(fb[:, 0:n], f32[:, 0:n]), reads=[f32], writes=[fb])
                    p.dma("gpsimd", Kf.ap()[o, cc * 128:(cc + 1) * 128, c0:c0 + n], fb[:, 0:n], reads=[fb], writes=[Kf])
            c0 += n


def emit_hy_conv(p, cfgh, U, wsh_d, Kf, Kfc, ident, antiid, zout, tag="hc"):
    B, L, Lc, CS = cfgh["B"], cfgh["L"], cfgh["Lc"], cfgh["CS"]
    WK, WKc = cfgh["WK"], cfgh["WKc"]
    nb, nbc = L // 128, Lc // 128
    NL = B * nb
    NCOL = NL + B * nbc
    Ttot = B * L + B * Lc
    WS = 2 * L - 1 - 127
    WSc = 2 * Lc - 1 - 127
    PIECE = min(2048, L)
    seqs = [(b * L, L) for b in range(B)] + [(B * L + b * Lc, Lc) for b in range(B)]
    with phase(p):
        VT = p.sbuf(tag + "VT", [128, 64, NCOL], BF16)
        XR = p.sbuf(tag + "XR", [128, 64, NCOL], BF16)
        X1T = p.sbuf(tag + "X1T", [128, 64, NCOL], BF16)
        X2T = p.sbuf(tag + "X2T", [128, 64, NCOL], BF16)
        Z1T = p.sbuf(tag + "Z1T", [128, 64, NCOL], BF16)
        Z2T = XR
        sks = Rot([p.sbuf(tag + "sk", [128, WS], BF16) for _ in range(2)])
        skcs = Rot([p.sbuf(tag + "skc", [128, WSc], BF16) for _ in range(2)])
        uts = Rot([p.sbuf(tag + "ut", [64, PIECE + 2], F32) for _ in range(2)])
        accs = Rot([p.sbuf(tag + "ac", [64, PIECE], F32) for _ in range(2)])
        cms = Rot([p.sbuf(tag + "cm", [64, PIECE], BF16) for _ in range(2)])
        wsh = Rot([p.sbuf(tag + "w", [64, 3, 4], F32) for _ in range(2)])
        pst = Rot([p.psum(tag + "pt", [128, 16, 64], BF16) for _ in range(2)])
        psf = Rot([p.psum(tag + "pf", [128, 512]) for _ in range(2)])
        psy = Rot([p.psum(tag + "py", [128, 512]) for _ in range(2)])
        pso = Rot([p.psum(tag + "po", [128, 512]) for _ in range(2)])
        obs = Rot([p.sbuf(tag + "ob", [64, 512], BF16) for _ in range(3)])
        for hg in range(CS // 64):
            wt = wsh.next()
            p.dma("sync", wt[:, :, :], wsh_d.ap()[hg, :, :, :], reads=[wsh_d], writes=[wt])
            for part, dst in ((2, VT), (0, XR), (1, X2T)):
                row0 = part * CS + hg * 64
                for (s0, Ls) in seqs:
                    for q0 in range(0, Ls, PIECE):
                        n = min(PIECE, Ls - q0)
                        ut, acc, cm = uts.next(), accs.next(), cms.next()
                        lo = q0 - 1
                        hi = q0 + n + 1
                        a, e_ = max(lo, 0), min(hi, Ls)
                        if lo < 0:
                            p.op("gpsimd", lambda e, ut=ut: e.memset(ut[:, 0:1], 0.0), writes=[ut])
                        if hi > Ls:
                            p.op("gpsimd", lambda e, ut=ut, n=n: e.memset(ut[:, n + 1:n + 2], 0.0), writes=[ut])
                        p.dma("sync", ut[:, a - lo:e_ - lo], U.ap()[row0:row0 + 64, s0 + a:s0 + e_], reads=[U], writes=[ut])
                        p.op("vector", lambda e, ut=ut, acc=acc, wt=wt, part=part, n=n: e.tensor_scalar(acc[:, 0:n], ut[:, 0:n], wt[:, part, 0:1], None, ALU.mult),
                             reads=[ut, wt], writes=[acc])
                        p.op("vector", lambda e, ut=ut, acc=acc, wt=wt, part=part, n=n: e.scalar_tensor_tensor(acc[:, 0:n], ut[:, 1:n + 1], wt[:, part, 1:2], acc[:, 0:n], ALU.mult, ALU.add),
                             reads=[ut, wt, acc], writes=[acc])
                        p.op("vector", lambda e, ut=ut, acc=acc, wt=wt, part=part, n=n: e.scalar_tensor_tensor(acc[:, 0:n], ut[:, 2:n + 2], wt[:, part, 2:3], acc[:, 0:n], ALU.mult, ALU.add),
                             reads=[ut, wt, acc], writes=[acc])
                        p.op("scalar", lambda e, acc=acc, cm=cm, wt=wt, part=part, n=n: e.activation(out=cm[:, 0:n], in_=acc[:, 0:n], func=AF.Identity, bias=wt[:, part, 3:4], scale=1.0),
                             reads=[acc, wt], writes=[cm])
                        col0 = (s0 + q0) // 128
                        nblk = n // 128
                        for g0 in range(0, nblk, 8):
                            ng = min(8, nblk - g0)
                            pt = pst.next()
                            for k in range(ng):
                                p.op("tensor", lambda e, pt=pt, cm=cm, k=k, g0=g0: e.transpose(pt[:, k, :], cm[:, (g0 + k) * 128:(g0 + k + 1) * 128], ident[0:64, 0:64]),
                                     reads=[cm, ident], writes=[pt])
                            c_a = col0 + g0
                            p.op("vector", lambda e, pt=pt, dst=dst, c_a=c_a, ng=ng: e.tensor_copy(dst[:, :, c_a:c_a + ng].rearrange("p c k -> p k c"), pt[:, 0:ng, :]),
                                 reads=[pt], writes=[dst])
            xrf = XR[:, :, :].rearrange("p c k -> p (c k)")
            x1f = X1T[:, :, :].rearrange("p c k -> p (c k)")
            tot = 64 * NCOL
            for f0 in range(0, tot, 512):
                n = min(512, tot - f0)
                ps = psf.next()
                p.op("tensor", lambda e, ps=ps, f0=f0, n=n: e.matmul(ps[:, 0:n], antiid[:, :], xrf[:, f0:f0 + n], start=True, stop=True), reads=[antiid, XR], writes=[ps])
                p.op("scalar", lambda e, ps=ps, f0=f0, n=n: e.copy(x1f[:, f0:f0 + n], ps[:, 0:n]), reads=[ps], writes=[X1T])
            for o, (src, gate, dstz) in enumerate(((VT, X1T, Z1T), (Z1T, X2T, Z2T))):
                sgn = -1 if o == 0 else 1
                for c in range(64):
                    ch = hg * 64 + c
                    sk, skc = sks.next(), skcs.next()
                    p.dma("sync", sk[:, :], dram_ap(Kf, (o * CS + ch) * WK, [[1, 128], [1, WS]]), reads=[Kf], writes=[sk])
                    p.dma("sync", skc[:, :], dram_ap(Kfc, (o * CS + ch) * WKc, [[1, 128], [1, WSc]]), reads=[Kfc], writes=[skc])
                    ps = psy.next()
                    dlist = [0] + [x for d in range(1, nb) for x in (d, -d)]
                    sv = src[:, c, 0:NL].rearrange("p (b j) -> p b j", b=B)
                    pv = ps[:, 0:NL].rearrange("p (b j) -> p b j", b=B)
                    for i_, d in enumerate(dlist):
                        j0, j1 = max(0, -d), min(nb, nb - d)
                        m0 = L - 128 + sgn * 128 * d
                        p.op("tensor", lambda e, sk=sk, m0=m0, sv=sv, pv=pv, j0=j0, j1=j1, d=d, i_=i_:
                             e.matmul(pv[:, :, j0 + d:j1 + d], sk[:, m0:m0 + 128], sv[:, :, j0:j1], start=(i_ == 0), stop=(i_ == len(dlist) - 1)),
                             reads=[sk, src], writes=[ps])
                    dlc = [0] + [x for d in range(1, nbc) for x in (d, -d)]
                    svc = src[:, c, NL:NCOL].rearrange("p (b j) -> p b j", b=B)
                    pvc = ps[:, NL:NCOL].rearrange("p (b j) -> p b j", b=B)
                    for i_, d in enumerate(dlc):
                        j0, j1 = max(0, -d), min(nbc, nbc - d)
                        m0 = Lc - 128 + sgn * 128 * d
                        p.op("tensor", lambda e, skc=skc, m0=m0, svc=svc, pvc=pvc, j0=j0, j1=j1, d=d, i_=i_:
                             e.matmul(pvc[:, :, j0 + d:j1 + d], skc[:, m0:m0 + 128], svc[:, :, j0:j1], start=(i_ == 0), stop=(i_ == len(dlc) - 1)),
                             reads=[skc, src], writes=[ps])
                    p.op("vector", lambda e, ps=ps, gate=gate, dstz=dstz, c=c: e.tensor_tensor(dstz[:, c, :], ps[:, 0:NCOL], gate[:, c, :], ALU.mult),
                         reads=[ps, gate], writes=[dstz])
            for g0 in range(0, NCOL, 4):
                ng = min(4, NCOL - g0)
                po = pso.next()
                for k in range(ng):
                    p.op("tensor", lambda e, po=po, k=k, g0=g0: e.matmul(po[0:64, k * 128:(k + 1) * 128], Z2T[:, :, g0 + k], ident[:, :], start=True, stop=True),
                         reads=[Z2T, ident], writes=[po])
                ob = obs.next()
                p.op("scalar", lambda e, po=po, ob=ob, ng=ng: e.copy(ob[:, 0:ng * 128], po[0:64, 0:ng * 128]), reads=[po], writes=[ob])
                p.dma("gpsimd", zout.ap()[hg * 64:(hg + 1) * 64, g0 * 128:(g0 + ng) * 128], ob[:, 0:ng * 128], reads=[ob], writes=[zout])


def build_hyfront(cfg):
    D, B, L, Lc, NC = cfg["D"], cfg["B"], cfg["L"], cfg["Lc"], cfg["NC"]
    CS = D // NC
    nd = D // 128
    Ttot = B * L + B * Lc
    WK = 2 * L
    WKc = 2 * Lc
    nc = bass.Bass("TRN2", target_bir_lowering=False)
    with contextlib.ExitStack() as st:
        p = Prog(nc, st)
        ext = lambda n, s, d: p.dram(n, s, d, kind="ExternalInput")
        xall = ext("xall", [D, Ttot], F32)
        modp_d = ext("modp", [128, 3, 2, nd], F32)
        win = ext("win", [D, 3 * CS], F32)
        bin_d = ext("bin", [128, 3 * CS // 128], F32)
        wsh_d = ext("wsh", [CS // 64, 64, 3, 4], F32)
        zext = ext("zext", [33, 2 * L], F32)
        zextc = ext("zextc", [33, 2 * Lc], F32)
        text = ext("text", [128, 2 * L], F32)
        textc = ext("textc", [128, 2 * Lc], F32)
        w1_d = ext("w1", [33, 64], F32)
        w2_d = ext("w2", [64, 64], F32)
        w3_d = ext("w3", [64, 64], F32)
        fp_d = ext("fp", [64, 8], F32)
        wout_d = ext("wout", [64, 4, CS], F32)
        ndl_d = ext("negdelta", [128, CS // 128], F32)
        skip_d = ext("skip", [1, 2, CS], F32)
        id_d = ext("ident", [128, 128], BF16)
        aid_d = ext("antiid", [128, 128], BF16)
        zout = p.dram("zout", [CS, Ttot], BF16, kind="ExternalOutput")
        H = p.dram("H", [D, Ttot], BF16)
        winb = p.dram("winb", [D, 3 * CS], BF16)
        U = p.dram("U", [3 * CS, Ttot], F32)
        dbg = cfg.get("dbg", False)
        Kf = p.dram("Kf", [2, CS, WK], BF16, kind="ExternalOutput" if dbg else "Internal")
        Kfc = p.dram("Kfc", [2, CS, WKc], BF16, kind="ExternalOutput" if dbg else "Internal")
        if dbg:
            DBG["h3"] = p.dram("dbg_h3", [64, 512], F32, kind="ExternalOutput")
            DBG["dec"] = p.dram("dbg_dec", [128, 512], F32, kind="ExternalOutput")
            DBG["f32"] = p.dram("dbg_f32", [128, 512], F32, kind="ExternalOutput")
        emit_cast(p, winb, win, D, 3 * CS)
        modp = load_tile(p, "modp", modp_d, [128, 3, 2, nd]) if False else None
        modp = p.sbuf("modp", [128, 3, 2, nd], F32)
        p.dma("sync", modp[:, :, :, :], modp_d.ap()[:, :, :, :], reads=[modp_d], writes=[modp])
        binT = load_tile(p, "bin", bin_d, [128, 3 * CS // 128])
        ident = load_tile(p, "ident", id_d, [128, 128], BF16)
        antiid = load_tile(p, "antiid", aid_d, [128, 128], BF16)
        with phase(p):
            xts = Rot([p.sbuf("b1x", [128, 512], F32) for _ in range(4)])
            hts = Rot([p.sbuf("b1h", [128, 512], BF16) for _ in range(4)])
            for t0 in range(0, Ttot, 512):
                n = min(512, Ttot - t0)
                cond = min(t0 // L, B) if t0 < B * L else B
                for k in range(nd):
                    xt, ht = xts.next(), hts.next()
                    p.dma("sync", xt[:, 0:n], xall.ap()[k * 128:(k + 1) * 128, t0:t0 + n], reads=[xall], writes=[xt])
                    p.op("scalar", lambda e, xt=xt, ht=ht, n=n, cond=cond, k=k: e.activation(out=ht[:, 0:n], in_=xt[:, 0:n], func=AF.Identity,
                                                                                             scale=modp[:, cond, 0, k:k + 1], bias=modp[:, cond, 1, k:k + 1]),
                         reads=[xt, modp], writes=[ht])
                    p.dma("gpsimd", H.ap()[k * 128:(k + 1) * 128, t0:t0 + n], ht[:, 0:n], reads=[ht], writes=[H])
        with phase(p):
            psums = Rot([p.psum("b2ps", [128, 512]) for _ in range(6)])
            outs = Rot([p.sbuf("b2o", [128, 512], F32) for _ in range(4)])

            def epi(p, ps, cc, t0, n):
                ob = outs.next()
                p.op("scalar", lambda e: e.activation(out=ob[:, 0:n], in_=ps[:, 0:n], func=AF.Identity, bias=binT[:, cc:cc + 1], scale=1.0),
                     reads=[ps, binT], writes=[ob])
                p.dma("gpsimd", U.ap()[cc * 128:(cc + 1) * 128, t0:t0 + n], ob[:, 0:n], reads=[ob], writes=[U])
            emit_linear(p, H, winb, D, 3 * CS, Ttot, epi, psums, TT=512, cb=256, tag="b2")
        with phase(p):
            w1 = load_tile(p, "w1", w1_d, [33, 64])
            w2 = load_tile(p, "w2", w2_d, [64, 64])
            w3 = load_tile(p, "w3", w3_d, [64, 64])
            fpr = load_tile(p, "fpr", fp_d, [64, 8])
            wout = load_tile(p, "wout", wout_d, [64, 4, CS])
            negdelta = load_tile(p, "ndl", ndl_d, [128, CS // 128])
            skip = load_tile(p, "skip", skip_d, [1, 2, CS])
            fp = p.sbuf("fp", [64, 8], F32)
            p.op("vector", lambda e: e.tensor_scalar(fp[:, 0:3], fpr[:, 0:3], 0.5, None, ALU.mult), reads=[fpr], writes=[fp])
            p.op("vector", lambda e: e.tensor_tensor(fp[:, 3:6], fp[:, 0:3], fpr[:, 3:6], ALU.mult), reads=[fp, fpr], writes=[fp])
            emit_hy_filters(p, L, zext, text, w1, w2, w3, fp, wout, negdelta, skip, Kf, CS, "fl")
            emit_hy_filters(p, Lc, zextc, textc, w1, w2, w3, fp, wout, negdelta, skip, Kfc, CS, "fc")
        cfgh = dict(B=B, L=L, Lc=Lc, CS=CS, WK=WK, WKc=WKc)
        emit_hy_conv(p, cfgh, U, wsh_d, Kf, Kfc, ident, antiid, zout)
        p.emit()
    return nc


def emit_modulate(p, xall, modp, Hout, D, Ttot, L, B, tag="md"):
    nd = D // 128
    with phase(p):
        xts = Rot([p.sbuf(tag + "x", [128, 512], F32) for _ in range(4)])
        hts = Rot([p.sbuf(tag + "h", [128, 512], BF16) for _ in range(4)])
        for t0 in range(0, Ttot, 512):
            n = min(512, Ttot - t0)
            cond = min(t0 // L, B) if t0 < B * L else B
            for k in range(nd):
                xt, ht = xts.next(), hts.next()
                p.dma("sync", xt[:, 0:n], xall.ap()[k * 128:(k + 1) * 128, t0:t0 + n], reads=[xall], writes=[xt])
                p.op("scalar", lambda e, xt=xt, ht=ht, n=n, cond=cond, k=k: e.activation(out=ht[:, 0:n], in_=xt[:, 0:n], func=AF.Identity,
                                                                                         scale=modp[:, cond, 0, k:k + 1], bias=modp[:, cond, 1, k:k + 1]),
                     reads=[xt, modp], writes=[ht])
                p.dma("gpsimd", Hout.ap()[k * 128:(k + 1) * 128, t0:t0 + n], ht[:, 0:n], reads=[ht], writes=[Hout])


def build_gdnfront(cfg):
    D, B, L, Lc, NC = cfg["D"], cfg["B"], cfg["L"], cfg["Lc"], cfg["NC"]
    rms_eps, l2_eps = cfg["rms_eps"], cfg["l2_eps"]
    H = D // 128
    HPC = H // NC
    CS = HPC * 128
    nd = D // 128
    Ttot = B * L + B * Lc
    NBLK = Ttot // 128
    NLB = B * L // 128
    nbl, nbc = L // 128, Lc // 128
    CO = 4 * CS + 128
    NAB = 4 * HPC
    nc = bass.Bass("TRN2", target_bir_lowering=False)
    with contextlib.ExitStack() as st:
        p = Prog(nc, st)
        ext = lambda n, s, d: p.dram(n, s, d, kind="ExternalInput")
        xall = ext("xall", [D, Ttot], F32)
        modp_d = ext("modp", [128, 3, 2, nd], F32)
        win = ext("win", [D, CO], F32)
        wcv_d = ext("wcv", [128, 3 * HPC, 3], F32)
        abc_d = ext("abc", [128, 2, 2 * HPC], F32)
        normw_d = ext("normw", [128, 128], F32)
        id_d = ext("ident", [128, 128], F32)
        msk_d = ext("masks", [128, 6, 128], F32)
        oout = p.dram("oout", [B * L, CS], BF16, kind="ExternalOutput")
        Hh = p.dram("H", [D, Ttot], BF16)
        winb = p.dram("winb", [D, CO], BF16)
        U = p.dram("U", [CO, Ttot], F32)
        QT = p.dram("QT", [CS, Ttot], F32)
        KT = p.dram("KT", [CS, Ttot], F32)
        KTM = p.dram("KTM", [Ttot, CS], F32)
        VTM = p.dram("VTM", [Ttot, CS], F32)
        OF = p.dram("OF", [B * L, CS], F32)
        OB = p.dram("OB", [B * L, CS], F32)
        emit_cast(p, winb, win, D, CO)
        modp = p.sbuf("modp", [128, 3, 2, nd], F32)
        p.dma("sync", modp[:, :, :, :], modp_d.ap()[:, :, :, :], reads=[modp_d], writes=[modp])
        ident = load_tile(p, "ident", id_d, [128, 128])
        masks = load_tile(p, "masks", msk_d, [128, 6, 128])
        ones = p.sbuf("ones", [128, 128], F32)
        p.op("vector", lambda e: e.memset(ones[:, :], 1.0), writes=[ones])
        Gtm = p.sbuf("Gtm", [128, NBLK, 2 * HPC], F32)
        Btm = p.sbuf("Btm", [128, NBLK, 2 * HPC], F32)
        emit_modulate(p, xall, modp, Hh, D, Ttot, L, B)
        with phase(p):
            psums = Rot([p.psum("g2ps", [128, 512]) for _ in range(6)])
            outs = Rot([p.sbuf("g2o", [128, 512], F32) for _ in range(4)])
            cnt = [0]

            def epi(p, ps, cc, t0, n):
                ob = outs.next()
                cnt[0] += 1
                if cnt[0] % 2 == 0:
                    p.op("scalar", lambda e: e.copy(ob[:, 0:n], ps[:, 0:n]), reads=[ps], writes=[ob])
                else:
                    p.op("vector", lambda e: e.tensor_copy(ob[:, 0:n], ps[:, 0:n]), reads=[ps], writes=[ob])
                p.dma("gpsimd", U.ap()[cc * 128:(cc + 1) * 128, t0:t0 + n], ob[:, 0:n], reads=[ob], writes=[U])
            emit_linear(p, Hh, winb, D, CO, Ttot, epi, psums, TT=512, cb=256, tag="g2")
        seqs = [(b * L, L) for b in range(B)] + [(B * L + b * Lc, Lc) for b in range(B)]
        with phase(p):
            wcv = load_tile(p, "wcv", wcv_d, [128, 3 * HPC, 3])
            PC = 512
            uts = Rot([p.sbuf("g3u", [128, PC + 2], F32) for _ in range(3)])
            accs = Rot([p.sbuf("g3a", [128, PC], F32) for _ in range(3)])
            sls = Rot([p.sbuf("g3s", [128, PC], F32) for _ in range(3)])
            sqs = Rot([p.sbuf("g3q", [128, PC], F32) for _ in range(2)])
            rss = Rot([p.sbuf("g3r", [128, PC], F32) for _ in range(2)])
            nrm = Rot([p.sbuf("g3n", [128, PC], F32) for _ in range(3)])
            tms = Rot([p.sbuf("g3t", [128, PC], F32) for _ in range(3)])
            pss = Rot([p.psum("g3ps", [128, 512]) for _ in range(3)])
            pst = Rot([p.psum("g3pt", [128, 512]) for _ in range(3)])
            for part in range(3):
                for hl in range(HPC):
                    ch = part * HPC + hl
                    row0 = part * CS + hl * 128
                    for (s0, Ls) in seqs:
                        for q0 in range(0, Ls, PC):
                            n = min(PC, Ls - q0)
                            ut, acc, sl = uts.next(), accs.next(), sls.next()
                            lo, hi = q0 - 1, q0 + n + 1
                            a, e_ = max(lo, 0), min(hi, Ls)
                            if lo < 0:
                                p.op("gpsimd", lambda e, ut=ut: e.memset(ut[:, 0:1], 0.0), writes=[ut])
                            if hi > Ls:
                                p.op("gpsimd", lambda e, ut=ut, n=n: e.memset(ut[:, n + 1:n + 2], 0.0), writes=[ut])
                            p.dma("sync", ut[:, a - lo:e_ - lo], U.ap()[row0:row0 + 128, s0 + a:s0 + e_], reads=[U], writes=[ut])
                            p.op("vector", lambda e, ut=ut, acc=acc, ch=ch, n=n: e.tensor_scalar(acc[:, 0:n], ut[:, 0:n], wcv[:, ch, 0:1], None, ALU.mult),
                                 reads=[ut, wcv], writes=[acc])
                            p.op("vector", lambda e, ut=ut, acc=acc, ch=ch, n=n: e.scalar_tensor_tensor(acc[:, 0:n], ut[:, 1:n + 1], wcv[:, ch, 1:2], acc[:, 0:n], ALU.mult, ALU.add),
                                 reads=[ut, wcv, acc], writes=[acc])
                            p.op("vector", lambda e, ut=ut, acc=acc, ch=ch, n=n: e.scalar_tensor_tensor(acc[:, 0:n], ut[:, 2:n + 2], wcv[:, ch, 2:3], acc[:, 0:n], ALU.mult, ALU.add),
                                 reads=[ut, wcv, acc], writes=[acc])
                            p.op("scalar", lambda e, acc=acc, sl=sl, n=n: e.activation(out=sl[:, 0:n], in_=acc[:, 0:n], func=AF.Silu), reads=[acc], writes=[sl])
                            cur = sl
                            if part < 2:
                                sq, rs, nr = sqs.next(), rss.next(), nrm.next()
                                ps = pss.next()
                                p.op("scalar", lambda e, sl=sl, sq=sq, n=n: e.activation(out=sq[:, 0:n], in_=sl[:, 0:n], func=AF.Square), reads=[sl], writes=[sq])
                                p.op("tensor", lambda e, ps=ps, sq=sq, n=n: e.matmul(ps[:, 0:n], ones[:, :], sq[:, 0:n], start=True, stop=True), reads=[ones, sq], writes=[ps])
                                p.op("vector", lambda e, ps=ps, rs=rs, n=n: e.tensor_scalar(rs[:, 0:n], ps[:, 0:n], float(l2_eps), None, ALU.add), reads=[ps], writes=[rs])
                                p.op("scalar", lambda e, rs=rs, n=n: e.activation(out=rs[:, 0:n], in_=rs[:, 0:n], func=AF.Sqrt), reads=[rs], writes=[rs])
                                p.op("vector", lambda e, rs=rs, n=n: e.reciprocal(rs[:, 0:n], rs[:, 0:n]), reads=[rs], writes=[rs])
                                sc = 128 ** -0.5 if part == 0 else 1.0
                                p.op("vector", lambda e, sl=sl, rs=rs, nr=nr, n=n, sc=sc: e.scalar_tensor_tensor(nr[:, 0:n], sl[:, 0:n], float(sc), rs[:, 0:n], ALU.mult, ALU.mult),
                                     reads=[sl, rs], writes=[nr])
                                dst = QT if part == 0 else KT
                                p.dma("gpsimd", dst.ap()[hl * 128:(hl + 1) * 128, s0 + q0:s0 + q0 + n], nr[:, 0:n], reads=[nr], writes=[dst])
                                cur = nr
                            if part >= 1:
                                pt = pst.next()
                                tm = tms.next()
                                nb_ = n // 128
                                for k in range(nb_):
                                    p.op("tensor", lambda e, pt=pt, cur=cur, k=k: e.transpose(pt[:, k * 128:(k + 1) * 128], cur[:, k * 128:(k + 1) * 128], ident[:, :]),
                                         reads=[cur, ident], writes=[pt])
                                p.op("scalar", lambda e, pt=pt, tm=tm, n=n: e.copy(tm[:, 0:n], pt[:, 0:n]), reads=[pt], writes=[tm])
                                dstm = KTM if part == 1 else VTM
                                blk0 = (s0 + q0) // 128
                                p.dma("gpsimd", dstm.ap()[blk0 * 128:(blk0 + nb_) * 128, hl * 128:(hl + 1) * 128].rearrange("(k t) c -> t k c", t=128),
                                      tm[:, 0:n].rearrange("t (k c) -> t k c", c=128), reads=[tm], writes=[dstm])
        with phase(p):
            abc = load_tile(p, "abc", abc_d, [128, 2, 2 * HPC])
            nexpA = p.sbuf("nexpA", [128, 2 * HPC], F32)
            p.op("scalar", lambda e: e.activation(out=nexpA[:, :], in_=abc[:, 1, :], func=AF.Exp), reads=[abc], writes=[nexpA])
            p.op("vector", lambda e: e.tensor_scalar(nexpA[:, :], nexpA[:, :], -1.0, None, ALU.mult), reads=[nexpA], writes=[nexpA])
            abt = Rot([p.sbuf("g4a", [NAB, 2048], F32) for _ in range(2)])
            pts = Rot([p.psum("g4p", [128, 512]) for _ in range(2)])
            tmp = Rot([p.sbuf("g4t", [128, 2 * HPC], F32) for _ in range(15)])
            for t0 in range(0, Ttot, 2048):
                n = min(2048, Ttot - t0)
                at = abt.next()
                p.dma("sync", at[:, 0:n], U.ap()[4 * CS:4 * CS + NAB, t0:t0 + n], reads=[U], writes=[at])
                for k in range(n // 128):
                    blk = t0 // 128 + k
                    pt = pts.next()
                    tt = tmp.next()
                    p.op("tensor", lambda e, pt=pt, at=at, k=k: e.transpose(pt[:, 0:NAB], at[:, k * 128:(k + 1) * 128], ident[0:NAB, 0:NAB]), reads=[at, ident], writes=[pt])
                    p.op("vector", lambda e, pt=pt, tt=tt: e.tensor_tensor(tt[:, :], pt[:, 0:2 * HPC], abc[:, 0, :], ALU.add), reads=[pt, abc], writes=[tt])
                    sa, sb, sc_, sd = tmp.next(), tmp.next(), tmp.next(), tmp.next()
                    p.op("scalar", lambda e, tt=tt: e.activation(out=tt[:, :], in_=tt[:, :], func=AF.Exp), reads=[tt], writes=[tt])
                    p.op("scalar", lambda e, tt=tt, sd=sd: e.activation(out=sd[:, :], in_=tt[:, :], func=AF.Ln, bias=1.0, scale=1.0), reads=[tt], writes=[sd])
                    p.op("vector", lambda e, tt=tt, sa=sa: e.tensor_scalar(sa[:, :], tt[:, :], 2.0, None, ALU.add), reads=[tt], writes=[sa])
                    p.op("vector", lambda e, sa=sa: e.reciprocal(sa[:, :], sa[:, :]), reads=[sa], writes=[sa])
                    p.op("vector", lambda e, tt=tt, sa=sa: e.tensor_tensor(sa[:, :], tt[:, :], sa[:, :], ALU.mult), reads=[tt, sa], writes=[sa])
                    p.op("vector", lambda e, sa=sa, sb=sb: e.tensor_tensor(sb[:, :], sa[:, :], sa[:, :], ALU.mult), reads=[sa], writes=[sb])
                    p.op("vector", lambda e, sb=sb, sc_=sc_: e.tensor_scalar(sc_[:, :], sb[:, :], 1.0 / 15, None, ALU.mult), reads=[sb], writes=[sc_])
                    for cf in (1.0 / 13, 1.0 / 11, 1.0 / 9, 1.0 / 7, 1.0 / 5, 1.0 / 3):
                        p.op("vector", lambda e, sb=sb, sc_=sc_, cf=cf: e.scalar_tensor_tensor(sc_[:, :], sc_[:, :], float(cf), sb[:, :], ALU.add, ALU.mult), reads=[sb, sc_], writes=[sc_])
                    p.op("vector", lambda e, sa=sa, sc_=sc_: e.scalar_tensor_tensor(sc_[:, :], sc_[:, :], 1.0, sa[:, :], ALU.add, ALU.mult), reads=[sa, sc_], writes=[sc_])
                    p.op("vector", lambda e, sc_=sc_: e.tensor_scalar(sc_[:, :], sc_[:, :], 2.0, None, ALU.mult), reads=[sc_], writes=[sc_])
                    p.op("vector", lambda e, tt=tt, sb=sb: e.tensor_scalar(sb[:, :], tt[:, :], 1.0, None, ALU.is_le), reads=[tt], writes=[sb])
                    p.op("vector", lambda e, sc_=sc_, sd=sd: e.tensor_tensor(sc_[:, :], sc_[:, :], sd[:, :], ALU.subtract), reads=[sc_, sd], writes=[sc_])
                    p.op("vector", lambda e, sc_=sc_, sb=sb: e.tensor_tensor(sc_[:, :], sc_[:, :], sb[:, :], ALU.mult), reads=[sc_, sb], writes=[sc_])
                    p.op("vector", lambda e, sc_=sc_, sd=sd: e.tensor_tensor(sc_[:, :], sc_[:, :], sd[:, :], ALU.add), reads=[sc_, sd], writes=[sc_])
                    p.op("vector", lambda e, sc_=sc_, blk=blk: e.tensor_tensor(Gtm[:, blk, :], sc_[:, :], nexpA[:, :], ALU.mult), reads=[sc_, nexpA], writes=[Gtm])
                    p.op("scalar", lambda e, pt=pt, blk=blk: e.activation(out=Btm[:, blk, :], in_=pt[:, 2 * HPC:4 * HPC], func=AF.Sigmoid), reads=[pt], writes=[Btm])
        emit_gdn_scan(p, cfg, QT, KT, KTM, VTM, Gtm, Btm, OF, OB, ident, masks, ones)
        emit_gdn_out(p, cfg, U, OF, OB, normw_d, ident, oout)
        p.emit()
    return nc


def emit_gdn_scan(p, cfg, QT, KT, KTM, VTM, Gtm, Btm, OF, OB, ident, masks, ones):
    D, B, L, Lc, NC = cfg["D"], cfg["B"], cfg["L"], cfg["Lc"], cfg["NC"]
    H = D // 128
    HPC = H // NC
    nbl, nbc = L // 128, Lc // 128
    T = lambda: [128, 128]
    with phase(p):
        def pool(name, k, shape=(128, 128)):
            return Rot([p.sbuf(name, list(shape), F32) for _ in range(k)])
        qTs, kTs, ktms, vtms = pool("sq", 4), pool("sk", 4), pool("skt", 4), pool("sv", 4)
        gcolBs, gcBs, colss = pool("sgb", 3), pool("sgc", 4), pool("scol", 4, (128, 8))
        Es, EMs, attns, bcolBs, EMSs, Ns, NTs, Ps = pool("sE", 3), pool("sEM", 3), pool("sat", 4), pool("sbb", 3), pool("sEMS", 3), pool("sN", 4), pool("sNT", 3), pool("sP", 5)
        Ms, Mts = pool("sM", 6), pool("sMt", 8)
        vbs, kbgs, us, wTs, egBs, qgTs, kgs, vnews, os_ = pool("svb", 3), pool("skbg", 3), pool("su", 3), pool("swT", 3), pool("seg", 3), pool("sqg", 3), pool("skg", 3), pool("svn", 3), pool("so", 3)
        banks = Rot([p.psum("sps", [128, 512]) for _ in range(8)])
        chains = [(b, hl, d) for b in range(B) for hl in range(HPC) for d in range(2)]
        S = {}
        for c in chains:
            S[c] = p.sbuf("S", [128, 128], F32)
            p.op("gpsimd", lambda e, s=S[c]: e.memset(s[:, :], 0.0), writes=[S[c]])
        nsteps = nbc + nbl
        cpy = [0]

        def evac(dst, ps, w=128):
            cpy[0] += 1
            if cpy[0] % 3 == 0:
                p.op("vector", lambda e: e.tensor_copy(dst[:, 0:w], ps[:, 0:w]), reads=[ps], writes=[dst])
            else:
                p.op("scalar", lambda e: e.copy(dst[:, 0:w], ps[:, 0:w]), reads=[ps], writes=[dst])

        def mm(ps, lhsT_buf, lhsT_ap, rhs_buf, rhs_ap, w=128, start=True, stop=True, off=0):
            p.op("tensor", lambda e: e.matmul(ps[:, off:off + w], lhsT_ap, rhs_ap, start=start, stop=stop), reads=[lhsT_buf, rhs_buf], writes=[ps])

        for step in range(nsteps):
            for (b, hl, d) in chains:
                is_ctx = step < nbc
                if is_ctx:
                    k = step if d == 0 else nbc - 1 - step
                    blk = (B * L + b * Lc) // 128 + k
                else:
                    k = (step - nbc) if d == 0 else nbl - 1 - (step - nbc)
                    blk = (b * L) // 128 + k
                Mi = masks[:, 2 * d, :]
                nMs = masks[:, 2 * d + 1, :]
                last = 127 if d == 0 else 0
                Sb = S[(b, hl, d)]
                gi = d * HPC + hl
                qT, kT, ktm, vtm = qTs.next(), kTs.next(), ktms.next(), vtms.next()
                p.dma("sync", qT[:, :], QT.ap()[hl * 128:(hl + 1) * 128, blk * 128:(blk + 1) * 128], reads=[QT], writes=[qT])
                p.dma("sync", kT[:, :], KT.ap()[hl * 128:(hl + 1) * 128, blk * 128:(blk + 1) * 128], reads=[KT], writes=[kT])
                p.dma("sync", ktm[:, :], KTM.ap()[blk * 128:(blk + 1) * 128, hl * 128:(hl + 1) * 128], reads=[KTM], writes=[ktm])
                p.dma("sync", vtm[:, :], VTM.ap()[blk * 128:(blk + 1) * 128, hl * 128:(hl + 1) * 128], reads=[VTM], writes=[vtm])
                gcolB, gcB, cols = gcolBs.next(), gcBs.next(), colss.next()
                p.op("vector", lambda e, gcolB=gcolB, blk=blk, gi=gi: e.tensor_scalar(gcolB[:, :], ones[:, :], Gtm[:, blk, gi:gi + 1], None, ALU.mult), reads=[ones, Gtm], writes=[gcolB])
                ps1 = banks.next()
                mm(ps1, gcolB, gcolB[:, :], masks, Mi)
                mm(ps1, masks, Mi, Gtm, Gtm[:, blk, gi:gi + 1], w=1, off=128)
                p.op("scalar", lambda e, gcB=gcB, ps1=ps1: e.copy(gcB[:, :], ps1[:, 0:128]), reads=[ps1], writes=[gcB])
                p.op("scalar", lambda e, cols=cols, ps1=ps1: e.copy(cols[:, 0:1], ps1[:, 128:129]), reads=[ps1], writes=[cols])
                p.op("scalar", lambda e, cols=cols: e.activation(out=cols[:, 1:2], in_=cols[:, 0:1], func=AF.Exp), reads=[cols], writes=[cols])
                p.op("scalar", lambda e, cols=cols, gcB=gcB, last=last: e.activation(out=cols[:, 2:3], in_=cols[:, 0:1], func=AF.Exp, scale=-1.0, bias=gcB[:, last:last + 1]),
                     reads=[cols, gcB], writes=[cols])
                p.op("scalar", lambda e, cols=cols, gcB=gcB, last=last: e.activation(out=cols[:, 3:4], in_=gcB[:, last:last + 1], func=AF.Exp), reads=[gcB], writes=[cols])
                p.op("vector", lambda e, cols=cols, blk=blk, gi=gi: e.tensor_tensor(cols[:, 4:5], cols[:, 1:2], Btm[:, blk, gi:gi + 1], ALU.mult), reads=[cols, Btm], writes=[cols])
                E, EM = Es.next(), EMs.next()
                p.op("vector", lambda e, E=E, gcB=gcB, cols=cols: e.tensor_scalar(E[:, :], gcB[:, :], cols[:, 0:1], 0.0, ALU.subtract, ALU.min), reads=[gcB, cols], writes=[E])
                p.op("scalar", lambda e, E=E: e.activation(out=E[:, :], in_=E[:, :], func=AF.Exp), reads=[E], writes=[E])
                p.op("vector", lambda e, E=E, EM=EM, Mi=Mi: e.tensor_tensor(EM[:, :], E[:, :], Mi, ALU.mult), reads=[E, masks], writes=[EM])
                psA = banks.next()
                mm(psA, kT, kT[:, :], kT, kT[:, :])
                mm(psA, kT, kT[:, :], qT, qT[:, :], off=128)
                attn = attns.next()
                if not is_ctx:
                    p.op("vector", lambda e, attn=attn, psA=psA, EM=EM: e.tensor_tensor(attn[:, :], psA[:, 128:256], EM[:, :], ALU.mult), reads=[psA, EM], writes=[attn])
                bcolB = bcolBs.next()
                p.op("vector", lambda e, bcolB=bcolB, blk=blk, gi=gi: e.tensor_scalar(bcolB[:, :], ones[:, :], Btm[:, blk, gi:gi + 1], None, ALU.mult), reads=[ones, Btm], writes=[bcolB])
                ps3 = banks.next()
                mm(ps3, bcolB, bcolB[:, :], ident, ident[:, :])
                EMS, Nn = EMSs.next(), Ns.next()
                p.op("gpsimd", lambda e, EMS=EMS, EM=EM, nMs=nMs: e.tensor_tensor(EMS[:, :], EM[:, :], nMs, ALU.mult), reads=[EM, masks], writes=[EMS])
                p.op("vector", lambda e, EMS=EMS, ps3=ps3: e.tensor_tensor(EMS[:, :], EMS[:, :], ps3[:, 0:128], ALU.mult), reads=[EMS, ps3], writes=[EMS])
                p.op("vector", lambda e, Nn=Nn, psA=psA, EMS=EMS: e.tensor_tensor(Nn[:, :], psA[:, 0:128], EMS[:, :], ALU.mult), reads=[psA, EMS], writes=[Nn])
                ps4 = banks.next()
                NT = NTs.next()
                p.op("tensor", lambda e, ps4=ps4, Nn=Nn: e.transpose(ps4[:, 0:128], Nn[:, :], ident[:, :]), reads=[Nn, ident], writes=[ps4])
                evac(NT, ps4)
                BD = masks[:, 4, :]
                ODm = masks[:, 5, :]
                Nd, NTd, No = Ms.next(), Mts.next(), Ns.next()
                p.op("gpsimd", lambda e, Nd=Nd, Nn=Nn, BD=BD: e.tensor_tensor(Nd[:, :], Nn[:, :], BD, ALU.mult), reads=[Nn, masks], writes=[Nd])
                p.op("gpsimd", lambda e, NTd=NTd, NT=NT, BD=BD: e.tensor_tensor(NTd[:, :], NT[:, :], BD, ALU.mult), reads=[NT, masks], writes=[NTd])
                p.op("gpsimd", lambda e, No=No, Nn=Nn, ODm=ODm: e.tensor_tensor(No[:, :], Nn[:, :], ODm, ALU.mult), reads=[Nn, masks], writes=[No])
                P = Ps.next()
                p.op("gpsimd", lambda e, P=P, Nd=Nd: e.tensor_tensor(P[:, :], Nd[:, :], ident[:, :], ALU.add), reads=[Nd, ident], writes=[P])
                M, Mt = Nd, NTd
                for lev in range(1, 5):
                    lastlev = (lev == 4)
                    M2 = None
                    if not lastlev:
                        psM = banks.next()
                        mm(psM, Mt, Mt[:, :], M, M[:, :])
                        M2 = Ms.next()
                        evac(M2, psM)
                    psMt = banks.next()
                    mm(psMt, M, M[:, :], Mt, Mt[:, :])
                    Mt2 = Mts.next()
                    evac(Mt2, psMt)
                    psP = banks.next()
                    mm(psP, Mt2, Mt2[:, :], P, P[:, :])
                    p.op("vector", lambda e, P=P, psP=psP: e.tensor_tensor(P[:, :], P[:, :], psP[:, 0:128], ALU.add), reads=[P, psP], writes=[P])
                    M, Mt = M2, Mt2
                Dinv = P
                psD = banks.next()
                DinvT = Mts.next()
                p.op("tensor", lambda e, psD=psD, Dinv=Dinv: e.transpose(psD[:, 0:128], Dinv[:, :], ident[:, :]), reads=[Dinv, ident], writes=[psD])
                evac(DinvT, psD)
                psG = banks.next()
                GT = Ms.next()
                mm(psG, No, No[:, :], DinvT, DinvT[:, :])
                evac(GT, psG)
                psY = banks.next()
                Y1 = Mts.next()
                mm(psY, GT, GT[:, :], Dinv, Dinv[:, :])
                evac(Y1, psY)
                psY2 = banks.next()
                mm(psY2, GT, GT[:, :], Y1, Y1[:, :])
                X1 = Ps.next()
                p.op("vector", lambda e, X1=X1, Dinv=Dinv, psY2=psY2: e.tensor_tensor(X1[:, :], Dinv[:, :], psY2[:, 0:128], ALU.add), reads=[Dinv, psY2], writes=[X1])
                psY3 = banks.next()
                mm(psY3, GT, GT[:, :], X1, X1[:, :])
                P = Ps.next()
                p.op("vector", lambda e, P=P, X1=X1, psY3=psY3: e.tensor_tensor(P[:, :], X1[:, :], psY3[:, 0:128], ALU.add), reads=[X1, psY3], writes=[P])
                vb, kbg = vbs.next(), kbgs.next()
                p.op("scalar", lambda e, vb=vb, vtm=vtm, blk=blk, gi=gi: e.activation(out=vb[:, :], in_=vtm[:, :], func=AF.Identity, scale=Btm[:, blk, gi:gi + 1], bias=0.0), reads=[vtm, Btm], writes=[vb])
                p.op("scalar", lambda e, kbg=kbg, ktm=ktm, cols=cols: e.activation(out=kbg[:, :], in_=ktm[:, :], func=AF.Identity, scale=cols[:, 4:5], bias=0.0), reads=[ktm, cols], writes=[kbg])
                psU = banks.next()
                mm(psU, P, P[:, :], vb, vb[:, :])
                mm(psU, kbg, kbg[:, :], P, P[:, :], off=128)
                u, wT = us.next(), wTs.next()
                evac(u, psU)
                p.op("scalar", lambda e, wT=wT, psU=psU: e.copy(wT[:, :], psU[:, 128:256]), reads=[psU], writes=[wT])
                kg = kgs.next()
                p.op("scalar", lambda e, kg=kg, ktm=ktm, cols=cols: e.activation(out=kg[:, :], in_=ktm[:, :], func=AF.Identity, scale=cols[:, 2:3], bias=0.0), reads=[ktm, cols], writes=[kg])
                psS = banks.next()
                mm(psS, wT, wT[:, :], Sb, Sb[:, :])
                vnew = vnews.next()
                p.op("vector", lambda e, vnew=vnew, u=u, psS=psS: e.tensor_tensor(vnew[:, :], u[:, :], psS[:, 0:128], ALU.subtract), reads=[u, psS], writes=[vnew])
                if not is_ctx:
                    egB, qgT = egBs.next(), qgTs.next()
                    p.op("scalar", lambda e, egB=egB, gcB=gcB: e.activation(out=egB[:, :], in_=gcB[:, :], func=AF.Exp), reads=[gcB], writes=[egB])
                    p.op("gpsimd", lambda e, qgT=qgT, qT=qT, egB=egB: e.tensor_tensor(qgT[:, :], qT[:, :], egB[:, :], ALU.mult), reads=[qT, egB], writes=[qgT])
                    psO = banks.next()
                    mm(psO, qgT, qgT[:, :], Sb, Sb[:, :], start=True, stop=False)
                    mm(psO, attn, attn[:, :], vnew, vnew[:, :], start=False, stop=True)
                    o = os_.next()
                    evac(o, psO)
                    dst = OF if d == 0 else OB
                    r0 = blk * 128
                    p.dma("gpsimd", dst.ap()[r0:r0 + 128, hl * 128:(hl + 1) * 128], o[:, :], reads=[o], writes=[dst])
                psK = banks.next()
                mm(psK, kg, kg[:, :], vnew, vnew[:, :])
                p.op("vector", lambda e, Sb=Sb, cols=cols, psK=psK: e.scalar_tensor_tensor(Sb[:, :], Sb[:, :], cols[:, 3:4], psK[:, 0:128], ALU.mult, ALU.add),
                     reads=[Sb, cols, psK], writes=[Sb])


def emit_gdn_out(p, cfg, U, OF, OB, normw_d, ident, oout):
    D, B, L, Lc, NC = cfg["D"], cfg["B"], cfg["L"], cfg["Lc"], cfg["NC"]
    rms_eps = cfg["rms_eps"]
    H = D // 128
    HPC = H // NC
    CS = HPC * 128
    with phase(p):
        normw = load_tile(p, "normw", normw_d, [128, 128])
        pool = lambda name, k, shape=(128, 128): Rot([p.sbuf(name, list(shape), F32) for _ in range(k)])
        ofs, obs, zs, szs, sqs, ys, sss = pool("oof", 3), pool("oob", 3), pool("oz", 3), pool("osz", 3), pool("osq", 2), pool("oy", 3), pool("oss", 3, (128, 2))
        pts = Rot([p.psum("opt", [128, 512]) for _ in range(3)])
        ybs = Rot([p.sbuf("oyb", [128, 128], BF16) for _ in range(3)])
        for blk in range(B * L // 128):
            for hl in range(HPC):
                of, ob, z, sz, sq, y, ss = ofs.next(), obs.next(), zs.next(), szs.next(), sqs.next(), ys.next(), sss.next()
                r0 = blk * 128
                p.dma("sync", of[:, :], OF.ap()[r0:r0 + 128, hl * 128:(hl + 1) * 128], reads=[OF], writes=[of])
                p.dma("sync", ob[:, :], OB.ap()[r0:r0 + 128, hl * 128:(hl + 1) * 128], reads=[OB], writes=[ob])
                p.dma("sync", z[:, :], U.ap()[3 * CS + hl * 128:3 * CS + (hl + 1) * 128, r0:r0 + 128], reads=[U], writes=[z])
                pt = pts.next()
                p.op("tensor", lambda e, pt=pt, z=z: e.transpose(pt[:, 0:128], z[:, :], ident[:, :]), reads=[z, ident], writes=[pt])
                p.op("scalar", lambda e, pt=pt, sz=sz: e.activation(out=sz[:, :], in_=pt[:, 0:128], func=AF.Silu), reads=[pt], writes=[sz])
                p.op("vector", lambda e, of=of, ob=ob: e.tensor_tensor(of[:, :], of[:, :], ob[:, :], ALU.add), reads=[of, ob], writes=[of])
                p.op("scalar", lambda e, of=of, sq=sq, ss=ss: e.activation(out=sq[:, :], in_=of[:, :], func=AF.Square, accum_out=ss[:, 0:1]), reads=[of], writes=[sq, ss])
                p.op("vector", lambda e, ss=ss: e.tensor_scalar(ss[:, 1:2], ss[:, 0:1], 1.0 / 128, float(rms_eps), ALU.mult, ALU.add), reads=[ss], writes=[ss])
                p.op("scalar", lambda e, ss=ss: e.activation(out=ss[:, 1:2], in_=ss[:, 1:2], func=AF.Sqrt), reads=[ss], writes=[ss])
                p.op("vector", lambda e, ss=ss: e.reciprocal(ss[:, 1:2], ss[:, 1:2]), reads=[ss], writes=[ss])
                p.op("vector", lambda e, y=y, of=of, ss=ss: e.scalar_tensor_tensor(y[:, :], of[:, :], ss[:, 1:2], normw[:, :], ALU.mult, ALU.mult), reads=[of, ss, normw], writes=[y])
                yb = ybs.next()
                p.op("gpsimd", lambda e, y=y, sz=sz, yb=yb: e.tensor_tensor(yb[:, :], y[:, :], sz[:, :], ALU.mult), reads=[y, sz], writes=[yb])
                p.dma("gpsimd", oout.ap()[r0:r0 + 128, hl * 128:(hl + 1) * 128], yb[:, :], reads=[yb], writes=[oout])


import ml_dtypes
BF = ml_dtypes.bfloat16

def ptile(v, nd=None):
    v = np.asarray(v, np.float32)
    return np.ascontiguousarray(v.reshape(-1, 128).T)

def back_inputs(cfg, j, has_ctx, zT_lat, xT_lat, zT_ctx, xT_ctx, mods, lnp, b_out, wmix, wup, wdw, bdw, wdown):
    D, F, TO, CO, GW, B, L, Lc, NC = (cfg[k] for k in ("D", "F", "TO", "CO", "GW", "B", "L", "Lc", "NC"))
    HL = GW + 1
    TL = TO + 2 * HL
    Tc = TL + (CO + 2 if has_ctx else 0)
    cpb = NC // B
    b, r = j // cpb, j % cpb
    zT = np.zeros((D, Tc), BF)
    xT = np.zeros((D, Tc), np.float32)
    cmask = np.zeros((128, Tc), np.float32)
    lo = r * TO - HL
    hi = r * TO + TO + HL
    a, e = max(lo, 0), min(hi, L)
    zT[:, a - lo:e - lo] = zT_lat[:, b * L + a:b * L + e]
    xT[:, a - lo:e - lo] = xT_lat[:, b * L + a:b * L + e]
    cmask[:, a - lo:e - lo] = 1.0
    if has_ctx:
        lo = r * CO - 1
        hi = r * CO + CO + 1
        a, e = max(lo, 0), min(hi, Lc)
        zT[:, TL + a - lo:TL + e - lo] = zT_ctx[:, b * Lc + a:b * Lc + e]
        xT[:, TL + a - lo:TL + e - lo] = xT_ctx[:, b * Lc + a:b * Lc + e]
        cmask[:, TL + a - lo:TL + e - lo] = 1.0
    tok = (r * TO - HL) + np.arange(TL)
    gc = tok % GW
    mcl = np.broadcast_to((gc != GW - 1).astype(np.float32), (128, TL)).copy()
    mcr = np.broadcast_to((gc != 0).astype(np.float32), (128, TL)).copy()
    def modpack(c):
        return np.ascontiguousarray(np.stack([ptile(mods[c, k]) for k in (2, 4, 3, 5)], axis=1))
    nf = F // 128
    im = {
        "zT": zT, "xT": xT, "cmask": cmask, "mcl": mcl, "mcr": mcr,
        "modl": modpack(b), "modc": modpack(2),
        "lnp": np.ascontiguousarray(np.stack([ptile(v) for v in lnp], axis=1)),
        "bo": ptile(b_out) if b_out is not None else np.zeros((128, D // 128), np.float32),
        "wmix": np.ascontiguousarray(wmix, np.float32), "wup": np.ascontiguousarray(wup, np.float32),
        "wdw": np.ascontiguousarray(wdw.reshape(9, nf, 128).transpose(2, 1, 0), np.float32),
        "bdw": ptile(bdw), "wdown": np.ascontiguousarray(wdown, np.float32),
    }
    return im

def back_gather(cfg, outs, has_ctx):
    D, TO, CO, GW, B, L, Lc, NC = (cfg[k] for k in ("D", "TO", "CO", "GW", "B", "L", "Lc", "NC"))
    HL = GW + 1
    TL = TO + 2 * HL
    cpb = NC // B
    xl = np.zeros((D, B * L), np.float32)
    xc = np.zeros((D, B * Lc), np.float32) if has_ctx else None
    for j, o in enumerate(outs):
        b, r = j // cpb, j % cpb
        xl[:, b * L + r * TO:b * L + (r + 1) * TO] = o[:, HL:HL + TO]
        if has_ctx:
            xc[:, b * Lc + r * CO:b * Lc + (r + 1) * CO] = o[:, TL + 1:TL + 1 + CO]
    return xl, xc

def ada_inputs(D, NC, j, conds, w_adas, b_adas):
    ncols = 6 * D // NC
    cond = np.ascontiguousarray(np.stack([ptile(c) for c in conds], axis=2))
    im = {"cond": cond}
    bl = []
    for l, (w, b) in enumerate(zip(w_adas, b_adas)):
        im["w%d" % l] = np.ascontiguousarray(w[:, j * ncols:(j + 1) * ncols], np.float32)
        bl.append(ptile(b[j * ncols:(j + 1) * ncols]))
    im["b"] = np.ascontiguousarray(np.stack(bl, axis=1))
    return im

def ada_gather(D, NC, outs, nlayers):
    ncols = 6 * D // NC
    res = []
    for l in range(nlayers):
        m = np.zeros((3, 6 * D), np.float32)
        for j, o in enumerate(outs):
            blk = o[:, l].transpose(2, 1, 0).reshape(3, ncols)
            m[:, j * ncols:(j + 1) * ncols] = blk
        res.append(m.reshape(3, 6, D))
    return res

def hy_posfeat(Lx, emb=33):
    f32 = np.float32
    t = np.linspace(0.0, 1.0, Lx, dtype=f32)
    bands = (emb - 1) // 2
    ang = (f32(2.0 * np.pi) * np.arange(Lx, dtype=f32) / f32(Lx)).astype(f32)
    f = np.linspace(1e-4, bands - 1, bands, dtype=f32)
    fa = (f[None, :] * ang[:, None]).astype(f32)
    z = np.concatenate([t[:, None], np.cos(fa), -np.sin(fa)], axis=-1).astype(f32)
    pos = np.abs(np.arange(2 * Lx - 1) - (Lx - 1))
    pos = np.concatenate([pos, [0]])
    return np.ascontiguousarray(z[pos].T), t[pos]

def hy_deltas(D):
    import math
    max_decay = math.log(1e-2) / 0.3
    min_decay = math.log(1e-2) / 1.5
    return np.abs(np.linspace(min_decay, max_decay, D, dtype=np.float32))

def hyfront_inputs(cfg, j, xall, mods, P):
    D, B, L, Lc, NC = cfg["D"], cfg["B"], cfg["L"], cfg["Lc"], cfg["NC"]
    CS = D // NC
    sl = lambda part: slice(part * D + j * CS, part * D + (j + 1) * CS)
    cols = np.concatenate([np.arange(part * D + j * CS, part * D + (j + 1) * CS) for part in range(3)])
    modp = np.stack([np.stack([ptile(1.0 + mods[c, 1]), ptile(mods[c, 0])], axis=1) for c in range(3)], axis=1)
    wsh = np.zeros((CS // 64, 64, 3, 4), np.float32)
    for part in range(3):
        wpart = P["hy_w_short"][:, sl(part)]
        bpart = P["hy_b_short"][sl(part)]
        wsh[:, :, part, 0:3] = wpart.T.reshape(CS // 64, 64, 3)
        wsh[:, :, part, 3] = bpart.reshape(CS // 64, 64)
    zext, tt = hy_posfeat(L)
    zextc, ttc = hy_posfeat(Lc)
    fpv = np.zeros((64, 8), np.float32)
    for i in range(3):
        fpv[:, i] = P["hy_f_freq"]
    fpv[:, 3] = P["hy_f_b1"]; fpv[:, 4] = P["hy_f_b2"]; fpv[:, 5] = P["hy_f_b3"]
    wout = P["hy_f_wout"].reshape(64, 4, D)[:, :, j * CS:(j + 1) * CS]
    delt = hy_deltas(D)[j * CS:(j + 1) * CS]
    eye = np.eye(128, dtype=np.float32)
    im = {
        "xall": xall, "modp": np.ascontiguousarray(modp, np.float32),
        "win": np.ascontiguousarray(P["hy_w_in"][:, cols], np.float32), "bin": ptile(P["hy_b_in"][cols]),
        "wsh": wsh, "zext": zext, "zextc": zextc,
        "text": np.ascontiguousarray(np.broadcast_to(tt, (128, tt.size)), np.float32),
        "textc": np.ascontiguousarray(np.broadcast_to(ttc, (128, ttc.size)), np.float32),
        "w1": np.ascontiguousarray(P["hy_f_w1"], np.float32), "w2": np.ascontiguousarray(P["hy_f_w2"], np.float32),
        "w3": np.ascontiguousarray(P["hy_f_w3"], np.float32), "fp": fpv,
        "wout": np.ascontiguousarray(wout, np.float32), "negdelta": ptile(-delt),
        "skip": np.ascontiguousarray(P["hy_skip"][None, :, j * CS:(j + 1) * CS], np.float32),
        "ident": eye.astype(BF), "antiid": np.ascontiguousarray(eye[::-1]).astype(BF),
    }
    return im

def gdn_masks():
    j = np.arange(128)[:, None]; i = np.arange(128)[None, :]
    m = np.zeros((128, 6, 128), np.float32)
    m[:, 4] = ((i // 32) == (j // 32)); m[:, 5] = 1.0 - m[:, 4]
    m[:, 0] = (i >= j); m[:, 1] = -(i > j).astype(np.float32)
    m[:, 2] = (i <= j); m[:, 3] = -(i < j).astype(np.float32)
    return m

def gdnfront_inputs(cfg, j, xall, mods, P):
    D, B, L, Lc, NC = cfg["D"], cfg["B"], cfg["L"], cfg["Lc"], cfg["NC"]
    H = D // 128; HPC = H // NC; CS = HPC * 128
    heads = np.arange(j * HPC, (j + 1) * HPC)
    ch = np.concatenate([np.arange(h * 128, (h + 1) * 128) for h in heads])
    cols = np.concatenate([part * D + ch for part in range(4)])
    abcols = np.array([4 * D + ab * 2 * H + d * H + h for ab in range(2) for d in range(2) for h in heads])
    win = np.zeros((D, 4 * CS + 128), np.float32)
    win[:, :4 * CS] = P["gdn_w_in"][:, cols]
    win[:, 4 * CS:4 * CS + abcols.size] = P["gdn_w_in"][:, abcols]
    modp = np.stack([np.stack([ptile(1.0 + mods[c, 1]), ptile(mods[c, 0])], axis=1) for c in range(3)], axis=1)
    wcv = np.zeros((128, 3 * HPC, 3), np.float32)
    for part in range(3):
        for hl, h in enumerate(heads):
            wcv[:, part * HPC + hl, :] = P["gdn_w_conv"][:, part * D + h * 128: part * D + (h + 1) * 128].T
    abc = np.zeros((128, 2, 2 * HPC), np.float32)
    for d in range(2):
        for hl, h in enumerate(heads):
            abc[:, 0, d * HPC + hl] = P["gdn_dt_bias"][d, h]
            abc[:, 1, d * HPC + hl] = P["gdn_a_log"][d, h]
    return {"xall": xall, "modp": np.ascontiguousarray(modp, np.float32), "win": win, "wcv": wcv, "abc": abc,
            "normw": np.ascontiguousarray(np.broadcast_to(P["gdn_norm_w"].astype(np.float32), (128, 128))),
            "ident": np.eye(128, dtype=np.float32), "masks": gdn_masks()}


D_MODEL, BATCH, SEQ, DEPTH = 4096, 2, 8192, 2
CTX_LEN, D_FF, GRID_W, NCORES = 256, 11008, 64, 8
CFG = dict(D=D_MODEL, F=D_FF, TO=SEQ * BATCH // NCORES, CO=CTX_LEN * BATCH // NCORES, GW=GRID_W, B=BATCH, L=SEQ, Lc=CTX_LEN,
           NC=NCORES, alpha=(2 * DEPTH) ** 0.25, ln_eps=1e-5, rms_eps=1e-6, l2_eps=1e-6)
_PROGS = {}


def _prog(key, fn):
    if key not in _PROGS:
        _PROGS[key] = fn()
    return _PROGS[key]


def _run(nc, ims):
    res = run_bass_kernel_spmd(nc, ims, core_ids=list(range(NCORES)))
    return res.results


def kernel(**inp):
    cfg = CFG
    D, B, L, Lc, NC = cfg["D"], cfg["B"], cfg["L"], cfg["Lc"], cfg["NC"]
    f32 = np.float32
    x = np.asarray(inp["x"], f32)
    ctx = np.asarray(inp["ctx"], f32)
    conds = np.concatenate([np.asarray(inp["c"], f32), np.asarray(inp["c_ctx"], f32)[None, :]], axis=0)
    lay = [{k[3:]: np.asarray(v) for k, v in inp.items() if k.startswith("l%d_" % i)} for i in range(DEPTH)]
    nc_a = _prog("ada", lambda: build_ada(D, 6 * D // NC, DEPTH))
    outs = _run(nc_a, [ada_inputs(D, NC, j, conds, [lay[i]["w_ada"] for i in range(DEPTH)], [lay[i]["b_ada"] for i in range(DEPTH)]) for j in range(NC)])
    mods = ada_gather(D, NC, [o["out"] for o in outs], DEPTH)
    xT = np.ascontiguousarray(x.reshape(B * L, D).T)
    cT = np.ascontiguousarray(ctx.reshape(B * Lc, D).T)
    P0 = lay[0]
    xall = np.ascontiguousarray(np.concatenate([xT, cT], axis=1))
    nc_b = _prog("hy", lambda: build_hyfront(cfg))
    outs = _run(nc_b, [hyfront_inputs(cfg, j, xall, mods[0], P0) for j in range(NC)])
    zT = np.concatenate([o["zout"] for o in outs], axis=0)
    del outs
    lnp0 = [P0["ln1_g"], P0["ln1_b"], P0["ln2_g"], P0["ln2_b"]]
    nc_c0 = _prog("back0", lambda: build_back(cfg, True, True))
    outs = _run(nc_c0, [back_inputs(cfg, j, True, zT[:, :B * L], xT, zT[:, B * L:], cT, mods[0], lnp0, P0["hy_b_out"], P0["hy_w_out"],
                                    P0["ffn_w_up"], P0["ffn_w_dw"], P0["ffn_b_dw"], P0["ffn_w_down"]) for j in range(NC)])
    x1T, c1T = back_gather(cfg, [o["out"] for o in outs], True)
    del outs, zT, xall
    P1 = lay[1]
    xall = np.ascontiguousarray(np.concatenate([x1T, c1T], axis=1))
    nc_d = _prog("gdn", lambda: build_gdnfront(cfg))
    outs = _run(nc_d, [gdnfront_inputs(cfg, j, xall, mods[1], P1) for j in range(NC)])
    oT = np.ascontiguousarray(np.concatenate([o["oout"] for o in outs], axis=1).T)
    del outs, xall
    lnp1 = [P1["ln1_g"], P1["ln1_b"], P1["ln2_g"], P1["ln2_b"]]
    nc_c1 = _prog("back1", lambda: build_back(cfg, False, False))
    outs = _run(nc_c1, [back_inputs(cfg, j, False, oT, x1T, None, None, mods[1], lnp1, None, P1["gdn_w_out"],
                                    P1["ffn_w_up"], P1["ffn_w_dw"], P1["ffn_b_dw"], P1["ffn_w_down"]) for j in range(NC)])
    x2T, _ = back_gather(cfg, [o["out"] for o in outs], False)
    return np.ascontiguousarray(x2T.T).reshape(B, L, D).astype(f32)
```

```python
import contextlib
import numpy as np
import concourse.bass as bass
import concourse.mybir as mybir
from concourse.bass_utils import run_bass_kernel_spmd

F32 = mybir.dt.float32
BF16 = mybir.dt.bfloat16
ALU = mybir.AluOpType
AF = mybir.ActivationFunctionType
AX = mybir.AxisListType

ENGS = ("tensor", "vector", "scalar", "gpsimd", "sync")
N_DMA_SEMS = 8


class Buf:
    __slots__ = ("t", "name", "last_w", "readers", "excl")

    def __init__(self, t, name="", excl=False):
        self.t = t
        self.name = name
        self.excl = excl
        self.last_w = None
        self.readers = {}

    def __getitem__(self, idx):
        return self.t[idx]

    def ap(self):
        return self.t.ap()


class Prog:
    def __init__(self, nc, stack, same_engine_sync=True):
        self.nc = nc
        self.ops = {e: [] for e in ENGS}
        self.cnt = {}
        self.known = {e: {} for e in ENGS}
        self.dma_i = {e: 0 for e in ENGS}
        self.same_engine_sync = same_engine_sync
        self.stack = stack
        self.sems = {}
        self.uid = 0

    def sbuf(self, name, shape, dt):
        self.uid += 1
        t = self.stack.enter_context(self.nc.sbuf_tensor("%s_%d" % (name, self.uid), list(shape), dt))
        return Buf(t, name)

    def psum(self, name, shape, dt=F32):
        self.uid += 1
        t = self.stack.enter_context(self.nc.psum_tensor("%s_%d" % (name, self.uid), list(shape), dt))
        return Buf(t, name, excl=True)

    def dram(self, name, shape, dt, kind="Internal"):
        t = self.nc.dram_tensor(name, list(shape), dt, kind=kind)
        return Buf(t, name)

    def _deps(self, eng, reads, writes):
        need = {}

        def add(k, v):
            if need.get(k, 0) < v:
                need[k] = v
        for b in reads:
            if b.last_w is not None:
                add(*b.last_w)
        for b in writes:
            if b.last_w is not None:
                add(*b.last_w)
            for k, v in b.readers.items():
                add(k, v)
        waits = []
        kn = self.known[eng]
        for k, v in need.items():
            if k == eng and (eng == "tensor" or not self.same_engine_sync):
                continue
            if kn.get(k, 0) >= v:
                continue
            kn[k] = v
            waits.append((k, v))
        return waits

    def _commit(self, key, val, reads, writes):
        for b in reads:
            if b.readers.get(key, 0) < val:
                b.readers[key] = val
        for b in writes:
            b.last_w = (key, val)
            b.readers = {}

    def op(self, eng, fn, reads=(), writes=()):
        if any(b.excl for b in reads):
            writes = list(writes) + [b for b in reads if b.excl and b not in writes]
            reads = [b for b in reads if not b.excl]
        waits = self._deps(eng, reads, writes)
        val = self.cnt.get(eng, 0) + 1
        self.cnt[eng] = val
        self.ops[eng].append((waits, fn, eng, 1))
        self._commit(eng, val, reads, writes)

    def dma(self, eng, out_ap, in_ap, reads=(), writes=()):
        i = self.dma_i[eng]
        self.dma_i[eng] = i + 1
        key = "dma_%s_%d" % (eng, i % N_DMA_SEMS)
        waits = self._deps(eng, reads, writes)
        prev = self.cnt.get(key, 0)
        kn = self.known[eng]
        if prev > 0 and kn.get(key, 0) < prev:
            kn[key] = prev
            waits.append((key, prev))
        val = prev + 16
        self.cnt[key] = val
        self.ops[eng].append((waits, lambda e: e.dma_start(out=out_ap, in_=in_ap), key, 16))
        self._commit(key, val, reads, writes)

    def barrier(self):
        snap = dict(self.cnt)
        for eng in ENGS:
            waits = []
            kn = self.known[eng]
            for k, v in snap.items():
                if k == eng:
                    continue
                if kn.get(k, 0) < v:
                    kn[k] = v
                    waits.append((k, v))
            if waits:
                self.ops[eng].append((waits, None, None, 0))

    def emit(self):
        nc = self.nc
        keys = list(self.cnt.keys())
        for k in keys:
            self.sems[k] = self.stack.enter_context(nc.semaphore("s_" + k))
        fin = [(k, self.cnt[k]) for k in keys]
        self.ops["sync"].append((fin, None, None, 0))
        with nc.Block() as block:
            def mk(engname):
                def body(e):
                    for waits, fn, key, inc in self.ops[engname]:
                        for (k, v) in waits:
                            e.wait_ge(self.sems[k], v)
                        if fn is not None:
                            fn(e).then_inc(self.sems[key], inc)
                return body
            block.tensor(mk("tensor"))
            block.vector(mk("vector"))
            block.scalar(mk("scalar"))
            block.gpsimd(mk("gpsimd"))
            block.sync(mk("sync"))


class Rot:
    def __init__(self, bufs):
        self.bufs = bufs
        self.i = 0

    def next(self):
        b = self.bufs[self.i % len(self.bufs)]
        self.i += 1
        return b


def emit_cast(p, dst, src, rows, cols, rb=None):
    if rb is None:
        rb = max(1, min(rows, (4 << 20) // (cols * 4)))
    r = 0
    while r < rows:
        n = min(rb, rows - r)
        p.dma("gpsimd", dst.ap()[r:r + n, :], src.ap()[r:r + n, :], reads=[src], writes=[dst])
        r += n


def emit_linear(p, inT, W, Ci, Co, T, epilogue, psums, TT=512, cb=256, kg=None, tag="lin",
                in_col0=0, w_col0=0):
    nk = Ci // 128
    if kg is None:
        kg = nk
        while kg * cb * 2 > 16384:
            kg = (kg + 1) // 2
    ngrp = (nk + kg - 1) // kg
    xslots = 2 if nk * TT * 2 <= 40000 else 1
    xin = Rot([p.sbuf(tag + "_x", [128, nk, TT], BF16) for _ in range(xslots)])
    wts = Rot([p.sbuf(tag + "_w", [128, kg, cb], BF16) for _ in range(3)])
    inv = inT.ap().rearrange("(k p) t -> p k t", p=128)
    wv = W.ap().rearrange("(k p) c -> p k c", p=128)
    nchunk = cb // 128
    t0 = 0
    while t0 < T:
        n = min(TT, T - t0)
        xb = xin.next()
        p.dma("sync", xb[:, :, 0:n], inv[:, :, in_col0 + t0:in_col0 + t0 + n], reads=[inT], writes=[xb])
        for c0 in range(0, Co, cb):
            ncc = min(nchunk, (Co - c0) // 128)
            pss = [psums.next() for _ in range(ncc)]
            for g in range(ngrp):
                k0 = g * kg
                nkk = min(kg, nk - k0)
                wb = wts.next()
                p.dma("sync", wb[:, 0:nkk, 0:ncc * 128], wv[:, k0:k0 + nkk, w_col0 + c0:w_col0 + c0 + ncc * 128],
                      reads=[W], writes=[wb])
                for c in range(ncc):
                    for k in range(nkk):
                        first = (g == 0 and k == 0)
                        last = (g == ngrp - 1 and k == nkk - 1)
                        p.op("tensor",
                             (lambda e, ps=pss[c], wb=wb, xb=xb, k=k, kk=k0 + k, c=c, n=n, first=first, last=last:
                              e.matmul(ps[:, 0:n], wb[:, k, c * 128:(c + 1) * 128], xb[:, kk, 0:n],
                                       start=first, stop=last)),
                             reads=[wb, xb], writes=[pss[c]])
                if g == ngrp - 1:
                    for c in range(ncc):
                        epilogue(p, pss[c], (c0 // 128) + c, t0, n)
        t0 += n


@contextlib.contextmanager
def phase(p):
    p.barrier()
    old = p.stack
    with contextlib.ExitStack() as st:
        p.stack = st
        yield
        p.barrier()
    p.stack = old


def segs(t0, n, tl):
    out = []
    if t0 < tl:
        out.append((0, min(n, tl - t0), 0))
    if t0 + n > tl:
        out.append((max(0, tl - t0), n, 1))
    return out


def emit_resid_linear(p, inT, Wb, Ci, D, Tc, TL, resid, bias, gates, V, meanB, rstdB, ones, eps, tag, alpha):
    nd = D // 128
    with phase(p):
        psums = Rot([p.psum(tag + "ps", [128, 512]) for _ in range(4)])
        st_s = Rot([p.psum(tag + "ss", [128, 512]) for _ in range(2)])
        st_q = Rot([p.psum(tag + "sq", [128, 512]) for _ in range(2)])
        xr = Rot([p.sbuf(tag + "xr", [128, 512], F32) for _ in range(3)])
        vb = Rot([p.sbuf(tag + "vb", [128, 512], F32) for _ in range(3)])
        vq = Rot([p.sbuf(tag + "vq", [128, 512], F32) for _ in range(3)])
        tmp = Rot([p.sbuf(tag + "tm", [128, 512], F32) for _ in range(2)])
        state = {}

        def epi(p, ps, cc, t0, n):
            x_t = xr.next()
            p.dma("sync", x_t[:, 0:n], resid.ap()[cc * 128:(cc + 1) * 128, t0:t0 + n], reads=[resid], writes=[x_t])
            v_t = vb.next()
            q_t = vq.next()
            for (a, b, ic) in segs(t0, n, TL):
                g = gates[ic]
                if bias is not None:
                    p.op("vector", lambda e, a=a, b=b, g=g: e.tensor_scalar(v_t[:, a:b], ps[:, a:b], bias[:, cc:cc + 1], g[:, cc:cc + 1], ALU.add, ALU.mult),
                         reads=[ps, bias, g], writes=[v_t])
                else:
                    p.op("vector", lambda e, a=a, b=b, g=g: e.tensor_scalar(v_t[:, a:b], ps[:, a:b], g[:, cc:cc + 1], None, ALU.mult),
                         reads=[ps, g], writes=[v_t])
            p.op("vector", lambda e: e.scalar_tensor_tensor(v_t[:, 0:n], x_t[:, 0:n], float(alpha), v_t[:, 0:n], ALU.mult, ALU.add),
                 reads=[x_t, v_t], writes=[v_t])
            p.op("scalar", lambda e: e.activation(out=q_t[:, 0:n], in_=v_t[:, 0:n], func=AF.Square), reads=[v_t], writes=[q_t])
            p.dma("gpsimd", V.ap()[cc * 128:(cc + 1) * 128, t0:t0 + n], v_t[:, 0:n], reads=[v_t], writes=[V])
            if cc == 0:
                state["s"] = st_s.next()
                state["q"] = st_q.next()
            s_ps, q_ps = state["s"], state["q"]
            p.op("tensor", lambda e: e.matmul(s_ps[:, 0:n], ones[:, :], v_t[:, 0:n], start=(cc == 0), stop=(cc == nd - 1)),
                 reads=[ones, v_t], writes=[s_ps])
            p.op("tensor", lambda e: e.matmul(q_ps[:, 0:n], ones[:, :], q_t[:, 0:n], start=(cc == 0), stop=(cc == nd - 1)),
                 reads=[ones, q_t], writes=[q_ps])
            if cc == nd - 1:
                m = meanB
                r = rstdB
                t_ = tmp.next()
                p.op("scalar", lambda e: e.activation(out=m[:, t0:t0 + n], in_=s_ps[:, 0:n], func=AF.Identity, scale=1.0 / D, bias=0.0),
                     reads=[s_ps], writes=[m])
                p.op("vector", lambda e: e.tensor_tensor(t_[:, 0:n], m[:, t0:t0 + n], m[:, t0:t0 + n], ALU.mult), reads=[m], writes=[t_])
                p.op("vector", lambda e: e.scalar_tensor_tensor(t_[:, 0:n], q_ps[:, 0:n], 1.0 / D, t_[:, 0:n], ALU.mult, ALU.subtract),
                     reads=[q_ps, t_], writes=[t_])
                p.op("vector", lambda e: e.tensor_scalar(t_[:, 0:n], t_[:, 0:n], float(eps), None, ALU.add), reads=[t_], writes=[t_])
                p.op("scalar", lambda e: e.activation(out=t_[:, 0:n], in_=t_[:, 0:n], func=AF.Sqrt), reads=[t_], writes=[t_])
                p.op("vector", lambda e: e.reciprocal(r[:, t0:t0 + n], t_[:, 0:n]), reads=[t_], writes=[r])
        emit_linear(p, inT, Wb, Ci, D, Tc, epi, psums, TT=512, cb=256, tag=tag)


def emit_ln_apply(p, V, D, Tc, TL, meanB, rstdB, g, b, outX, mods, outH, mask, tag):
    nd = D // 128
    with phase(p):
        vt = Rot([p.sbuf(tag + "v", [128, 512], F32) for _ in range(3)])
        xt = Rot([p.sbuf(tag + "x", [128, 512], F32) for _ in range(3)])
        ht = Rot([p.sbuf(tag + "h", [128, 512], BF16) for _ in range(3)])
        t0 = 0
        while t0 < Tc:
            n = min(512, Tc - t0)
            for cc in range(nd):
                v_t = vt.next()
                x_t = xt.next()
                p.dma("sync", v_t[:, 0:n], V.ap()[cc * 128:(cc + 1) * 128, t0:t0 + n], reads=[V], writes=[v_t])
                p.op("vector", lambda e, v_t=v_t, t0=t0, n=n: e.tensor_tensor(v_t[:, 0:n], v_t[:, 0:n], meanB[:, t0:t0 + n], ALU.subtract),
                     reads=[v_t, meanB], writes=[v_t])
                p.op("vector", lambda e, v_t=v_t, t0=t0, n=n: e.tensor_tensor(v_t[:, 0:n], v_t[:, 0:n], rstdB[:, t0:t0 + n], ALU.mult),
                     reads=[v_t, rstdB], writes=[v_t])
                p.op("scalar", lambda e, v_t=v_t, x_t=x_t, n=n, cc=cc: e.activation(out=x_t[:, 0:n], in_=v_t[:, 0:n], func=AF.Identity,
                                                                                   scale=g[:, cc:cc + 1], bias=b[:, cc:cc + 1]),
                     reads=[v_t, g, b], writes=[x_t])
                p.dma("gpsimd", outX.ap()[cc * 128:(cc + 1) * 128, t0:t0 + n], x_t[:, 0:n], reads=[x_t], writes=[outX])
                if mods is not None:
                    h_t = ht.next()
                    for (a, bb, ic) in segs(t0, n, TL):
                        sc1p, sh = mods[ic]
                        p.op("scalar", lambda e, a=a, bb=bb, sc1p=sc1p, sh=sh, x_t=x_t, v_t=v_t, cc=cc:
                             e.activation(out=v_t[:, a:bb], in_=x_t[:, a:bb], func=AF.Identity, scale=sc1p[:, cc:cc + 1], bias=sh[:, cc:cc + 1]),
                             reads=[x_t, sc1p, sh], writes=[v_t])
                    p.op("vector", lambda e, v_t=v_t, h_t=h_t, t0=t0, n=n: e.tensor_tensor(h_t[:, 0:n], v_t[:, 0:n], mask[:, t0:t0 + n], ALU.mult),
                         reads=[v_t, mask], writes=[h_t])
                    p.dma("gpsimd", outH.ap()[cc * 128:(cc + 1) * 128, t0:t0 + n], h_t[:, 0:n], reads=[h_t], writes=[outH])
            t0 += n


def emit_convglu(p, G, A, F, TO, CO, GW, wdw, bdw, mcl, mcr, has_ctx, tag="cg"):
    nf = F // 128
    HL = GW + 1
    TL = TO + 2 * HL
    Tc = TL + (CO + 2 if has_ctx else 0)
    with phase(p):
        z = p.sbuf(tag + "z", [128, nf, HL], BF16)
        p.op("vector", lambda e: e.memset(z[:, :, :], 0.0), writes=[z])
        Av = A.ap().rearrange("(c q) t -> q c t", q=128)
        p.dma("gpsimd", Av[:, :, 0:HL], z[:, :, :], reads=[z], writes=[A])
        p.dma("gpsimd", Av[:, :, HL + TO:TL], z[:, :, :], reads=[z], writes=[A])
        gts = Rot([p.sbuf(tag + "g", [128, Tc], F32) for _ in range(2)])
        vts = Rot([p.sbuf(tag + "v", [128, Tc], F32) for _ in range(2)])
        gls = Rot([p.sbuf(tag + "l", [128, TL], F32) for _ in range(2)])
        grs = Rot([p.sbuf(tag + "r", [128, TL], F32) for _ in range(2)])
        accs = Rot([p.sbuf(tag + "a", [128, TO + CO], F32) for _ in range(2)])
        acc2s = Rot([p.sbuf(tag + "b", [128, TO + CO], F32) for _ in range(2)])
        outs = Rot([p.sbuf(tag + "o", [128, TO + CO + 2], BF16) for _ in range(2)])
        for ob_ in outs.bufs:
            p.op("vector", lambda e, ob_=ob_: e.memset(ob_[:, :], 0.0), writes=[ob_])
        for f in range(nf):
            gt, vt, gl, gr, acc, acc2, ob = gts.next(), vts.next(), gls.next(), grs.next(), accs.next(), acc2s.next(), outs.next()
            p.dma("sync", gt[:, :], G.ap()[f * 128:(f + 1) * 128, :], reads=[G], writes=[gt])
            p.dma("sync", vt[:, :], G.ap()[F + f * 128:F + (f + 1) * 128, :], reads=[G], writes=[vt])
            p.op("gpsimd", lambda e, gl=gl, gt=gt: e.tensor_tensor(gl[:, :], gt[:, 0:TL], mcl[:, :], ALU.mult), reads=[gt, mcl], writes=[gl])
            p.op("gpsimd", lambda e, gr=gr, gt=gt: e.tensor_tensor(gr[:, :], gt[:, 0:TL], mcr[:, :], ALU.mult), reads=[gt, mcr], writes=[gr])
            first = True
            for dr in (0, -1, 1):
                for dc in (0, -1, 1):
                    src = gt if dc == 0 else (gl if dc == -1 else gr)
                    o = HL + GW * dr + dc
                    tap = (dr + 1) * 3 + (dc + 1)
                    if first:
                        p.op("vector", lambda e, src=src, o=o, tap=tap, acc=acc, f=f:
                             e.tensor_scalar(acc[:, 0:TO], src[:, o:o + TO], wdw[:, f, tap:tap + 1], None, ALU.mult),
                             reads=[src, wdw], writes=[acc])
                        first = False
                    else:
                        p.op("vector", lambda e, src=src, o=o, tap=tap, acc=acc, f=f:
                             e.scalar_tensor_tensor(acc[:, 0:TO], src[:, o:o + TO], wdw[:, f, tap:tap + 1], acc[:, 0:TO], ALU.mult, ALU.add),
                             reads=[src, wdw, acc], writes=[acc])
            if has_ctx:
                for dc in (0, -1, 1):
                    o = TL + 1 + dc
                    tap = 3 + (dc + 1)
                    if dc == 0:
                        p.op("vector", lambda e, o=o, tap=tap, acc=acc, gt=gt, f=f:
                             e.tensor_scalar(acc[:, TO:TO + CO], gt[:, o:o + CO], wdw[:, f, tap:tap + 1], None, ALU.mult),
                             reads=[gt, wdw, acc], writes=[acc])
                    else:
                        p.op("vector", lambda e, o=o, tap=tap, acc=acc, gt=gt, f=f:
                             e.scalar_tensor_tensor(acc[:, TO:TO + CO], gt[:, o:o + CO], wdw[:, f, tap:tap + 1], acc[:, TO:TO + CO], ALU.mult, ALU.add),
                             reads=[gt, wdw, acc], writes=[acc])
            nn = TO + (CO if has_ctx else 0)
            p.op("scalar", lambda e, acc=acc, acc2=acc2, f=f, nn=nn: e.activation(out=acc2[:, 0:nn], in_=acc[:, 0:nn], func=AF.Gelu,
                                                                                 bias=bdw[:, f:f + 1], scale=1.0),
                 reads=[acc, bdw], writes=[acc2])
            p.op("vector", lambda e, acc2=acc2, vt=vt, ob=ob: e.tensor_tensor(ob[:, 0:TO], acc2[:, 0:TO], vt[:, HL:HL + TO], ALU.mult),
                 reads=[acc2, vt], writes=[ob])
            if has_ctx:
                p.op("vector", lambda e, acc2=acc2, vt=vt, ob=ob: e.tensor_tensor(ob[:, TO + 1:TO + 1 + CO], acc2[:, TO:TO + CO], vt[:, TL + 1:TL + 1 + CO], ALU.mult),
                     reads=[acc2, vt], writes=[ob])
            p.dma("gpsimd", A.ap()[f * 128:(f + 1) * 128, HL:HL + TO], ob[:, 0:TO], reads=[ob], writes=[A])
            if has_ctx:
                p.dma("gpsimd", A.ap()[f * 128:(f + 1) * 128, TL:TL + CO + 2], ob[:, TO:TO + CO + 2], reads=[ob], writes=[A])


def load_tile(p, name, src, shape, dt=F32):
    t = p.sbuf(name, shape, dt)
    if len(shape) == 2:
        p.dma("sync", t[:, :], src.ap()[:, :], reads=[src], writes=[t])
    else:
        p.dma("sync", t[:, :, :], src.ap()[:, :, :], reads=[src], writes=[t])
    return t


def build_back(cfg, has_ctx, has_bias):
    D, F, TO, CO, GW = cfg["D"], cfg["F"], cfg["TO"], cfg["CO"], cfg["GW"]
    alpha, eps = cfg["alpha"], cfg["ln_eps"]
    nd, nf = D // 128, F // 128
    HL = GW + 1
    TL = TO + 2 * HL
    Tc = TL + (CO + 2 if has_ctx else 0)
    nc = bass.Bass("TRN2", target_bir_lowering=False)
    with contextlib.ExitStack() as st:
        p = Prog(nc, st)
        ext = lambda n, s, d: p.dram(n, s, d, kind="ExternalInput")
        zT = ext("zT", [D, Tc], BF16)
        xT = ext("xT", [D, Tc], F32)
        cmask = ext("cmask", [128, Tc], F32)
        mcl_d = ext("mcl", [128, TL], F32)
        mcr_d = ext("mcr", [128, TL], F32)
        modl_d = ext("modl", [128, 4, nd], F32)
        modc_d = ext("modc", [128, 4, nd], F32)
        lnp_d = ext("lnp", [128, 4, nd], F32)
        bo_d = ext("bo", [128, nd], F32)
        wmix = ext("wmix", [D, D], F32)
        wup = ext("wup", [D, 2 * F], F32)
        wdw_d = ext("wdw", [128, nf, 9], F32)
        bdw_d = ext("bdw", [128, nf], F32)
        wdown = ext("wdown", [F, D], F32)
        out = p.dram("out", [D, Tc], F32, kind="ExternalOutput")
        wmix_b = p.dram("wmix_b", [D, D], BF16)
        wup_b = p.dram("wup_b", [D, 2 * F], BF16)
        wdown_b = p.dram("wdown_b", [F, D], BF16)
        V = p.dram("V", [D, Tc], F32)
        X1 = p.dram("X1", [D, Tc], F32)
        H2 = p.dram("H2", [D, Tc], BF16)
        G = p.dram("G", [2 * F, Tc], F32)
        A = p.dram("A", [F, Tc], BF16)
        emit_cast(p, wmix_b, wmix, D, D)
        emit_cast(p, wup_b, wup, D, 2 * F)
        emit_cast(p, wdown_b, wdown, F, D)
        modl = load_tile(p, "modl", modl_d, [128, 4, nd])
        modc = load_tile(p, "modc", modc_d, [128, 4, nd])
        lnp = load_tile(p, "lnp", lnp_d, [128, 4, nd])
        bo = load_tile(p, "bo", bo_d, [128, nd])
        mask = load_tile(p, "mask", cmask, [128, Tc])
        ones = p.sbuf("ones", [128, 128], F32)
        p.op("vector", lambda e: e.memset(ones[:, :], 1.0), writes=[ones])
        sc1p_l = p.sbuf("sc1pl", [128, nd], F32)
        sc1p_c = p.sbuf("sc1pc", [128, nd], F32)
        p.op("vector", lambda e: e.tensor_scalar(sc1p_l[:, :], modl[:, 1, :], 1.0, None, ALU.add), reads=[modl], writes=[sc1p_l])
        p.op("vector", lambda e: e.tensor_scalar(sc1p_c[:, :], modc[:, 1, :], 1.0, None, ALU.add), reads=[modc], writes=[sc1p_c])
        meanB = p.sbuf("meanB", [128, Tc], F32)
        rstdB = p.sbuf("rstdB", [128, Tc], F32)

        def plane_tile(name, src, i):
            t = p.sbuf(name, [128, nd], F32)
            p.op("vector", lambda e: e.tensor_copy(t[:, :], src[:, i, :]), reads=[src], writes=[t])
            return t
        g1l, sh2l, g2l = plane_tile("g1l", modl, 0), plane_tile("sh2l", modl, 2), plane_tile("g2l", modl, 3)
        g1c, sh2c, g2c = plane_tile("g1c", modc, 0), plane_tile("sh2c", modc, 2), plane_tile("g2c", modc, 3)
        ln1g, ln1b, ln2g, ln2b = (plane_tile("ln%d" % i, lnp, i) for i in range(4))

        emit_resid_linear(p, zT, wmix_b, D, D, Tc, TL, xT, bo if has_bias else None, (g1l, g1c), V, meanB, rstdB, ones, eps, "r1", alpha)
        emit_ln_apply(p, V, D, Tc, TL, meanB, rstdB, ln1g, ln1b, X1, ((sc1p_l, sh2l), (sc1p_c, sh2c)), H2, mask, "l1")
        with phase(p):
            psums = Rot([p.psum("ups", [128, 512]) for _ in range(6)])
            gouts = Rot([p.sbuf("gout", [128, 512], F32) for _ in range(4)])
            cnt = [0]

            def epi_up(p, ps, cc, t0, n):
                ob = gouts.next()
                cnt[0] += 1
                if cnt[0] % 2 == 0:
                    p.op("scalar", lambda e: e.copy(ob[:, 0:n], ps[:, 0:n]), reads=[ps], writes=[ob])
                else:
                    p.op("vector", lambda e: e.tensor_copy(ob[:, 0:n], ps[:, 0:n]), reads=[ps], writes=[ob])
                p.dma("gpsimd", G.ap()[cc * 128:(cc + 1) * 128, t0:t0 + n], ob[:, 0:n], reads=[ob], writes=[G])
            emit_linear(p, H2, wup_b, D, 2 * F, Tc, epi_up, psums, TT=512, cb=256, tag="up")
        with phase(p):
            wdw = load_tile(p, "wdw", wdw_d, [128, nf, 9])
            bdw = load_tile(p, "bdw", bdw_d, [128, nf])
            mcl = load_tile(p, "mcl", mcl_d, [128, TL])
            mcr = load_tile(p, "mcr", mcr_d, [128, TL])
            emit_convglu(p, G, A, F, TO, CO, GW, wdw, bdw, mcl, mcr, has_ctx)
        emit_resid_linear(p, A, wdown_b, F, D, Tc, TL, X1, None, (g2l, g2c), V, meanB, rstdB, ones, eps, "r2", alpha)
        emit_ln_apply(p, V, D, Tc, TL, meanB, rstdB, ln2g, ln2b, out, None, None, None, "l2")
        p.emit()
    return nc


def build_ada(D, ncols, nlayers=2):
    nd = D // 128
    ncc = ncols // 128
    nc = bass.Bass("TRN2", target_bir_lowering=False)
    with contextlib.ExitStack() as st:
        p = Prog(nc, st)
        cond_d = p.dram("cond", [128, nd, 3], F32, kind="ExternalInput")
        ws = [p.dram("w%d" % l, [D, ncols], F32, kind="ExternalInput") for l in range(nlayers)]
        bs_d = p.dram("b", [128, nlayers, ncc], F32, kind="ExternalInput")
        out = p.dram("out", [128, nlayers, ncc, 3], F32, kind="ExternalOutput")
        cond = load_tile(p, "cond", cond_d, [128, nd, 3])
        bs = load_tile(p, "bs", bs_d, [128, nlayers, ncc])
        sT = p.sbuf("sT", [128, nd, 3], F32)
        p.op("scalar", lambda e: e.activation(out=sT[:, :, :], in_=cond[:, :, :], func=AF.Silu), reads=[cond], writes=[sT])
        ot = p.sbuf("ot", [128, nlayers, ncc, 3], F32)
        wts = Rot([p.sbuf("wt", [128, nd, 128], F32) for _ in range(3)])
        pss = Rot([p.psum("ps", [128, 512]) for _ in range(4)])
        for l in range(nlayers):
            wv = ws[l].ap().rearrange("(k p) c -> p k c", p=128)
            for cc in range(ncc):
                wt = wts.next()
                ps = pss.next()
                p.dma("sync", wt[:, :, :], wv[:, :, cc * 128:(cc + 1) * 128], reads=[ws[l]], writes=[wt])
                for k in range(nd):
                    p.op("tensor", lambda e, wt=wt, ps=ps, k=k: e.matmul(ps[:, 0:3], wt[:, k, :], sT[:, k, :], start=(k == 0), stop=(k == nd - 1)),
                         reads=[wt, sT], writes=[ps])
                p.op("vector", lambda e, ps=ps, l=l, cc=cc: e.tensor_scalar(ot[:, l, cc, :], ps[:, 0:3], bs[:, l, cc:cc + 1], None, ALU.add),
                     reads=[ps, bs], writes=[ot])
        p.dma("sync", out.ap()[:, :, :, :], ot[:, :, :, :], reads=[ot], writes=[out])
        p.emit()
    return nc


def dram_ap(buf, offset, pattern):
    return bass.AP(buf.t, offset, [list(x) for x in pattern])


def emit_sin_layer(p, ps, n, fp, li, out, tmp1, tmp2, pi2, K=64):
    hf = fp[0:K, li:li + 1]
    hfb = fp[0:K, 3 + li:4 + li]
    p.op("scalar", lambda e: e.activation(out=tmp1[0:K, 0:n], in_=ps[0:K, 0:n], func=AF.Sin, scale=hf, bias=hfb), reads=[ps, fp], writes=[tmp1])
    p.op("scalar", lambda e: e.activation(out=tmp2[0:K, 0:n], in_=ps[0:K, 0:n], func=AF.Abs, scale=hf, bias=hfb), reads=[ps, fp], writes=[tmp2])
    p.op("scalar", lambda e: e.activation(out=tmp2[0:K, 0:n], in_=tmp2[0:K, 0:n], func=AF.Sin, scale=-1.0, bias=pi2[0:K, 0:1]), reads=[tmp2, pi2], writes=[tmp2])
    p.op("vector", lambda e: e.scalar_tensor_tensor(out[0:K, 0:n], tmp1[0:K, 0:n], 2.0, tmp2[0:K, 0:n], ALU.mult, ALU.mult),
         reads=[tmp1, tmp2], writes=[out])


DBG = {}


def emit_hy_filters(p, Lx, zext, text, w1, w2, w3, fp, wout, negdelta, skip, Kf, CS, tag):
    W = 2 * Lx - 1
    ctr = Lx - 1
    ncc = CS // 128
    with phase(p):
        pi2 = p.sbuf(tag + "pi2", [128, 1], F32)
        p.op("vector", lambda e: e.memset(pi2[:, :], float(np.pi / 2)), writes=[pi2])
        zts = Rot([p.sbuf(tag + "z", [33, 512], F32) for _ in range(2)])
        tts = Rot([p.sbuf(tag + "t", [128, 512], F32) for _ in range(2)])
        hs = Rot([p.sbuf(tag + "h", [64, 512], F32) for _ in range(6)])
        t1s = Rot([p.sbuf(tag + "a", [64, 512], F32) for _ in range(2)])
        t2s = Rot([p.sbuf(tag + "b", [64, 512], F32) for _ in range(2)])
        pss = Rot([p.psum(tag + "ps", [128, 512]) for _ in range(4)])
        decs = Rot([p.sbuf(tag + "d", [128, 512], F32) for _ in range(2)])
        f32s = Rot([p.sbuf(tag + "f", [128, 512], F32) for _ in range(3)])
        fbs = Rot([p.sbuf(tag + "fb", [128, 512], BF16) for _ in range(3)])
        one1 = p.sbuf(tag + "one1", [1, 1], F32)
        p.op("vector", lambda e: e.memset(one1[:, :], 1.0), writes=[one1])
        c0 = 0
        while c0 < W:
            n = min(512, W - c0)
            zt, tt = zts.next(), tts.next()
            p.dma("sync", zt[:, 0:n], zext.ap()[:, c0:c0 + n], reads=[zext], writes=[zt])
            p.dma("sync", tt[:, 0:n], text.ap()[:, c0:c0 + n], reads=[text], writes=[tt])
            cur = zt
            Kdim = 33
            for li, wl in enumerate((w1, w2, w3)):
                ps = pss.next()
                p.op("tensor", lambda e, ps=ps, wl=wl, cur=cur, Kdim=Kdim, n=n: e.matmul(ps[0:64, 0:n], wl[0:Kdim, :], cur[0:Kdim, 0:n], start=True, stop=True),
                     reads=[wl, cur], writes=[ps])
                h = hs.next()
                emit_sin_layer(p, ps, n, fp, li, h, t1s.next(), t2s.next(), pi2)
                cur = h
                Kdim = 64
            h3 = cur
            has_c = (c0 <= ctr < c0 + n)
            for o in range(2):
                gl, gr = (0, 1) if o == 0 else (3, 2)
                for cc in range(ncc):
                    ps = pss.next()
                    a_end = min(n, max(0, ctr + 1 - c0))
                    if a_end > 0:
                        p.op("tensor", lambda e, ps=ps, gl=gl, cc=cc, a_end=a_end, h3=h3: e.matmul(ps[:, 0:a_end], wout[0:64, gl, cc * 128:(cc + 1) * 128], h3[0:64, 0:a_end], start=True, stop=True),
                             reads=[wout, h3], writes=[ps])
                    if a_end < n:
                        p.op("tensor", lambda e, ps=ps, gr=gr, cc=cc, a_end=a_end, h3=h3, n=n: e.matmul(ps[:, a_end:n], wout[0:64, gr, cc * 128:(cc + 1) * 128], h3[0:64, a_end:n], start=(a_end == 0), stop=True, skip_group_check=True),
                             reads=[wout, h3], writes=[ps])
                    if has_c:
                        ci = ctr - c0
                        p.op("tensor", lambda e, ps=ps, gr=gr, cc=cc, ci=ci, h3=h3: e.matmul(ps[:, ci:ci + 1], wout[0:64, gr, cc * 128:(cc + 1) * 128], h3[0:64, ci:ci + 1], start=False, stop=False, skip_group_check=True),
                             reads=[wout, h3], writes=[ps])
                        p.op("tensor", lambda e, ps=ps, o=o, cc=cc, ci=ci: e.matmul(ps[:, ci:ci + 1], skip[0:1, o, cc * 128:(cc + 1) * 128], one1[0:1, 0:1], start=False, stop=True, skip_group_check=True),
                             reads=[skip, one1], writes=[ps])
                    dec = decs.next()
                    p.op("scalar", lambda e, dec=dec, tt=tt, cc=cc, n=n: e.activation(out=dec[:, 0:n], in_=tt[:, 0:n], func=AF.Exp, scale=negdelta[:, cc:cc + 1]),
                         reads=[tt, negdelta], writes=[dec])
                    f32 = f32s.next()
                    p.op("vector", lambda e, f32=f32, ps=ps, dec=dec, n=n: e.tensor_tensor(f32[:, 0:n], ps[:, 0:n], dec[:, 0:n], ALU.mult), reads=[ps, dec], writes=[f32])
                    if DBG and tag == "fl" and c0 == 0 and o == 0 and cc == 0:
                        p.dma("sync", DBG["h3"].ap()[:, :], h3[:, :], reads=[h3], writes=[DBG["h3"]])
                        p.dma("sync", DBG["dec"].ap()[:, :], dec[:, :], reads=[dec], writes=[DBG["dec"]])
                        p.dma("sync", DBG["f32"].ap()[:, :], tt[:, :], reads=[tt], writes=[DBG["f32"]])
                    fb = fbs.next()
                    p.op("scalar", lambda e, fb=fb, f32=f32, n=n: e.copy(fb[:, 0:n], f32[:, 0:n]), reads=[f32], writes=[fb])
                    p.dma("gpsimd", Kf.ap()[o, cc * 128:(cc + 1) * 128, c0:c0 + n], fb[:, 0:n], reads=[fb], writes=[Kf])
            c0 += n


def emit_hy_conv(p, cfgh, U, wsh_d, Kf, Kfc, ident, antiid, zout, tag="hc"):
    B, L, Lc, CS = cfgh["B"], cfgh["L"], cfgh["Lc"], cfgh["CS"]
    WK, WKc = cfgh["WK"], cfgh["WKc"]
    nb, nbc = L // 128, Lc // 128
    NL = B * nb
    NCOL = NL + B * nbc
    Ttot = B * L + B * Lc
    WS = 2 * L - 1 - 127
    WSc = 2 * Lc - 1 - 127
    PIECE = min(2048, L)
    seqs = [(b * L, L) for b in range(B)] + [(B * L + b * Lc, Lc) for b in range(B)]
    with phase(p):
        VT = p.sbuf(tag + "VT", [128, 64, NCOL], BF16)
        XR = p.sbuf(tag + "XR", [128, 64, NCOL], BF16)
        X1T = p.sbuf(tag + "X1T", [128, 64, NCOL], BF16)
        X2T = p.sbuf(tag + "X2T", [128, 64, NCOL], BF16)
        Z1T = p.sbuf(tag + "Z1T", [128, 64, NCOL], BF16)
        Z2T = XR
        sks = Rot([p.sbuf(tag + "sk", [128, WS], BF16) for _ in range(2)])
        skcs = Rot([p.sbuf(tag + "skc", [128, WSc], BF16) for _ in range(2)])
        uts = Rot([p.sbuf(tag + "ut", [64, PIECE + 2], F32) for _ in range(2)])
        accs = Rot([p.sbuf(tag + "ac", [64, PIECE], F32) for _ in range(2)])
        cms = Rot([p.sbuf(tag + "cm", [64, PIECE], BF16) for _ in range(2)])
        wsh = Rot([p.sbuf(tag + "w", [64, 3, 4], F32) for _ in range(2)])
        pst = Rot([p.psum(tag + "pt", [128, 16, 64], BF16) for _ in range(2)])
        psf = Rot([p.psum(tag + "pf", [128, 512]) for _ in range(2)])
        psy = Rot([p.psum(tag + "py", [128, 512]) for _ in range(2)])
        pso = Rot([p.psum(tag + "po", [128, 512]) for _ in range(2)])
        obs = Rot([p.sbuf(tag + "ob", [64, 512], BF16) for _ in range(3)])
        for hg in range(CS // 64):
            wt = wsh.next()
            p.dma("sync", wt[:, :, :], wsh_d.ap()[hg, :, :, :], reads=[wsh_d], writes=[wt])
            for part, dst in ((2, VT), (0, XR), (1, X2T)):
                row0 = part * CS + hg * 64
                for (s0, Ls) in seqs:
                    for q0 in range(0, Ls, PIECE):
                        n = min(PIECE, Ls - q0)
                        ut, acc, cm = uts.next(), accs.next(), cms.next()
                        lo = q0 - 1
                        hi = q0 + n + 1
                        a, e_ = max(lo, 0), min(hi, Ls)
                        if lo < 0:
                            p.op("gpsimd", lambda e, ut=ut: e.memset(ut[:, 0:1], 0.0), writes=[ut])
                        if hi > Ls:
                            p.op("gpsimd", lambda e, ut=ut, n=n: e.memset(ut[:, n + 1:n + 2], 0.0), writes=[ut])
                        p.dma("sync", ut[:, a - lo:e_ - lo], U.ap()[row0:row0 + 64, s0 + a:s0 + e_], reads=[U], writes=[ut])
                        p.op("vector", lambda e, ut=ut, acc=acc, wt=wt, part=part, n=n: e.tensor_scalar(acc[:, 0:n], ut[:, 0:n], wt[:, part, 0:1], None, ALU.mult),
                             reads=[ut, wt], writes=[acc])
                        p.op("vector", lambda e, ut=ut, acc=acc, wt=wt, part=part, n=n: e.scalar_tensor_tensor(acc[:, 0:n], ut[:, 1:n + 1], wt[:, part, 1:2], acc[:, 0:n], ALU.mult, ALU.add),
                             reads=[ut, wt, acc], writes=[acc])
                        p.op("vector", lambda e, ut=ut, acc=acc, wt=wt, part=part, n=n: e.scalar_tensor_tensor(acc[:, 0:n], ut[:, 2:n + 2], wt[:, part, 2:3], acc[:, 0:n], ALU.mult, ALU.add),
                             reads=[ut, wt, acc], writes=[acc])
                        p.op("scalar", lambda e, acc=acc, cm=cm, wt=wt, part=part, n=n: e.activation(out=cm[:, 0:n], in_=acc[:, 0:n], func=AF.Identity, bias=wt[:, part, 3:4], scale=1.0),
                             reads=[acc, wt], writes=[cm])
                        col0 = (s0 + q0) // 128
                        nblk = n // 128
                        for g0 in range(0, nblk, 8):
                            ng = min(8, nblk - g0)
                            pt = pst.next()
                            for k in range(ng):
                                p.op("tensor", lambda e, pt=pt, cm=cm, k=k, g0=g0: e.transpose(pt[:, k, :], cm[:, (g0 + k) * 128:(g0 + k + 1) * 128], ident[0:64, 0:64]),
                                     reads=[cm, ident], writes=[pt])
                            c_a = col0 + g0
                            p.op("vector", lambda e, pt=pt, dst=dst, c_a=c_a, ng=ng: e.tensor_copy(dst[:, :, c_a:c_a + ng].rearrange("p c k -> p k c"), pt[:, 0:ng, :]),
                                 reads=[pt], writes=[dst])
            xrf = XR[:, :, :].rearrange("p c k -> p (c k)")
            x1f = X1T[:, :, :].rearrange("p c k -> p (c k)")
            tot = 64 * NCOL
            for f0 in range(0, tot, 512):
                n = min(512, tot - f0)
                ps = psf.next()
                p.op("tensor", lambda e, ps=ps, f0=f0, n=n: e.matmul(ps[:, 0:n], antiid[:, :], xrf[:, f0:f0 + n], start=True, stop=True), reads=[antiid, XR], writes=[ps])
                p.op("scalar", lambda e, ps=ps, f0=f0, n=n: e.copy(x1f[:, f0:f0 + n], ps[:, 0:n]), reads=[ps], writes=[X1T])
            for o, (src, gate, dstz) in enumerate(((VT, X1T, Z1T), (Z1T, X2T, Z2T))):
                sgn = -1 if o == 0 else 1
                for c in range(64):
                    ch = hg * 64 + c
                    sk, skc = sks.next(), skcs.next()
                    p.dma("sync", sk[:, :], dram_ap(Kf, (o * CS + ch) * WK, [[1, 128], [1, WS]]), reads=[Kf], writes=[sk])
                    p.dma("sync", skc[:, :], dram_ap(Kfc, (o * CS + ch) * WKc, [[1, 128], [1, WSc]]), reads=[Kfc], writes=[skc])
                    ps = psy.next()
                    dlist = [0] + [x for d in range(1, nb) for x in (d, -d)]
                    sv = src[:, c, 0:NL].rearrange("p (b j) -> p b j", b=B)
                    pv = ps[:, 0:NL].rearrange("p (b j) -> p b j", b=B)
                    for i_, d in enumerate(dlist):
                        j0, j1 = max(0, -d), min(nb, nb - d)
                        m0 = L - 128 + sgn * 128 * d
                        p.op("tensor", lambda e, sk=sk, m0=m0, sv=sv, pv=pv, j0=j0, j1=j1, d=d, i_=i_:
                             e.matmul(pv[:, :, j0 + d:j1 + d], sk[:, m0:m0 + 128], sv[:, :, j0:j1], start=(i_ == 0), stop=(i_ == len(dlist) - 1)),
                             reads=[sk, src], writes=[ps])
                    dlc = [0] + [x for d in range(1, nbc) for x in (d, -d)]
                    svc = src[:, c, NL:NCOL].rearrange("p (b j) -> p b j", b=B)
                    pvc = ps[:, NL:NCOL].rearrange("p (b j) -> p b j", b=B)
                    for i_, d in enumerate(dlc):
                        j0, j1 = max(0, -d), min(nbc, nbc - d)
                        m0 = Lc - 128 + sgn * 128 * d
                        p.op("tensor", lambda e, skc=skc, m0=m0, svc=svc, pvc=pvc, j0=j0, j1=j1, d=d, i_=i_:
                             e.matmul(pvc[:, :, j0 + d:j1 + d], skc[:, m0:m0 + 128], svc[:, :, j0:j1], start=(i_ == 0), stop=(i_ == len(dlc) - 1)),
                             reads=[skc, src], writes=[ps])
                    p.op("vector", lambda e, ps=ps, gate=gate, dstz=dstz, c=c: e.tensor_tensor(dstz[:, c, :], ps[:, 0:NCOL], gate[:, c, :], ALU.mult),
                         reads=[ps, gate], writes=[dstz])
            for g0 in range(0, NCOL, 4):
                ng = min(4, NCOL - g0)
                po = pso.next()
                for k in range(ng):
                    p.op("tensor", lambda e, po=po, k=k, g0=g0: e.matmul(po[0:64, k * 128:(k + 1) * 128], Z2T[:, :, g0 + k], ident[:, :], start=True, stop=True),
                         reads=[Z2T, ident], writes=[po])
                ob = obs.next()
                p.op("scalar", lambda e, po=po, ob=ob, ng=ng: e.copy(ob[:, 0:ng * 128], po[0:64, 0:ng * 128]), reads=[po], writes=[ob])
                p.dma("gpsimd", zout.ap()[hg * 64:(hg + 1) * 64, g0 * 128:(g0 + ng) * 128], ob[:, 0:ng * 128], reads=[ob], writes=[zout])


def build_hyfront(cfg):
    D, B, L, Lc, NC = cfg["D"], cfg["B"], cfg["L"], cfg["Lc"], cfg["NC"]
    CS = D // NC
    nd = D // 128
    Ttot = B * L + B * Lc
    WK = 2 * L
    WKc = 2 * Lc
    nc = bass.Bass("TRN2", target_bir_lowering=False)
    with contextlib.ExitStack() as st:
        p = Prog(nc, st)
        ext = lambda n, s, d: p.dram(n, s, d, kind="ExternalInput")
        xall = ext("xall", [D, Ttot], F32)
        modp_d = ext("modp", [128, 3, 2, nd], F32)
        win = ext("win", [D, 3 * CS], F32)
        bin_d = ext("bin", [128, 3 * CS // 128], F32)
        wsh_d = ext("wsh", [CS // 64, 64, 3, 4], F32)
        zext = ext("zext", [33, 2 * L], F32)
        zextc = ext("zextc", [33, 2 * Lc], F32)
        text = ext("text", [128, 2 * L], F32)
        textc = ext("textc", [128, 2 * Lc], F32)
        w1_d = ext("w1", [33, 64], F32)
        w2_d = ext("w2", [64, 64], F32)
        w3_d = ext("w3", [64, 64], F32)
        fp_d = ext("fp", [64, 8], F32)
        wout_d = ext("wout", [64, 4, CS], F32)
        ndl_d = ext("negdelta", [128, CS // 128], F32)
        skip_d = ext("skip", [1, 2, CS], F32)
        id_d = ext("ident", [128, 128], BF16)
        aid_d = ext("antiid", [128, 128], BF16)
        zout = p.dram("zout", [CS, Ttot], BF16, kind="ExternalOutput")
        H = p.dram("H", [D, Ttot], BF16)
        winb = p.dram("winb", [D, 3 * CS], BF16)
        U = p.dram("U", [3 * CS, Ttot], F32)
        dbg = cfg.get("dbg", False)
        Kf = p.dram("Kf", [2, CS, WK], BF16, kind="ExternalOutput" if dbg else "Internal")
        Kfc = p.dram("Kfc", [2, CS, WKc], BF16, kind="ExternalOutput" if dbg else "Internal")
        if dbg:
            DBG["h3"] = p.dram("dbg_h3", [64, 512], F32, kind="ExternalOutput")
            DBG["dec"] = p.dram("dbg_dec", [128, 512], F32, kind="ExternalOutput")
            DBG["f32"] = p.dram("dbg_f32", [128, 512], F32, kind="ExternalOutput")
        emit_cast(p, winb, win, D, 3 * CS)
        modp = load_tile(p, "modp", modp_d, [128, 3, 2, nd]) if False else None
        modp = p.sbuf("modp", [128, 3, 2, nd], F32)
        p.dma("sync", modp[:, :, :, :], modp_d.ap()[:, :, :, :], reads=[modp_d], writes=[modp])
        binT = load_tile(p, "bin", bin_d, [128, 3 * CS // 128])
        ident = load_tile(p, "ident", id_d, [128, 128], BF16)
        antiid = load_tile(p, "antiid", aid_d, [128, 128], BF16)
        with phase(p):
            xts = Rot([p.sbuf("b1x", [128, 512], F32) for _ in range(4)])
            hts = Rot([p.sbuf("b1h", [128, 512], BF16) for _ in range(4)])
            for t0 in range(0, Ttot, 512):
                n = min(512, Ttot - t0)
                cond = min(t0 // L, B) if t0 < B * L else B
                for k in range(nd):
                    xt, ht = xts.next(), hts.next()
                    p.dma("sync", xt[:, 0:n], xall.ap()[k * 128:(k + 1) * 128, t0:t0 + n], reads=[xall], writes=[xt])
                    p.op("scalar", lambda e, xt=xt, ht=ht, n=n, cond=cond, k=k: e.activation(out=ht[:, 0:n], in_=xt[:, 0:n], func=AF.Identity,
                                                                                             scale=modp[:, cond, 0, k:k + 1], bias=modp[:, cond, 1, k:k + 1]),
                         reads=[xt, modp], writes=[ht])
                    p.dma("gpsimd", H.ap()[k * 128:(k + 1) * 128, t0:t0 + n], ht[:, 0:n], reads=[ht], writes=[H])
        with phase(p):
            psums = Rot([p.psum("b2ps", [128, 512]) for _ in range(6)])
            outs = Rot([p.sbuf("b2o", [128, 512], F32) for _ in range(4)])

            def epi(p, ps, cc, t0, n):
                ob = outs.next()
                p.op("scalar", lambda e: e.activation(out=ob[:, 0:n], in_=ps[:, 0:n], func=AF.Identity, bias=binT[:, cc:cc + 1], scale=1.0),
                     reads=[ps, binT], writes=[ob])
                p.dma("gpsimd", U.ap()[cc * 128:(cc + 1) * 128, t0:t0 + n], ob[:, 0:n], reads=[ob], writes=[U])
            emit_linear(p, H, winb, D, 3 * CS, Ttot, epi, psums, TT=512, cb=256, tag="b2")
        with phase(p):
            w1 = load_tile(p, "w1", w1_d, [33, 64])
            w2 = load_tile(p, "w2", w2_d, [64, 64])
            w3 = load_tile(p, "w3", w3_d, [64, 64])
            fpr = load_tile(p, "fpr", fp_d, [64, 8])
            wout = load_tile(p, "wout", wout_d, [64, 4, CS])
            negdelta = load_tile(p, "ndl", ndl_d, [128, CS // 128])
            skip = load_tile(p, "skip", skip_d, [1, 2, CS])
            fp = p.sbuf("fp", [64, 8], F32)
            p.op("vector", lambda e: e.tensor_scalar(fp[:, 0:3], fpr[:, 0:3], 0.5, None, ALU.mult), reads=[fpr], writes=[fp])
            p.op("vector", lambda e: e.tensor_tensor(fp[:, 3:6], fp[:, 0:3], fpr[:, 3:6], ALU.mult), reads=[fp, fpr], writes=[fp])
            emit_hy_filters(p, L, zext, text, w1, w2, w3, fp, wout, negdelta, skip, Kf, CS, "fl")
            emit_hy_filters(p, Lc, zextc, textc, w1, w2, w3, fp, wout, negdelta, skip, Kfc, CS, "fc")
        cfgh = dict(B=B, L=L, Lc=Lc, CS=CS, WK=WK, WKc=WKc)
        emit_hy_conv(p, cfgh, U, wsh_d, Kf, Kfc, ident, antiid, zout)
        p.emit()
    return nc


def emit_modulate(p, xall, modp, Hout, D, Ttot, L, B, tag="md"):
    nd = D // 128
    with phase(p):
        xts = Rot([p.sbuf(tag + "x", [128, 512], F32) for _ in range(4)])
        hts = Rot([p.sbuf(tag + "h", [128, 512], BF16) for _ in range(4)])
        for t0 in range(0, Ttot, 512):
            n = min(512, Ttot - t0)
            cond = min(t0 // L, B) if t0 < B * L else B
            for k in range(nd):
                xt, ht = xts.next(), hts.next()
                p.dma("sync", xt[:, 0:n], xall.ap()[k * 128:(k + 1) * 128, t0:t0 + n], reads=[xall], writes=[xt])
                p.op("scalar", lambda e, xt=xt, ht=ht, n=n, cond=cond, k=k: e.activation(out=ht[:, 0:n], in_=xt[:, 0:n], func=AF.Identity,
                                                                                         scale=modp[:, cond, 0, k:k + 1], bias=modp[:, cond, 1, k:k + 1]),
                     reads=[xt, modp], writes=[ht])
                p.dma("gpsimd", Hout.ap()[k * 128:(k + 1) * 128, t0:t0 + n], ht[:, 0:n], reads=[ht], writes=[Hout])


def build_gdnfront(cfg):
    D, B, L, Lc, NC = cfg["D"], cfg["B"], cfg["L"], cfg["Lc"], cfg["NC"]
    rms_eps, l2_eps = cfg["rms_eps"], cfg["l2_eps"]
    H = D // 128
    HPC = H // NC
    CS = HPC * 128
    nd = D // 128
    Ttot = B * L + B * Lc
    NBLK = Ttot // 128
    NLB = B * L // 128
    nbl, nbc = L // 128, Lc // 128
    CO = 4 * CS + 128
    NAB = 4 * HPC
    nc = bass.Bass("TRN2", target_bir_lowering=False)
    with contextlib.ExitStack() as st:
        p = Prog(nc, st)
        ext = lambda n, s, d: p.dram(n, s, d, kind="ExternalInput")
        xall = ext("xall", [D, Ttot], F32)
        modp_d = ext("modp", [128, 3, 2, nd], F32)
        win = ext("win", [D, CO], F32)
        wcv_d = ext("wcv", [128, 3 * HPC, 3], F32)
        abc_d = ext("abc", [128, 2, 2 * HPC], F32)
        normw_d = ext("normw", [128, 128], F32)
        id_d = ext("ident", [128, 128], F32)
        msk_d = ext("masks", [128, 6, 128], F32)
        oout = p.dram("oout", [B * L, CS], BF16, kind="ExternalOutput")
        Hh = p.dram("H", [D, Ttot], BF16)
        winb = p.dram("winb", [D, CO], BF16)
        U = p.dram("U", [CO, Ttot], F32)
        QT = p.dram("QT", [CS, Ttot], F32)
        KT = p.dram("KT", [CS, Ttot], F32)
        KTM = p.dram("KTM", [Ttot, CS], F32)
        VTM = p.dram("VTM", [Ttot, CS], F32)
        OF = p.dram("OF", [B * L, CS], F32)
        OB = p.dram("OB", [B * L, CS], F32)
        emit_cast(p, winb, win, D, CO)
        modp = p.sbuf("modp", [128, 3, 2, nd], F32)
        p.dma("sync", modp[:, :, :, :], modp_d.ap()[:, :, :, :], reads=[modp_d], writes=[modp])
        ident = load_tile(p, "ident", id_d, [128, 128])
        masks = load_tile(p, "masks", msk_d, [128, 6, 128])
        ones = p.sbuf("ones", [128, 128], F32)
        p.op("vector", lambda e: e.memset(ones[:, :], 1.0), writes=[ones])
        Gtm = p.sbuf("Gtm", [128, NBLK, 2 * HPC], F32)
        Btm = p.sbuf("Btm", [128, NBLK, 2 * HPC], F32)
        emit_modulate(p, xall, modp, Hh, D, Ttot, L, B)
        with phase(p):
            psums = Rot([p.psum("g2ps", [128, 512]) for _ in range(6)])
            outs = Rot([p.sbuf("g2o", [128, 512], F32) for _ in range(4)])
            cnt = [0]

            def epi(p, ps, cc, t0, n):
                ob = outs.next()
                cnt[0] += 1
                if cnt[0] % 2 == 0:
                    p.op("scalar", lambda e: e.copy(ob[:, 0:n], ps[:, 0:n]), reads=[ps], writes=[ob])
                else:
                    p.op("vector", lambda e: e.tensor_copy(ob[:, 0:n], ps[:, 0:n]), reads=[ps], writes=[ob])
                p.dma("gpsimd", U.ap()[cc * 128:(cc + 1) * 128, t0:t0 + n], ob[:, 0:n], reads=[ob], writes=[U])
            emit_linear(p, Hh, winb, D, CO, Ttot, epi, psums, TT=512, cb=256, tag="g2")
        seqs = [(b * L, L) for b in range(B)] + [(B * L + b * Lc, Lc) for b in range(B)]
        with phase(p):
            wcv = load_tile(p, "wcv", wcv_d, [128, 3 * HPC, 3])
            PC = 512
            uts = Rot([p.sbuf("g3u", [128, PC + 2], F32) for _ in range(3)])
            accs = Rot([p.sbuf("g3a", [128, PC], F32) for _ in range(3)])
            sls = Rot([p.sbuf("g3s", [128, PC], F32) for _ in range(3)])
            sqs = Rot([p.sbuf("g3q", [128, PC], F32) for _ in range(2)])
            rss = Rot([p.sbuf("g3r", [128, PC], F32) for _ in range(2)])
            nrm = Rot([p.sbuf("g3n", [128, PC], F32) for _ in range(3)])
            tms = Rot([p.sbuf("g3t", [128, PC], F32) for _ in range(3)])
            pss = Rot([p.psum("g3ps", [128, 512]) for _ in range(3)])
            pst = Rot([p.psum("g3pt", [128, 512]) for _ in range(3)])
            for part in range(3):
                for hl in range(HPC):
                    ch = part * HPC + hl
                    row0 = part * CS + hl * 128
                    for (s0, Ls) in seqs:
                        for q0 in range(0, Ls, PC):
                            n = min(PC, Ls - q0)
                            ut, acc, sl = uts.next(), accs.next(), sls.next()
                            lo, hi = q0 - 1, q0 + n + 1
                            a, e_ = max(lo, 0), min(hi, Ls)
                            if lo < 0:
                                p.op("gpsimd", lambda e, ut=ut: e.memset(ut[:, 0:1], 0.0), writes=[ut])
                            if hi > Ls:
                                p.op("gpsimd", lambda e, ut=ut, n=n: e.memset(ut[:, n + 1:n + 2], 0.0), writes=[ut])
                            p.dma("sync", ut[:, a - lo:e_ - lo], U.ap()[row0:row0 + 128, s0 + a:s0 + e_], reads=[U], writes=[ut])
                            p.op("vector", lambda e, ut=ut, acc=acc, ch=ch, n=n: e.tensor_scalar(acc[:, 0:n], ut[:, 0:n], wcv[:, ch, 0:1], None, ALU.mult),
                                 reads=[ut, wcv], writes=[acc])
                            p.op("vector", lambda e, ut=ut, acc=acc, ch=ch, n=n: e.scalar_tensor_tensor(acc[:, 0:n], ut[:, 1:n + 1], wcv[:, ch, 1:2], acc[:, 0:n], ALU.mult, ALU.add),
                                 reads=[ut, wcv, acc], writes=[acc])
                            p.op("vector", lambda e, ut=ut, acc=acc, ch=ch, n=n: e.scalar_tensor_tensor(acc[:, 0:n], ut[:, 2:n + 2], wcv[:, ch, 2:3], acc[:, 0:n], ALU.mult, ALU.add),
                                 reads=[ut, wcv, acc], writes=[acc])
                            p.op("scalar", lambda e, acc=acc, sl=sl, n=n: e.activation(out=sl[:, 0:n], in_=acc[:, 0:n], func=AF.Silu), reads=[acc], writes=[sl])
                            cur = sl
                            if part < 2:
                                sq, rs, nr = sqs.next(), rss.next(), nrm.next()
                                ps = pss.next()
                                p.op("scalar", lambda e, sl=sl, sq=sq, n=n: e.activation(out=sq[:, 0:n], in_=sl[:, 0:n], func=AF.Square), reads=[sl], writes=[sq])
                                p.op("tensor", lambda e, ps=ps, sq=sq, n=n: e.matmul(ps[:, 0:n], ones[:, :], sq[:, 0:n], start=True, stop=True), reads=[ones, sq], writes=[ps])
                                p.op("vector", lambda e, ps=ps, rs=rs, n=n: e.tensor_scalar(rs[:, 0:n], ps[:, 0:n], float(l2_eps), None, ALU.add), reads=[ps], writes=[rs])
                                p.op("scalar", lambda e, rs=rs, n=n: e.activation(out=rs[:, 0:n], in_=rs[:, 0:n], func=AF.Sqrt), reads=[rs], writes=[rs])
                                p.op("vector", lambda e, rs=rs, n=n: e.reciprocal(rs[:, 0:n], rs[:, 0:n]), reads=[rs], writes=[rs])
                                sc = 128 ** -0.5 if part == 0 else 1.0
                                p.op("vector", lambda e, sl=sl, rs=rs, nr=nr, n=n, sc=sc: e.scalar_tensor_tensor(nr[:, 0:n], sl[:, 0:n], float(sc), rs[:, 0:n], ALU.mult, ALU.mult),
                                     reads=[sl, rs], writes=[nr])
                                dst = QT if part == 0 else KT
                                p.dma("gpsimd", dst.ap()[hl * 128:(hl + 1) * 128, s0 + q0:s0 + q0 + n], nr[:, 0:n], reads=[nr], writes=[dst])
                                cur = nr
                            if part >= 1:
                                pt = pst.next()
                                tm = tms.next()
                                nb_ = n // 128
                                for k in range(nb_):
                                    p.op("tensor", lambda e, pt=pt, cur=cur, k=k: e.transpose(pt[:, k * 128:(k + 1) * 128], cur[:, k * 128:(k + 1) * 128], ident[:, :]),
                                         reads=[cur, ident], writes=[pt])
                                p.op("scalar", lambda e, pt=pt, tm=tm, n=n: e.copy(tm[:, 0:n], pt[:, 0:n]), reads=[pt], writes=[tm])
                                dstm = KTM if part == 1 else VTM
                                blk0 = (s0 + q0) // 128
                                p.dma("gpsimd", dstm.ap()[blk0 * 128:(blk0 + nb_) * 128, hl * 128:(hl + 1) * 128].rearrange("(k t) c -> t k c", t=128),
                                      tm[:, 0:n].rearrange("t (k c) -> t k c", c=128), reads=[tm], writes=[dstm])
        with phase(p):
            abc = load_tile(p, "abc", abc_d, [128, 2, 2 * HPC])
            nexpA = p.sbuf("nexpA", [128, 2 * HPC], F32)
            p.op("scalar", lambda e: e.activation(out=nexpA[:, :], in_=abc[:, 1, :], func=AF.Exp), reads=[abc], writes=[nexpA])
            p.op("vector", lambda e: e.tensor_scalar(nexpA[:, :], nexpA[:, :], -1.0, None, ALU.mult), reads=[nexpA], writes=[nexpA])
            abt = Rot([p.sbuf("g4a", [NAB, 2048], F32) for _ in range(2)])
            pts = Rot([p.psum("g4p", [128, 512]) for _ in range(2)])
            tmp = Rot([p.sbuf("g4t", [128, 2 * HPC], F32) for _ in range(15)])
            for t0 in range(0, Ttot, 2048):
                n = min(2048, Ttot - t0)
                at = abt.next()
                p.dma("sync", at[:, 0:n], U.ap()[4 * CS:4 * CS + NAB, t0:t0 + n], reads=[U], writes=[at])
                for k in range(n // 128):
                    blk = t0 // 128 + k
                    pt = pts.next()
                    tt = tmp.next()
                    p.op("tensor", lambda e, pt=pt, at=at, k=k: e.transpose(pt[:, 0:NAB], at[:, k * 128:(k + 1) * 128], ident[0:NAB, 0:NAB]), reads=[at, ident], writes=[pt])
                    p.op("vector", lambda e, pt=pt, tt=tt: e.tensor_tensor(tt[:, :], pt[:, 0:2 * HPC], abc[:, 0, :], ALU.add), reads=[pt, abc], writes=[tt])
                    sa, sb, sc_, sd = tmp.next(), tmp.next(), tmp.next(), tmp.next()
                    p.op("scalar", lambda e, tt=tt: e.activation(out=tt[:, :], in_=tt[:, :], func=AF.Exp), reads=[tt], writes=[tt])
                    p.op("scalar", lambda e, tt=tt, sd=sd: e.activation(out=sd[:, :], in_=tt[:, :], func=AF.Ln, bias=1.0, scale=1.0), reads=[tt], writes=[sd])
                    p.op("vector", lambda e, tt=tt, sa=sa: e.tensor_scalar(sa[:, :], tt[:, :], 2.0, None, ALU.add), reads=[tt], writes=[sa])
                    p.op("vector", lambda e, sa=sa: e.reciprocal(sa[:, :], sa[:, :]), reads=[sa], writes=[sa])
                    p.op("vector", lambda e, tt=tt, sa=sa: e.tensor_tensor(sa[:, :], tt[:, :], sa[:, :], ALU.mult), reads=[tt, sa], writes=[sa])
                    p.op("vector", lambda e, sa=sa, sb=sb: e.tensor_tensor(sb[:, :], sa[:, :], sa[:, :], ALU.mult), reads=[sa], writes=[sb])
                    p.op("vector", lambda e, sb=sb, sc_=sc_: e.tensor_scalar(sc_[:, :], sb[:, :], 1.0 / 15, None, ALU.mult), reads=[sb], writes=[sc_])
                    for cf in (1.0 / 13, 1.0 / 11, 1.0 / 9, 1.0 / 7, 1.0 / 5, 1.0 / 3):
                        p.op("vector", lambda e, sb=sb, sc_=sc_, cf=cf: e.scalar_tensor_tensor(sc_[:, :], sc_[:, :], float(cf), sb[:, :], ALU.add, ALU.mult), reads=[sb, sc_], writes=[sc_])
                    p.op("vector", lambda e, sa=sa, sc_=sc_: e.scalar_tensor_tensor(sc_[:, :], sc_[:, :], 1.0, sa[:, :], ALU.add, ALU.mult), reads=[sa, sc_], writes=[sc_])
                    p.op("vector", lambda e, sc_=sc_: e.tensor_scalar(sc_[:, :], sc_[:, :], 2.0, None, ALU.mult), reads=[sc_], writes=[sc_])
                    p.op("vector", lambda e, tt=tt, sb=sb: e.tensor_scalar(sb[:, :], tt[:, :], 1.0, None, ALU.is_le), reads=[tt], writes=[sb])
                    p.op("vector", lambda e, sc_=sc_, sd=sd: e.tensor_tensor(sc_[:, :], sc_[:, :], sd[:, :], ALU.subtract), reads=[sc_, sd], writes=[sc_])
                    p.op("vector", lambda e, sc_=sc_, sb=sb: e.tensor_tensor(sc_[:, :], sc_[:, :], sb[:, :], ALU.mult), reads=[sc_, sb], writes=[sc_])
                    p.op("vector", lambda e, sc_=sc_, sd=sd: e.tensor_tensor(sc_[:, :], sc_[:, :], sd[:, :], ALU.add), reads=[sc_, sd], writes=[sc_])
                    p.op("vector", lambda e, sc_=sc_, blk=blk: e.tensor_tensor(Gtm[:, blk, :], sc_[:, :], nexpA[:, :], ALU.mult), reads=[sc_, nexpA], writes=[Gtm])
                    p.op("scalar", lambda e, pt=pt, blk=blk: e.activation(out=Btm[:, blk, :], in_=pt[:, 2 * HPC:4 * HPC], func=AF.Sigmoid), reads=[pt], writes=[Btm])
        emit_gdn_scan(p, cfg, QT, KT, KTM, VTM, Gtm, Btm, OF, OB, ident, masks, ones)
        emit_gdn_out(p, cfg, U, OF, OB, normw_d, ident, oout)
        p.emit()
    return nc


def emit_gdn_scan(p, cfg, QT, KT, KTM, VTM, Gtm, Btm, OF, OB, ident, masks, ones):
    D, B, L, Lc, NC = cfg["D"], cfg["B"], cfg["L"], cfg["Lc"], cfg["NC"]
    H = D // 128
    HPC = H // NC
    nbl, nbc = L // 128, Lc // 128
    T = lambda: [128, 128]
    with phase(p):
        def pool(name, k, shape=(128, 128)):
            return Rot([p.sbuf(name, list(shape), F32) for _ in range(k)])
        qTs, kTs, ktms, vtms = pool("sq", 4), pool("sk", 4), pool("skt", 4), pool("sv", 4)
        gcolBs, gcBs, colss = pool("sgb", 3), pool("sgc", 4), pool("scol", 4, (128, 8))
        Es, EMs, attns, bcolBs, EMSs, Ns, NTs, Ps = pool("sE", 3), pool("sEM", 3), pool("sat", 4), pool("sbb", 3), pool("sEMS", 3), pool("sN", 4), pool("sNT", 3), pool("sP", 5)
        Ms, Mts = pool("sM", 6), pool("sMt", 8)
        vbs, kbgs, us, wTs, egBs, qgTs, kgs, vnews, os_ = pool("svb", 3), pool("skbg", 3), pool("su", 3), pool("swT", 3), pool("seg", 3), pool("sqg", 3), pool("skg", 3), pool("svn", 3), pool("so", 3)
        banks = Rot([p.psum("sps", [128, 512]) for _ in range(8)])
        chains = [(b, hl, d) for b in range(B) for hl in range(HPC) for d in range(2)]
        S = {}
        for c in chains:
            S[c] = p.sbuf("S", [128, 128], F32)
            p.op("gpsimd", lambda e, s=S[c]: e.memset(s[:, :], 0.0), writes=[S[c]])
        nsteps = nbc + nbl
        cpy = [0]

        def evac(dst, ps, w=128):
            cpy[0] += 1
            if cpy[0] % 3 == 0:
                p.op("vector", lambda e: e.tensor_copy(dst[:, 0:w], ps[:, 0:w]), reads=[ps], writes=[dst])
            else:
                p.op("scalar", lambda e: e.copy(dst[:, 0:w], ps[:, 0:w]), reads=[ps], writes=[dst])

        def mm(ps, lhsT_buf, lhsT_ap, rhs_buf, rhs_ap, w=128, start=True, stop=True, off=0):
            p.op("tensor", lambda e: e.matmul(ps[:, off:off + w], lhsT_ap, rhs_ap, start=start, stop=stop), reads=[lhsT_buf, rhs_buf], writes=[ps])

        for step in range(nsteps):
            for (b, hl, d) in chains:
                is_ctx = step < nbc
                if is_ctx:
                    k = step if d == 0 else nbc - 1 - step
                    blk = (B * L + b * Lc) // 128 + k
                else:
                    k = (step - nbc) if d == 0 else nbl - 1 - (step - nbc)
                    blk = (b * L) // 128 + k
                Mi = masks[:, 2 * d, :]
                nMs = masks[:, 2 * d + 1, :]
                last = 127 if d == 0 else 0
                Sb = S[(b, hl, d)]
                gi = d * HPC + hl
                qT, kT, ktm, vtm = qTs.next(), kTs.next(), ktms.next(), vtms.next()
                p.dma("sync", qT[:, :], QT.ap()[hl * 128:(hl + 1) * 128, blk * 128:(blk + 1) * 128], reads=[QT], writes=[qT])
                p.dma("sync", kT[:, :], KT.ap()[hl * 128:(hl + 1) * 128, blk * 128:(blk + 1) * 128], reads=[KT], writes=[kT])
                p.dma("sync", ktm[:, :], KTM.ap()[blk * 128:(blk + 1) * 128, hl * 128:(hl + 1) * 128], reads=[KTM], writes=[ktm])
                p.dma("sync", vtm[:, :], VTM.ap()[blk * 128:(blk + 1) * 128, hl * 128:(hl + 1) * 128], reads=[VTM], writes=[vtm])
                gcolB, gcB, cols = gcolBs.next(), gcBs.next(), colss.next()
                p.op("vector", lambda e, gcolB=gcolB, blk=blk, gi=gi: e.tensor_scalar(gcolB[:, :], ones[:, :], Gtm[:, blk, gi:gi + 1], None, ALU.mult), reads=[ones, Gtm], writes=[gcolB])
                ps1 = banks.next()
                mm(ps1, gcolB, gcolB[:, :], masks, Mi)
                mm(ps1, masks, Mi, Gtm, Gtm[:, blk, gi:gi + 1], w=1, off=128)
                p.op("scalar", lambda e, gcB=gcB, ps1=ps1: e.copy(gcB[:, :], ps1[:, 0:128]), reads=[ps1], writes=[gcB])
                p.op("scalar", lambda e, cols=cols, ps1=ps1: e.copy(cols[:, 0:1], ps1[:, 128:129]), reads=[ps1], writes=[cols])
                p.op("scalar", lambda e, cols=cols: e.activation(out=cols[:, 1:2], in_=cols[:, 0:1], func=AF.Exp), reads=[cols], writes=[cols])
                p.op("scalar", lambda e, cols=cols, gcB=gcB, last=last: e.activation(out=cols[:, 2:3], in_=cols[:, 0:1], func=AF.Exp, scale=-1.0, bias=gcB[:, last:last + 1]),
                     reads=[cols, gcB], writes=[cols])
                p.op("scalar", lambda e, cols=cols, gcB=gcB, last=last: e.activation(out=cols[:, 3:4], in_=gcB[:, last:last + 1], func=AF.Exp), reads=[gcB], writes=[cols])
                p.op("vector", lambda e, cols=cols, blk=blk, gi=gi: e.tensor_tensor(cols[:, 4:5], cols[:, 1:2], Btm[:, blk, gi:gi + 1], ALU.mult), reads=[cols, Btm], writes=[cols])
                E, EM = Es.next(), EMs.next()
                p.op("vector", lambda e, E=E, gcB=gcB, cols=cols: e.tensor_scalar(E[:, :], gcB[:, :], cols[:, 0:1], 0.0, ALU.subtract, ALU.min), reads=[gcB, cols], writes=[E])
                p.op("scalar", lambda e, E=E: e.activation(out=E[:, :], in_=E[:, :], func=AF.Exp), reads=[E], writes=[E])
                p.op("vector", lambda e, E=E, EM=EM, Mi=Mi: e.tensor_tensor(EM[:, :], E[:, :], Mi, ALU.mult), reads=[E, masks], writes=[EM])
                psA = banks.next()
                mm(psA, kT, kT[:, :], kT, kT[:, :])
                mm(psA, kT, kT[:, :], qT, qT[:, :], off=128)
                attn = attns.next()
                if not is_ctx:
                    p.op("vector", lambda e, attn=attn, psA=psA, EM=EM: e.tensor_tensor(attn[:, :], psA[:, 128:256], EM[:, :], ALU.mult), reads=[psA, EM], writes=[attn])
                bcolB = bcolBs.next()
                p.op("vector", lambda e, bcolB=bcolB, blk=blk, gi=gi: e.tensor_scalar(bcolB[:, :], ones[:, :], Btm[:, blk, gi:gi + 1], None, ALU.mult), reads=[ones, Btm], writes=[bcolB])
                ps3 = banks.next()
                mm(ps3, bcolB, bcolB[:, :], ident, ident[:, :])
                EMS, Nn = EMSs.next(), Ns.next()
                p.op("gpsimd", lambda e, EMS=EMS, EM=EM, nMs=nMs: e.tensor_tensor(EMS[:, :], EM[:, :], nMs, ALU.mult), reads=[EM, masks], writes=[EMS])
                p.op("vector", lambda e, EMS=EMS, ps3=ps3: e.tensor_tensor(EMS[:, :], EMS[:, :], ps3[:, 0:128], ALU.mult), reads=[EMS, ps3], writes=[EMS])
                p.op("vector", lambda e, Nn=Nn, psA=psA, EMS=EMS: e.tensor_tensor(Nn[:, :], psA[:, 0:128], EMS[:, :], ALU.mult), reads=[psA, EMS], writes=[Nn])
                ps4 = banks.next()
                NT = NTs.next()
                p.op("tensor", lambda e, ps4=ps4, Nn=Nn: e.transpose(ps4[:, 0:128], Nn[:, :], ident[:, :]), reads=[Nn, ident], writes=[ps4])
                evac(NT, ps4)
                BD = masks[:, 4, :]
                ODm = masks[:, 5, :]
                Nd, NTd, No = Ms.next(), Mts.next(), Ns.next()
                p.op("gpsimd", lambda e, Nd=Nd, Nn=Nn, BD=BD: e.tensor_tensor(Nd[:, :], Nn[:, :], BD, ALU.mult), reads=[Nn, masks], writes=[Nd])
                p.op("gpsimd", lambda e, NTd=NTd, NT=NT, BD=BD: e.tensor_tensor(NTd[:, :], NT[:, :], BD, ALU.mult), reads=[NT, masks], writes=[NTd])
                p.op("gpsimd", lambda e, No=No, Nn=Nn, ODm=ODm: e.tensor_tensor(No[:, :], Nn[:, :], ODm, ALU.mult), reads=[Nn, masks], writes=[No])
                P = Ps.next()
                p.op("gpsimd", lambda e, P=P, Nd=Nd: e.tensor_tensor(P[:, :], Nd[:, :], ident[:, :], ALU.add), reads=[Nd, ident], writes=[P])
                M, Mt = Nd, NTd
                for lev in range(1, 5):
                    lastlev = (lev == 4)
                    M2 = None
                    if not lastlev:
                        psM = banks.next()
                        mm(psM, Mt, Mt[:, :], M, M[:, :])
                        M2 = Ms.next()
                        evac(M2, psM)
                    psMt = banks.next()
                    mm(psMt, M, M[:, :], Mt, Mt[:, :])
                    Mt2 = Mts.next()
                    evac(Mt2, psMt)
                    psP = banks.next()
                    mm(psP, Mt2, Mt2[:, :], P, P[:, :])
                    p.op("vector", lambda e, P=P, psP=psP: e.tensor_tensor(P[:, :], P[:, :], psP[:, 0:128], ALU.add), reads=[P, psP], writes=[P])
                    M, Mt = M2, Mt2
                Dinv = P
                psD = banks.next()
                DinvT = Mts.next()
                p.op("tensor", lambda e, psD=psD, Dinv=Dinv: e.transpose(psD[:, 0:128], Dinv[:, :], ident[:, :]), reads=[Dinv, ident], writes=[psD])
                evac(DinvT, psD)
                psG = banks.next()
                GT = Ms.next()
                mm(psG, No, No[:, :], DinvT, DinvT[:, :])
                evac(GT, psG)
                psY = banks.next()
                Y1 = Mts.next()
                mm(psY, GT, GT[:, :], Dinv, Dinv[:, :])
                evac(Y1, psY)
                psY2 = banks.next()
                mm(psY2, GT, GT[:, :], Y1, Y1[:, :])
                X1 = Ps.next()
                p.op("vector", lambda e, X1=X1, Dinv=Dinv, psY2=psY2: e.tensor_tensor(X1[:, :], Dinv[:, :], psY2[:, 0:128], ALU.add), reads=[Dinv, psY2], writes=[X1])
                psY3 = banks.next()
                mm(psY3, GT, GT[:, :], X1, X1[:, :])
                P = Ps.next()
                p.op("vector", lambda e, P=P, X1=X1, psY3=psY3: e.tensor_tensor(P[:, :], X1[:, :], psY3[:, 0:128], ALU.add), reads=[X1, psY3], writes=[P])
                vb, kbg = vbs.next(), kbgs.next()
                p.op("scalar", lambda e, vb=vb, vtm=vtm, blk=blk, gi=gi: e.activation(out=vb[:, :], in_=vtm[:, :], func=AF.Identity, scale=Btm[:, blk, gi:gi + 1], bias=0.0), reads=[vtm, Btm], writes=[vb])
                p.op("scalar", lambda e, kbg=kbg, ktm=ktm, cols=cols: e.activation(out=kbg[:, :], in_=ktm[:, :], func=AF.Identity, scale=cols[:, 4:5], bias=0.0), reads=[ktm, cols], writes=[kbg])
                psU = banks.next()
                mm(psU, P, P[:, :], vb, vb[:, :])
                mm(psU, kbg, kbg[:, :], P, P[:, :], off=128)
                u, wT = us.next(), wTs.next()
                evac(u, psU)
                p.op("scalar", lambda e, wT=wT, psU=psU: e.copy(wT[:, :], psU[:, 128:256]), reads=[psU], writes=[wT])
                kg = kgs.next()
                p.op("scalar", lambda e, kg=kg, ktm=ktm, cols=cols: e.activation(out=kg[:, :], in_=ktm[:, :], func=AF.Identity, scale=cols[:, 2:3], bias=0.0), reads=[ktm, cols], writes=[kg])
                psS = banks.next()
                mm(psS, wT, wT[:, :], Sb, Sb[:, :])
                vnew = vnews.next()
                p.op("vector", lambda e, vnew=vnew, u=u, psS=psS: e.tensor_tensor(vnew[:, :], u[:, :], psS[:, 0:128], ALU.subtract), reads=[u, psS], writes=[vnew])
                if not is_ctx:
                    egB, qgT = egBs.next(), qgTs.next()
                    p.op("scalar", lambda e, egB=egB, gcB=gcB: e.activation(out=egB[:, :], in_=gcB[:, :], func=AF.Exp), reads=[gcB], writes=[egB])
                    p.op("gpsimd", lambda e, qgT=qgT, qT=qT, egB=egB: e.tensor_tensor(qgT[:, :], qT[:, :], egB[:, :], ALU.mult), reads=[qT, egB], writes=[qgT])
                    psO = banks.next()
                    mm(psO, qgT, qgT[:, :], Sb, Sb[:, :], start=True, stop=False)
                    mm(psO, attn, attn[:, :], vnew, vnew[:, :], start=False, stop=True)
                    o = os_.next()
                    evac(o, psO)
                    dst = OF if d == 0 else OB
                    r0 = blk * 128
                    p.dma("gpsimd", dst.ap()[r0:r0 + 128, hl * 128:(hl + 1) * 128], o[:, :], reads=[o], writes=[dst])
                psK = banks.next()
                mm(psK, kg, kg[:, :], vnew, vnew[:, :])
                p.op("vector", lambda e, Sb=Sb, cols=cols, psK=psK: e.scalar_tensor_tensor(Sb[:, :], Sb[:, :], cols[:, 3:4], psK[:, 0:128], ALU.mult, ALU.add),
                     reads=[Sb, cols, psK], writes=[Sb])


def emit_gdn_out(p, cfg, U, OF, OB, normw_d, ident, oout):
    D, B, L, Lc, NC = cfg["D"], cfg["B"], cfg["L"], cfg["Lc"], cfg["NC"]
    rms_eps = cfg["rms_eps"]
    H = D // 128
    HPC = H // NC
    CS = HPC * 128
    with phase(p):
        normw = load_tile(p, "normw", normw_d, [128, 128])
        pool = lambda name, k, shape=(128, 128): Rot([p.sbuf(name, list(shape), F32) for _ in range(k)])
        ofs, obs, zs, szs, sqs, ys, sss = pool("oof", 3), pool("oob", 3), pool("oz", 3), pool("osz", 3), pool("osq", 2), pool("oy", 3), pool("oss", 3, (128, 2))
        pts = Rot([p.psum("opt", [128, 512]) for _ in range(3)])
        ybs = Rot([p.sbuf("oyb", [128, 128], BF16) for _ in range(3)])
        for blk in range(B * L // 128):
            for hl in range(HPC):
                of, ob, z, sz, sq, y, ss = ofs.next(), obs.next(), zs.next(), szs.next(), sqs.next(), ys.next(), sss.next()
                r0 = blk * 128
                p.dma("sync", of[:, :], OF.ap()[r0:r0 + 128, hl * 128:(hl + 1) * 128], reads=[OF], writes=[of])
                p.dma("sync", ob[:, :], OB.ap()[r0:r0 + 128, hl * 128:(hl + 1) * 128], reads=[OB], writes=[ob])
                p.dma("sync", z[:, :], U.ap()[3 * CS + hl * 128:3 * CS + (hl + 1) * 128, r0:r0 + 128], reads=[U], writes=[z])
                pt = pts.next()
                p.op("tensor", lambda e, pt=pt, z=z: e.transpose(pt[:, 0:128], z[:, :], ident[:, :]), reads=[z, ident], writes=[pt])
                p.op("scalar", lambda e, pt=pt, sz=sz: e.activation(out=sz[:, :], in_=pt[:, 0:128], func=AF.Silu), reads=[pt], writes=[sz])
                p.op("vector", lambda e, of=of, ob=ob: e.tensor_tensor(of[:, :], of[:, :], ob[:, :], ALU.add), reads=[of, ob], writes=[of])
                p.op("scalar", lambda e, of=of, sq=sq, ss=ss: e.activation(out=sq[:, :], in_=of[:, :], func=AF.Square, accum_out=ss[:, 0:1]), reads=[of], writes=[sq, ss])
                p.op("vector", lambda e, ss=ss: e.tensor_scalar(ss[:, 1:2], ss[:, 0:1], 1.0 / 128, float(rms_eps), ALU.mult, ALU.add), reads=[ss], writes=[ss])
                p.op("scalar", lambda e, ss=ss: e.activation(out=ss[:, 1:2], in_=ss[:, 1:2], func=AF.Sqrt), reads=[ss], writes=[ss])
                p.op("vector", lambda e, ss=ss: e.reciprocal(ss[:, 1:2], ss[:, 1:2]), reads=[ss], writes=[ss])
                p.op("vector", lambda e, y=y, of=of, ss=ss: e.scalar_tensor_tensor(y[:, :], of[:, :], ss[:, 1:2], normw[:, :], ALU.mult, ALU.mult), reads=[of, ss, normw], writes=[y])
                yb = ybs.next()
                p.op("gpsimd", lambda e, y=y, sz=sz, yb=yb: e.tensor_tensor(yb[:, :], y[:, :], sz[:, :], ALU.mult), reads=[y, sz], writes=[yb])
                p.dma("gpsimd", oout.ap()[r0:r0 + 128, hl * 128:(hl + 1) * 128], yb[:, :], reads=[yb], writes=[oout])


import ml_dtypes
BF = ml_dtypes.bfloat16

def ptile(v, nd=None):
    v = np.asarray(v, np.float32)
    return np.ascontiguousarray(v.reshape(-1, 128).T)

def back_inputs(cfg, j, has_ctx, zT_lat, xT_lat, zT_ctx, xT_ctx, mods, lnp, b_out, wmix, wup, wdw, bdw, wdown):
    D, F, TO, CO, GW, B, L, Lc, NC = (cfg[k] for k in ("D", "F", "TO", "CO", "GW", "B", "L", "Lc", "NC"))
    HL = GW + 1
    TL = TO + 2 * HL
    Tc = TL + (CO + 2 if has_ctx else 0)
    cpb = NC // B
    b, r = j // cpb, j % cpb
    zT = np.zeros((D, Tc), BF)
    xT = np.zeros((D, Tc), np.float32)
    cmask = np.zeros((128, Tc), np.float32)
    lo = r * TO - HL
    hi = r * TO + TO + HL
    a, e = max(lo, 0), min(hi, L)
    zT[:, a - lo:e - lo] = zT_lat[:, b * L + a:b * L + e]
    xT[:, a - lo:e - lo] = xT_lat[:, b * L + a:b * L + e]
    cmask[:, a - lo:e - lo] = 1.0
    if has_ctx:
        lo = r * CO - 1
        hi = r * CO + CO + 1
        a, e = max(lo, 0), min(hi, Lc)
        zT[:, TL + a - lo:TL + e - lo] = zT_ctx[:, b * Lc + a:b * Lc + e]
        xT[:, TL + a - lo:TL + e - lo] = xT_ctx[:, b * Lc + a:b * Lc + e]
        cmask[:, TL + a - lo:TL + e - lo] = 1.0
    tok = (r * TO - HL) + np.arange(TL)
    gc = tok % GW
    mcl = np.broadcast_to((gc != GW - 1).astype(np.float32), (128, TL)).copy()
    mcr = np.broadcast_to((gc != 0).astype(np.float32), (128, TL)).copy()
    def modpack(c):
        return np.ascontiguousarray(np.stack([ptile(mods[c, k]) for k in (2, 4, 3, 5)], axis=1))
    nf = F // 128
    im = {
        "zT": zT, "xT": xT, "cmask": cmask, "mcl": mcl, "mcr": mcr,
        "modl": modpack(b), "modc": modpack(2),
        "lnp": np.ascontiguousarray(np.stack([ptile(v) for v in lnp], axis=1)),
        "bo": ptile(b_out) if b_out is not None else np.zeros((128, D // 128), np.float32),
        "wmix": np.ascontiguousarray(wmix, np.float32), "wup": np.ascontiguousarray(wup, np.float32),
        "wdw": np.ascontiguousarray(wdw.reshape(9, nf, 128).transpose(2, 1, 0), np.float32),
        "bdw": ptile(bdw), "wdown": np.ascontiguousarray(wdown, np.float32),
    }
    return im

def back_gather(cfg, outs, has_ctx):
    D, TO, CO, GW, B, L, Lc, NC = (cfg[k] for k in ("D", "TO", "CO", "GW", "B", "L", "Lc", "NC"))
    HL = GW + 1
    TL = TO + 2 * HL
    cpb = NC // B
    xl = np.zeros((D, B * L), np.float32)
    xc = np.zeros((D, B * Lc), np.float32) if has_ctx else None
    for j, o in enumerate(outs):
        b, r = j // cpb, j % cpb
        xl[:, b * L + r * TO:b * L + (r + 1) * TO] = o[:, HL:HL + TO]
        if has_ctx:
            xc[:, b * Lc + r * CO:b * Lc + (r + 1) * CO] = o[:, TL + 1:TL + 1 + CO]
    return xl, xc

def ada_inputs(D, NC, j, conds, w_adas, b_adas):
    ncols = 6 * D // NC
    cond = np.ascontiguousarray(np.stack([ptile(c) for c in conds], axis=2))
    im = {"cond": cond}
    bl = []
    for l, (w, b) in enumerate(zip(w_adas, b_adas)):
        im["w%d" % l] = np.ascontiguousarray(w[:, j * ncols:(j + 1) * ncols], np.float32)
        bl.append(ptile(b[j * ncols:(j + 1) * ncols]))
    im["b"] = np.ascontiguousarray(np.stack(bl, axis=1))
    return im

def ada_gather(D, NC, outs, nlayers):
    ncols = 6 * D // NC
    res = []
    for l in range(nlayers):
        m = np.zeros((3, 6 * D), np.float32)
        for j, o in enumerate(outs):
            blk = o[:, l].transpose(2, 1, 0).reshape(3, ncols)
            m[:, j * ncols:(j + 1) * ncols] = blk
        res.append(m.reshape(3, 6, D))
    return res

def hy_posfeat(Lx, emb=33):
    f32 = np.float32
    t = np.linspace(0.0, 1.0, Lx, dtype=f32)
    bands = (emb - 1) // 2
    ang = (f32(2.0 * np.pi) * np.arange(Lx, dtype=f32) / f32(Lx)).astype(f32)
    f = np.linspace(1e-4, bands - 1, bands, dtype=f32)
    fa = (f[None, :] * ang[:, None]).astype(f32)
    z = np.concatenate([t[:, None], np.cos(fa), -np.sin(fa)], axis=-1).astype(f32)
    pos = np.abs(np.arange(2 * Lx - 1) - (Lx - 1))
    pos = np.concatenate([pos, [0]])
    return np.ascontiguousarray(z[pos].T), t[pos]

def hy_deltas(D):
    import math
    max_decay = math.log(1e-2) / 0.3
    min_decay = math.log(1e-2) / 1.5
    return np.abs(np.linspace(min_decay, max_decay, D, dtype=np.float32))

def hyfront_inputs(cfg, j, xall, mods, P):
    D, B, L, Lc, NC = cfg["D"], cfg["B"], cfg["L"], cfg["Lc"], cfg["NC"]
    CS = D // NC
    sl = lambda part: slice(part * D + j * CS, part * D + (j + 1) * CS)
    cols = np.concatenate([np.arange(part * D + j * CS, part * D + (j + 1) * CS) for part in range(3)])
    modp = np.stack([np.stack([ptile(1.0 + mods[c, 1]), ptile(mods[c, 0])], axis=1) for c in range(3)], axis=1)
    wsh = np.zeros((CS // 64, 64, 3, 4), np.float32)
    for part in range(3):
        wpart = P["hy_w_short"][:, sl(part)]
        bpart = P["hy_b_short"][sl(part)]
        wsh[:, :, part, 0:3] = wpart.T.reshape(CS // 64, 64, 3)
        wsh[:, :, part, 3] = bpart.reshape(CS // 64, 64)
    zext, tt = hy_posfeat(L)
    zextc, ttc = hy_posfeat(Lc)
    fpv = np.zeros((64, 8), np.float32)
    for i in range(3):
        fpv[:, i] = P["hy_f_freq"]
    fpv[:, 3] = P["hy_f_b1"]; fpv[:, 4] = P["hy_f_b2"]; fpv[:, 5] = P["hy_f_b3"]
    wout = P["hy_f_wout"].reshape(64, 4, D)[:, :, j * CS:(j + 1) * CS]
    delt = hy_deltas(D)[j * CS:(j + 1) * CS]
    eye = np.eye(128, dtype=np.float32)
    im = {
        "xall": xall, "modp": np.ascontiguousarray(modp, np.float32),
        "win": np.ascontiguousarray(P["hy_w_in"][:, cols], np.float32), "bin": ptile(P["hy_b_in"][cols]),
        "wsh": wsh, "zext": zext, "zextc": zextc,
        "text": np.ascontiguousarray(np.broadcast_to(tt, (128, tt.size)), np.float32),
        "textc": np.ascontiguousarray(np.broadcast_to(ttc, (128, ttc.size)), np.float32),
        "w1": np.ascontiguousarray(P["hy_f_w1"], np.float32), "w2": np.ascontiguousarray(P["hy_f_w2"], np.float32),
        "w3": np.ascontiguousarray(P["hy_f_w3"], np.float32), "fp": fpv,
        "wout": np.ascontiguousarray(wout, np.float32), "negdelta": ptile(-delt),
        "skip": np.ascontiguousarray(P["hy_skip"][None, :, j * CS:(j + 1) * CS], np.float32),
        "ident": eye.astype(BF), "antiid": np.ascontiguousarray(eye[::-1]).astype(BF),
    }
    return im

def gdn_masks():
    j = np.arange(128)[:, None]; i = np.arange(128)[None, :]
    m = np.zeros((128, 6, 128), np.float32)
    m[:, 4] = ((i // 32) == (j // 32)); m[:, 5] = 1.0 - m[:, 4]
    m[:, 0] = (i >= j); m[:, 1] = -(i > j).astype(np.float32)
    m[:, 2] = (i <= j); m[:, 3] = -(i < j).astype(np.float32)
    return m

def gdnfront_inputs(cfg, j, xall, mods, P):
    D, B, L, Lc, NC = cfg["D"], cfg["B"], cfg["L"], cfg["Lc"], cfg["NC"]
    H = D // 128; HPC = H // NC; CS = HPC * 128
    heads = np.arange(j * HPC, (j + 1) * HPC)
    ch = np.concatenate([np.arange(h * 128, (h + 1) * 128) for h in heads])
    cols = np.concatenate([part * D + ch for part in range(4)])
    abcols = np.array([4 * D + ab * 2 * H + d * H + h for ab in range(2) for d in range(2) for h in heads])
    win = np.zeros((D, 4 * CS + 128), np.float32)
    win[:, :4 * CS] = P["gdn_w_in"][:, cols]
    win[:, 4 * CS:4 * CS + abcols.size] = P["gdn_w_in"][:, abcols]
    modp = np.stack([np.stack([ptile(1.0 + mods[c, 1]), ptile(mods[c, 0])], axis=1) for c in range(3)], axis=1)
    wcv = np.zeros((128, 3 * HPC, 3), np.float32)
    for part in range(3):
        for hl, h in enumerate(heads):
            wcv[:, part * HPC + hl, :] = P["gdn_w_conv"][:, part * D + h * 128: part * D + (h + 1) * 128].T
    abc = np.zeros((128, 2, 2 * HPC), np.float32)
    for d in range(2):
        for hl, h in enumerate(heads):
            abc[:, 0, d * HPC + hl] = P["gdn_dt_bias"][d, h]
            abc[:, 1, d * HPC + hl] = P["gdn_a_log"][d, h]
    return {"xall": xall, "modp": np.ascontiguousarray(modp, np.float32), "win": win, "wcv": wcv, "abc": abc,
            "normw": np.ascontiguousarray(np.broadcast_to(P["gdn_norm_w"].astype(np.float32), (128, 128))),
            "ident": np.eye(128, dtype=np.float32), "masks": gdn_masks()}

def fused_inputs(cfg, j, xall, conds, lay):
    D, F, B, L, Lc, NC, GW = cfg["D"], cfg["F"], cfg["B"], cfg["L"], cfg["Lc"], cfg["NC"], cfg["GW"]
    CS = D // NC
    ncs = CS // 128
    nfc = -(-(F // 128) // NC)
    FS = nfc * 128
    FP = FS * NC
    own = slice(j * CS, (j + 1) * CS)
    im = {"xall": xall, "xown": np.ascontiguousarray(xall[own])}
    sel = np.zeros((128, NC), np.float32); sel[:, j] = 1.0
    im["sel"] = sel
    a = ada_inputs(D, NC, j, conds, [lay[0]["w_ada"], lay[1]["w_ada"]], [lay[0]["b_ada"], lay[1]["b_ada"]])
    im["cond"] = a["cond"]; im["wada0"] = a["w0"]; im["wada1"] = a["w1"]; im["bada"] = a["b"]
    fp = np.arange(j * FS, (j + 1) * FS)
    valid = fp < F
    fpc = np.minimum(fp, F - 1)
    for l in range(2):
        P = lay[l]
        im["lnp%d" % l] = np.ascontiguousarray(np.stack([ptile(np.asarray(P[k])[own]) for k in ("ln1_g", "ln1_b", "ln2_g", "ln2_b")], axis=1))
        wm = P["hy_w_out"] if l == 0 else P["gdn_w_out"]
        im["wmix%d" % l] = np.ascontiguousarray(wm[:, own], np.float32)
        wu = np.zeros((D, 2 * FS), np.float32)
        wu[:, :FS][:, valid] = P["ffn_w_up"][:, fp[valid]]
        wu[:, FS:][:, valid] = P["ffn_w_up"][:, F + fp[valid]]
        im["wup%d" % l] = wu
        wd = np.zeros((9, FS), np.float32)
        wd[:, valid] = P["ffn_w_dw"].reshape(9, F)[:, fp[valid]]
        im["wdw%d" % l] = np.ascontiguousarray(wd.reshape(9, nfc, 128).transpose(2, 1, 0))
        bd = np.zeros((FS,), np.float32)
        bd[valid] = P["ffn_b_dw"][fp[valid]]
        im["bdw%d" % l] = ptile(bd)
        wdn = np.zeros((FP, CS), np.float32)
        wdn[:F] = P["ffn_w_down"][:, own]
        im["wdown%d" % l] = wdn
    im["bo0"] = ptile(np.asarray(lay[0]["hy_b_out"])[own])
    W_ = 2048 + 2 * (GW + 1)
    gc = (np.arange(W_) - (GW + 1)) % GW
    im["mcl"] = np.ascontiguousarray(np.broadcast_to((gc != GW - 1).astype(np.float32), (128, W_)))
    im["mcr"] = np.ascontiguousarray(np.broadcast_to((gc != 0).astype(np.float32), (128, W_)))
    dummy_mods = np.zeros((3, 6, D), np.float32)
    h = hyfront_inputs(cfg, j, xall, dummy_mods, lay[0])
    for k, v in h.items():
        if k not in ("xall", "modp"):
            im["hy_" + k] = v
    g = gdnfront_inputs(cfg, j, xall, dummy_mods, lay[1])
    for k, v in g.items():
        if k not in ("xall", "modp"):
            im["gd_" + k] = v
    return im


D_MODEL, BATCH, SEQ, DEPTH = 4096, 2, 8192, 2
CTX_LEN, D_FF, GRID_W, NCORES = 256, 11008, 64, 8
CFG = dict(D=D_MODEL, F=D_FF, TO=SEQ * BATCH // NCORES, CO=CTX_LEN * BATCH // NCORES, GW=GRID_W, B=BATCH, L=SEQ, Lc=CTX_LEN,
           NC=NCORES, alpha=(2 * DEPTH) ** 0.25, ln_eps=1e-5, rms_eps=1e-6, l2_eps=1e-6)
_PROGS = {}


def _prog(key, fn):
    if key not in _PROGS:
        _PROGS[key] = fn()
    return _PROGS[key]


def _run(nc, ims):
    res = run_bass_kernel_spmd(nc, ims, core_ids=list(range(NCORES)))
    return res.results


def kernel(x, c, ctx, c_ctx,
           l0_w_ada, l0_b_ada, l0_ln1_g, l0_ln1_b, l0_ln2_g, l0_ln2_b,
           l0_hy_w_in, l0_hy_b_in, l0_hy_w_short, l0_hy_b_short,
           l0_hy_f_w1, l0_hy_f_b1, l0_hy_f_w2, l0_hy_f_b2, l0_hy_f_w3, l0_hy_f_b3,
           l0_hy_f_wout, l0_hy_f_freq, l0_hy_skip, l0_hy_w_out, l0_hy_b_out,
           l0_ffn_w_up, l0_ffn_w_dw, l0_ffn_b_dw, l0_ffn_w_down,
           l1_w_ada, l1_b_ada, l1_ln1_g, l1_ln1_b, l1_ln2_g, l1_ln2_b,
           l1_gdn_w_in, l1_gdn_w_conv, l1_gdn_a_log, l1_gdn_dt_bias, l1_gdn_norm_w, l1_gdn_w_out,
           l1_ffn_w_up, l1_ffn_w_dw, l1_ffn_b_dw, l1_ffn_w_down):
    cfg = CFG
    D, B, L, Lc, NC = cfg["D"], cfg["B"], cfg["L"], cfg["Lc"], cfg["NC"]
    f32 = np.float32
    A = lambda v: np.asarray(v, f32)
    P0 = dict(w_ada=A(l0_w_ada), b_ada=A(l0_b_ada), ln1_g=A(l0_ln1_g), ln1_b=A(l0_ln1_b), ln2_g=A(l0_ln2_g), ln2_b=A(l0_ln2_b),
              hy_w_in=A(l0_hy_w_in), hy_b_in=A(l0_hy_b_in), hy_w_short=A(l0_hy_w_short), hy_b_short=A(l0_hy_b_short),
              hy_f_w1=A(l0_hy_f_w1), hy_f_b1=A(l0_hy_f_b1), hy_f_w2=A(l0_hy_f_w2), hy_f_b2=A(l0_hy_f_b2),
              hy_f_w3=A(l0_hy_f_w3), hy_f_b3=A(l0_hy_f_b3), hy_f_wout=A(l0_hy_f_wout), hy_f_freq=A(l0_hy_f_freq),
              hy_skip=A(l0_hy_skip), hy_w_out=A(l0_hy_w_out), hy_b_out=A(l0_hy_b_out),
              ffn_w_up=A(l0_ffn_w_up), ffn_w_dw=A(l0_ffn_w_dw), ffn_b_dw=A(l0_ffn_b_dw), ffn_w_down=A(l0_ffn_w_down))
    P1 = dict(w_ada=A(l1_w_ada), b_ada=A(l1_b_ada), ln1_g=A(l1_ln1_g), ln1_b=A(l1_ln1_b), ln2_g=A(l1_ln2_g), ln2_b=A(l1_ln2_b),
              gdn_w_in=A(l1_gdn_w_in), gdn_w_conv=A(l1_gdn_w_conv), gdn_a_log=A(l1_gdn_a_log),
              gdn_dt_bias=A(l1_gdn_dt_bias), gdn_norm_w=A(l1_gdn_norm_w), gdn_w_out=A(l1_gdn_w_out),
              ffn_w_up=A(l1_ffn_w_up), ffn_w_dw=A(l1_ffn_w_dw), ffn_b_dw=A(l1_ffn_b_dw), ffn_w_down=A(l1_ffn_w_down))
    conds = np.concatenate([A(c), A(c_ctx)[None, :]], axis=0)
    nc_a = _prog("ada", lambda: build_ada(D, 6 * D // NC, DEPTH))
    outs = _run(nc_a, [ada_inputs(D, NC, j, conds, [P0["w_ada"], P1["w_ada"]], [P0["b_ada"], P1["b_ada"]]) for j in range(NC)])
    mods = ada_gather(D, NC, [o["out"] for o in outs], DEPTH)
    xT = np.ascontiguousarray(A(x).reshape(B * L, D).T)
    cT = np.ascontiguousarray(A(ctx).reshape(B * Lc, D).T)
    xall = np.ascontiguousarray(np.concatenate([xT, cT], axis=1))
    nc_b = _prog("hy", lambda: build_hyfront(cfg))
    outs = _run(nc_b, [hyfront_inputs(cfg, j, xall, mods[0], P0) for j in range(NC)])
    zT = np.concatenate([o["zout"] for o in outs], axis=0)
    del outs
    lnp0 = [P0["ln1_g"], P0["ln1_b"], P0["ln2_g"], P0["ln2_b"]]
    nc_c0 = _prog("back0", lambda: build_back(cfg, True, True))
    outs = _run(nc_c0, [back_inputs(cfg, j, True, zT[:, :B * L], xT, zT[:, B * L:], cT, mods[0], lnp0, P0["hy_b_out"], P0["hy_w_out"],
                                    P0["ffn_w_up"], P0["ffn_w_dw"], P0["ffn_b_dw"], P0["ffn_w_down"]) for j in range(NC)])
    x1T, c1T = back_gather(cfg, [o["out"] for o in outs], True)
    del outs, zT, xall
    xall = np.ascontiguousarray(np.concatenate([x1T, c1T], axis=1))
    nc_d = _prog("gdn", lambda: build_gdnfront(cfg))
    outs = _run(nc_d, [gdnfront_inputs(cfg, j, xall, mods[1], P1) for j in range(NC)])
    oT = np.ascontiguousarray(np.concatenate([o["oout"] for o in outs], axis=1).T)
    del outs, xall
    lnp1 = [P1["ln1_g"], P1["ln1_b"], P1["ln2_g"], P1["ln2_b"]]
    nc_c1 = _prog("back1", lambda: build_back(cfg, False, False))
    outs = _run(nc_c1, [back_inputs(cfg, j, False, oT, x1T, None, None, mods[1], lnp1, None, P1["gdn_w_out"],
                                    P1["ffn_w_up"], P1["ffn_w_dw"], P1["ffn_b_dw"], P1["ffn_w_down"]) for j in range(NC)])
    x2T, _ = back_gather(cfg, [o["out"] for o in outs], False)
    return np.ascontiguousarray(x2T.T).reshape(B, L, D).astype(f32)
```

```python
import contextlib
import numpy as np
import concourse.bass as bass
import concourse.mybir as mybir
from concourse.bass_utils import run_bass_kernel_spmd

F32 = mybir.dt.float32
BF16 = mybir.dt.bfloat16
ALU = mybir.AluOpType
AF = mybir.ActivationFunctionType
AX = mybir.AxisListType

ENGS = ("tensor", "vector", "scalar", "gpsimd", "sync")
N_DMA_SEMS = 8


class Buf:
    __slots__ = ("t", "name", "last_w", "readers", "excl")

    def __init__(self, t, name="", excl=False):
        self.t = t
        self.name = name
        self.excl = excl
        self.last_w = None
        self.readers = {}

    def __getitem__(self, idx):
        return self.t[idx]

    def ap(self):
        return self.t.ap()


class Prog:
    def __init__(self, nc, stack, same_engine_sync=True):
        self.nc = nc
        self.ops = {e: [] for e in ENGS}
        self.cnt = {}
        self.known = {e: {} for e in ENGS}
        self.dma_i = {e: 0 for e in ENGS}
        self.same_engine_sync = same_engine_sync
        self.stack = stack
        self.sems = {}
        self.uid = 0

    def sbuf(self, name, shape, dt):
        self.uid += 1
        t = self.stack.enter_context(self.nc.sbuf_tensor("%s_%d" % (name, self.uid), list(shape), dt))
        return Buf(t, name)

    def psum(self, name, shape, dt=F32):
        self.uid += 1
        t = self.stack.enter_context(self.nc.psum_tensor("%s_%d" % (name, self.uid), list(shape), dt))
        return Buf(t, name, excl=True)

    def dram(self, name, shape, dt, kind="Internal"):
        t = self.nc.dram_tensor(name, list(shape), dt, kind=kind)
        return Buf(t, name)

    def _deps(self, eng, reads, writes):
        need = {}

        def add(k, v):
            if need.get(k, 0) < v:
                need[k] = v
        for b in reads:
            if b.last_w is not None:
                add(*b.last_w)
        for b in writes:
            if b.last_w is not None:
                add(*b.last_w)
            for k, v in b.readers.items():
                add(k, v)
        waits = []
        kn = self.known[eng]
        for k, v in need.items():
            if k == eng and (eng == "tensor" or not self.same_engine_sync):
                continue
            if kn.get(k, 0) >= v:
                continue
            kn[k] = v
            waits.append((k, v))
        return waits

    def _commit(self, key, val, reads, writes):
        for b in reads:
            if b.readers.get(key, 0) < val:
                b.readers[key] = val
        for b in writes:
            b.last_w = (key, val)
            b.readers = {}

    def op(self, eng, fn, reads=(), writes=()):
        if any(b.excl for b in reads):
            writes = list(writes) + [b for b in reads if b.excl and b not in writes]
            reads = [b for b in reads if not b.excl]
        waits = self._deps(eng, reads, writes)
        val = self.cnt.get(eng, 0) + 1
        self.cnt[eng] = val
        self.ops[eng].append((waits, fn, eng, 1))
        self._commit(eng, val, reads, writes)

    def dma(self, eng, out_ap, in_ap, reads=(), writes=()):
        i = self.dma_i[eng]
        self.dma_i[eng] = i + 1
        key = "dma_%s_%d" % (eng, i % N_DMA_SEMS)
        waits = self._deps(eng, reads, writes)
        prev = self.cnt.get(key, 0)
        kn = self.known[eng]
        if prev > 0 and kn.get(key, 0) < prev:
            kn[key] = prev
            waits.append((key, prev))
        val = prev + 16
        self.cnt[key] = val
        self.ops[eng].append((waits, lambda e: e.dma_start(out=out_ap, in_=in_ap), key, 16))
        self._commit(key, val, reads, writes)

    def barrier(self):
        snap = dict(self.cnt)
        for eng in ENGS:
            waits = []
            kn = self.known[eng]
            for k, v in snap.items():
                if k == eng:
                    continue
                if kn.get(k, 0) < v:
                    kn[k] = v
                    waits.append((k, v))
            if waits:
                self.ops[eng].append((waits, None, None, 0))

    def emit(self):
        nc = self.nc
        keys = list(self.cnt.keys())
        for k in keys:
            self.sems[k] = self.stack.enter_context(nc.semaphore("s_" + k))
        fin = [(k, self.cnt[k]) for k in keys]
        self.ops["sync"].append((fin, None, None, 0))
        with nc.Block() as block:
            def mk(engname):
                def body(e):
                    for waits, fn, key, inc in self.ops[engname]:
                        for (k, v) in waits:
                            e.wait_ge(self.sems[k], v)
                        if fn is not None:
                            fn(e).then_inc(self.sems[key], inc)
                return body
            block.tensor(mk("tensor"))
            block.vector(mk("vector"))
            block.scalar(mk("scalar"))
            block.gpsimd(mk("gpsimd"))
            block.sync(mk("sync"))


class Rot:
    def __init__(self, bufs):
        self.bufs = bufs
        self.i = 0

    def next(self):
        b = self.bufs[self.i % len(self.bufs)]
        self.i += 1
        return b


def emit_cast(p, dst, src, rows, cols, rb=None):
    if rb is None:
        rb = max(1, min(rows, (4 << 20) // (cols * 4)))
    r = 0
    while r < rows:
        n = min(rb, rows - r)
        p.dma("gpsimd", dst.ap()[r:r + n, :], src.ap()[r:r + n, :], reads=[src], writes=[dst])
        r += n


def emit_linear(p, inT, W, Ci, Co, T, epilogue, psums, TT=512, cb=256, kg=None, tag="lin",
                in_col0=0, w_col0=0):
    nk = Ci // 128
    if kg is None:
        kg = nk
        while kg * cb * 2 > 16384:
            kg = (kg + 1) // 2
    ngrp = (nk + kg - 1) // kg
    xslots = 2 if nk * TT * 2 <= 40000 else 1
    xin = Rot([p.sbuf(tag + "_x", [128, nk, TT], BF16) for _ in range(xslots)])
    wts = Rot([p.sbuf(tag + "_w", [128, kg, cb], BF16) for _ in range(3)])
    inv = inT.ap().rearrange("(k p) t -> p k t", p=128)
    wv = W.ap().rearrange("(k p) c -> p k c", p=128)
    nchunk = cb // 128
    t0 = 0
    while t0 < T:
        n = min(TT, T - t0)
        xb = xin.next()
        p.dma("sync", xb[:, :, 0:n], inv[:, :, in_col0 + t0:in_col0 + t0 + n], reads=[inT], writes=[xb])
        for c0 in range(0, Co, cb):
            ncc = min(nchunk, (Co - c0) // 128)
            pss = [psums.next() for _ in range(ncc)]
            for g in range(ngrp):
                k0 = g * kg
                nkk = min(kg, nk - k0)
                wb = wts.next()
                p.dma("sync", wb[:, 0:nkk, 0:ncc * 128], wv[:, k0:k0 + nkk, w_col0 + c0:w_col0 + c0 + ncc * 128],
                      reads=[W], writes=[wb])
                for c in range(ncc):
                    for k in range(nkk):
                        first = (g == 0 and k == 0)
                        last = (g == ngrp - 1 and k == nkk - 1)
                        p.op("tensor",
                             (lambda e, ps=pss[c], wb=wb, xb=xb, k=k, kk=k0 + k, c=c, n=n, first=first, last=last:
                              e.matmul(ps[:, 0:n], wb[:, k, c * 128:(c + 1) * 128], xb[:, kk, 0:n],
                                       start=first, stop=last)),
                             reads=[wb, xb], writes=[pss[c]])
                if g == ngrp - 1:
                    for c in range(ncc):
                        epilogue(p, pss[c], (c0 // 128) + c, t0, n)
        t0 += n


@contextlib.contextmanager
def phase(p):
    p.barrier()
    old = p.stack
    with contextlib.ExitStack() as st:
        p.stack = st
        yield
        p.barrier()
    p.stack = old


def segs(t0, n, tl):
    out = []
    if t0 < tl:
        out.append((0, min(n, tl - t0), 0))
    if t0 + n > tl:
        out.append((max(0, tl - t0), n, 1))
    return out


def emit_resid_linear(p, inT, Wb, Ci, D, Tc, TL, resid, bias, gates, V, meanB, rstdB, ones, eps, tag, alpha):
    nd = D // 128
    with phase(p):
        psums = Rot([p.psum(tag + "ps", [128, 512]) for _ in range(4)])
        st_s = Rot([p.psum(tag + "ss", [128, 512]) for _ in range(2)])
        st_q = Rot([p.psum(tag + "sq", [128, 512]) for _ in range(2)])
        xr = Rot([p.sbuf(tag + "xr", [128, 512], F32) for _ in range(3)])
        vb = Rot([p.sbuf(tag + "vb", [128, 512], F32) for _ in range(3)])
        vq = Rot([p.sbuf(tag + "vq", [128, 512], F32) for _ in range(3)])
        tmp = Rot([p.sbuf(tag + "tm", [128, 512], F32) for _ in range(2)])
        state = {}

        def epi(p, ps, cc, t0, n):
            x_t = xr.next()
            p.dma("sync", x_t[:, 0:n], resid.ap()[cc * 128:(cc + 1) * 128, t0:t0 + n], reads=[resid], writes=[x_t])
            v_t = vb.next()
            q_t = vq.next()
            for (a, b, ic) in segs(t0, n, TL):
                g = gates[ic]
                if bias is not None:
                    p.op("vector", lambda e, a=a, b=b, g=g: e.tensor_scalar(v_t[:, a:b], ps[:, a:b], bias[:, cc:cc + 1], g[:, cc:cc + 1], ALU.add, ALU.mult),
                         reads=[ps, bias, g], writes=[v_t])
                else:
                    p.op("vector", lambda e, a=a, b=b, g=g: e.tensor_scalar(v_t[:, a:b], ps[:, a:b], g[:, cc:cc + 1], None, ALU.mult),
                         reads=[ps, g], writes=[v_t])
            p.op("vector", lambda e: e.scalar_tensor_tensor(v_t[:, 0:n], x_t[:, 0:n], float(alpha), v_t[:, 0:n], ALU.mult, ALU.add),
                 reads=[x_t, v_t], writes=[v_t])
            p.op("scalar", lambda e: e.activation(out=q_t[:, 0:n], in_=v_t[:, 0:n], func=AF.Square), reads=[v_t], writes=[q_t])
            p.dma("gpsimd", V.ap()[cc * 128:(cc + 1) * 128, t0:t0 + n], v_t[:, 0:n], reads=[v_t], writes=[V])
            if cc == 0:
                state["s"] = st_s.next()
                state["q"] = st_q.next()
            s_ps, q_ps = state["s"], state["q"]
            p.op("tensor", lambda e: e.matmul(s_ps[:, 0:n], ones[:, :], v_t[:, 0:n], start=(cc == 0), stop=(cc == nd - 1)),
                 reads=[ones, v_t], writes=[s_ps])
            p.op("tensor", lambda e: e.matmul(q_ps[:, 0:n], ones[:, :], q_t[:, 0:n], start=(cc == 0), stop=(cc == nd - 1)),
                 reads=[ones, q_t], writes=[q_ps])
            if cc == nd - 1:
                m = meanB
                r = rstdB
                t_ = tmp.next()
                p.op("scalar", lambda e: e.activation(out=m[:, t0:t0 + n], in_=s_ps[:, 0:n], func=AF.Identity, scale=1.0 / D, bias=0.0),
                     reads=[s_ps], writes=[m])
                p.op("vector", lambda e: e.tensor_tensor(t_[:, 0:n], m[:, t0:t0 + n], m[:, t0:t0 + n], ALU.mult), reads=[m], writes=[t_])
                p.op("vector", lambda e: e.scalar_tensor_tensor(t_[:, 0:n], q_ps[:, 0:n], 1.0 / D, t_[:, 0:n], ALU.mult, ALU.subtract),
                     reads=[q_ps, t_], writes=[t_])
                p.op("vector", lambda e: e.tensor_scalar(t_[:, 0:n], t_[:, 0:n], float(eps), None, ALU.add), reads=[t_], writes=[t_])
                p.op("scalar", lambda e: e.activation(out=t_[:, 0:n], in_=t_[:, 0:n], func=AF.Sqrt), reads=[t_], writes=[t_])
                p.op("vector", lambda e: e.reciprocal(r[:, t0:t0 + n], t_[:, 0:n]), reads=[t_], writes=[r])
        emit_linear(p, inT, Wb, Ci, D, Tc, epi, psums, TT=512, cb=256, tag=tag)


def emit_ln_apply(p, V, D, Tc, TL, meanB, rstdB, g, b, outX, mods, outH, mask, tag):
    nd = D // 128
    with phase(p):
        vt = Rot([p.sbuf(tag + "v", [128, 512], F32) for _ in range(3)])
        xt = Rot([p.sbuf(tag + "x", [128, 512], F32) for _ in range(3)])
        ht = Rot([p.sbuf(tag + "h", [128, 512], BF16) for _ in range(3)])
        t0 = 0
        while t0 < Tc:
            n = min(512, Tc - t0)
            for cc in range(nd):
                v_t = vt.next()
                x_t = xt.next()
                p.dma("sync", v_t[:, 0:n], V.ap()[cc * 128:(cc + 1) * 128, t0:t0 + n], reads=[V], writes=[v_t])
                p.op("vector", lambda e, v_t=v_t, t0=t0, n=n: e.tensor_tensor(v_t[:, 0:n], v_t[:, 0:n], meanB[:, t0:t0 + n], ALU.subtract),
                     reads=[v_t, meanB], writes=[v_t])
                p.op("vector", lambda e, v_t=v_t, t0=t0, n=n: e.tensor_tensor(v_t[:, 0:n], v_t[:, 0:n], rstdB[:, t0:t0 + n], ALU.mult),
                     reads=[v_t, rstdB], writes=[v_t])
                p.op("scalar", lambda e, v_t=v_t, x_t=x_t, n=n, cc=cc: e.activation(out=x_t[:, 0:n], in_=v_t[:, 0:n], func=AF.Identity,
                                                                                   scale=g[:, cc:cc + 1], bias=b[:, cc:cc + 1]),
                     reads=[v_t, g, b], writes=[x_t])
                p.dma("gpsimd", outX.ap()[cc * 128:(cc + 1) * 128, t0:t0 + n], x_t[:, 0:n], reads=[x_t], writes=[outX])
                if mods is not None:
                    h_t = ht.next()
                    for (a, bb, ic) in segs(t0, n, TL):
                        sc1p, sh = mods[ic]
                        p.op("scalar", lambda e, a=a, bb=bb, sc1p=sc1p, sh=sh, x_t=x_t, v_t=v_t, cc=cc:
                             e.activation(out=v_t[:, a:bb], in_=x_t[:, a:bb], func=AF.Identity, scale=sc1p[:, cc:cc + 1], bias=sh[:, cc:cc + 1]),
                             reads=[x_t, sc1p, sh], writes=[v_t])
                    p.op("vector", lambda e, v_t=v_t, h_t=h_t, t0=t0, n=n: e.tensor_tensor(h_t[:, 0:n], v_t[:, 0:n], mask[:, t0:t0 + n], ALU.mult),
                         reads=[v_t, mask], writes=[h_t])
                    p.dma("gpsimd", outH.ap()[cc * 128:(cc + 1) * 128, t0:t0 + n], h_t[:, 0:n], reads=[h_t], writes=[outH])
            t0 += n


def emit_convglu(p, G, A, F, TO, CO, GW, wdw, bdw, mcl, mcr, has_ctx, tag="cg"):
    nf = F // 128
    HL = GW + 1
    TL = TO + 2 * HL
    Tc = TL + (CO + 2 if has_ctx else 0)
    with phase(p):
        z = p.sbuf(tag + "z", [128, nf, HL], BF16)
        p.op("vector", lambda e: e.memset(z[:, :, :], 0.0), writes=[z])
        Av = A.ap().rearrange("(c q) t -> q c t", q=128)
        p.dma("gpsimd", Av[:, :, 0:HL], z[:, :, :], reads=[z], writes=[A])
        p.dma("gpsimd", Av[:, :, HL + TO:TL], z[:, :, :], reads=[z], writes=[A])
        gts = Rot([p.sbuf(tag + "g", [128, Tc], F32) for _ in range(2)])
        vts = Rot([p.sbuf(tag + "v", [128, Tc], F32) for _ in range(2)])
        gls = Rot([p.sbuf(tag + "l", [128, TL], F32) for _ in range(2)])
        grs = Rot([p.sbuf(tag + "r", [128, TL], F32) for _ in range(2)])
        accs = Rot([p.sbuf(tag + "a", [128, TO + CO], F32) for _ in range(2)])
        acc2s = Rot([p.sbuf(tag + "b", [128, TO + CO], F32) for _ in range(2)])
        outs = Rot([p.sbuf(tag + "o", [128, TO + CO + 2], BF16) for _ in range(2)])
        for ob_ in outs.bufs:
            p.op("vector", lambda e, ob_=ob_: e.memset(ob_[:, :], 0.0), writes=[ob_])
        for f in range(nf):
            gt, vt, gl, gr, acc, acc2, ob = gts.next(), vts.next(), gls.next(), grs.next(), accs.next(), acc2s.next(), outs.next()
            p.dma("sync", gt[:, :], G.ap()[f * 128:(f + 1) * 128, :], reads=[G], writes=[gt])
            p.dma("sync", vt[:, :], G.ap()[F + f * 128:F + (f + 1) * 128, :], reads=[G], writes=[vt])
            p.op("gpsimd", lambda e, gl=gl, gt=gt: e.tensor_tensor(gl[:, :], gt[:, 0:TL], mcl[:, :], ALU.mult), reads=[gt, mcl], writes=[gl])
            p.op("gpsimd", lambda e, gr=gr, gt=gt: e.tensor_tensor(gr[:, :], gt[:, 0:TL], mcr[:, :], ALU.mult), reads=[gt, mcr], writes=[gr])
            first = True
            for dr in (0, -1, 1):
                for dc in (0, -1, 1):
                    src = gt if dc == 0 else (gl if dc == -1 else gr)
                    o = HL + GW * dr + dc
                    tap = (dr + 1) * 3 + (dc + 1)
                    if first:
                        p.op("vector", lambda e, src=src, o=o, tap=tap, acc=acc, f=f:
                             e.tensor_scalar(acc[:, 0:TO], src[:, o:o + TO], wdw[:, f, tap:tap + 1], None, ALU.mult),
                             reads=[src, wdw], writes=[acc])
                        first = False
                    else:
                        p.op("vector", lambda e, src=src, o=o, tap=tap, acc=acc, f=f:
                             e.scalar_tensor_tensor(acc[:, 0:TO], src[:, o:o + TO], wdw[:, f, tap:tap + 1], acc[:, 0:TO], ALU.mult, ALU.add),
                             reads=[src, wdw, acc], writes=[acc])
            if has_ctx:
                for dc in (0, -1, 1):
                    o = TL + 1 + dc
                    tap = 3 + (dc + 1)
                    if dc == 0:
                        p.op("vector", lambda e, o=o, tap=tap, acc=acc, gt=gt, f=f:
                             e.tensor_scalar(acc[:, TO:TO + CO], gt[:, o:o + CO], wdw[:, f, tap:tap + 1], None, ALU.mult),
                             reads=[gt, wdw, acc], writes=[acc])
                    else:
                        p.op("vector", lambda e, o=o, tap=tap, acc=acc, gt=gt, f=f:
                             e.scalar_tensor_tensor(acc[:, TO:TO + CO], gt[:, o:o + CO], wdw[:, f, tap:tap + 1], acc[:, TO:TO + CO], ALU.mult, ALU.add),
                             reads=[gt, wdw, acc], writes=[acc])
            nn = TO + (CO if has_ctx else 0)
            p.op("scalar", lambda e, acc=acc, acc2=acc2, f=f, nn=nn: e.activation(out=acc2[:, 0:nn], in_=acc[:, 0:nn], func=AF.Gelu,
                                                                                 bias=bdw[:, f:f + 1], scale=1.0),
                 reads=[acc, bdw], writes=[acc2])
            p.op("vector", lambda e, acc2=acc2, vt=vt, ob=ob: e.tensor_tensor(ob[:, 0:TO], acc2[:, 0:TO], vt[:, HL:HL + TO], ALU.mult),
                 reads=[acc2, vt], writes=[ob])
            if has_ctx:
                p.op("vector", lambda e, acc2=acc2, vt=vt, ob=ob: e.tensor_tensor(ob[:, TO + 1:TO + 1 + CO], acc2[:, TO:TO + CO], vt[:, TL + 1:TL + 1 + CO], ALU.mult),
                     reads=[acc2, vt], writes=[ob])
            p.dma("gpsimd", A.ap()[f * 128:(f + 1) * 128, HL:HL + TO], ob[:, 0:TO], reads=[ob], writes=[A])
            if has_ctx:
                p.dma("gpsimd", A.ap()[f * 128:(f + 1) * 128, TL:TL + CO + 2], ob[:, TO:TO + CO + 2], reads=[ob], writes=[A])


def load_tile(p, name, src, shape, dt=F32):
    t = p.sbuf(name, shape, dt)
    if len(shape) == 2:
        p.dma("sync", t[:, :], src.ap()[:, :], reads=[src], writes=[t])
    else:
        p.dma("sync", t[:, :, :], src.ap()[:, :, :], reads=[src], writes=[t])
    return t


def build_back(cfg, has_ctx, has_bias):
    D, F, TO, CO, GW = cfg["D"], cfg["F"], cfg["TO"], cfg["CO"], cfg["GW"]
    alpha, eps = cfg["alpha"], cfg["ln_eps"]
    nd, nf = D // 128, F // 128
    HL = GW + 1
    TL = TO + 2 * HL
    Tc = TL + (CO + 2 if has_ctx else 0)
    nc = bass.Bass("TRN2", target_bir_lowering=False)
    with contextlib.ExitStack() as st:
        p = Prog(nc, st)
        ext = lambda n, s, d: p.dram(n, s, d, kind="ExternalInput")
        zT = ext("zT", [D, Tc], BF16)
        xT = ext("xT", [D, Tc], F32)
        cmask = ext("cmask", [128, Tc], F32)
        mcl_d = ext("mcl", [128, TL], F32)
        mcr_d = ext("mcr", [128, TL], F32)
        modl_d = ext("modl", [128, 4, nd], F32)
        modc_d = ext("modc", [128, 4, nd], F32)
        lnp_d = ext("lnp", [128, 4, nd], F32)
        bo_d = ext("bo", [128, nd], F32)
        wmix = ext("wmix", [D, D], F32)
        wup = ext("wup", [D, 2 * F], F32)
        wdw_d = ext("wdw", [128, nf, 9], F32)
        bdw_d = ext("bdw", [128, nf], F32)
        wdown = ext("wdown", [F, D], F32)
        out = p.dram("out", [D, Tc], F32, kind="ExternalOutput")
        wmix_b = p.dram("wmix_b", [D, D], BF16)
        wup_b = p.dram("wup_b", [D, 2 * F], BF16)
        wdown_b = p.dram("wdown_b", [F, D], BF16)
        V = p.dram("V", [D, Tc], F32)
        X1 = p.dram("X1", [D, Tc], F32)
        H2 = p.dram("H2", [D, Tc], BF16)
        G = p.dram("G", [2 * F, Tc], F32)
        A = p.dram("A", [F, Tc], BF16)
        emit_cast(p, wmix_b, wmix, D, D)
        emit_cast(p, wup_b, wup, D, 2 * F)
        emit_cast(p, wdown_b, wdown, F, D)
        modl = load_tile(p, "modl", modl_d, [128, 4, nd])
        modc = load_tile(p, "modc", modc_d, [128, 4, nd])
        lnp = load_tile(p, "lnp", lnp_d, [128, 4, nd])
        bo = load_tile(p, "bo", bo_d, [128, nd])
        mask = load_tile(p, "mask", cmask, [128, Tc])
        ones = p.sbuf("ones", [128, 128], F32)
        p.op("vector", lambda e: e.memset(ones[:, :], 1.0), writes=[ones])
        sc1p_l = p.sbuf("sc1pl", [128, nd], F32)
        sc1p_c = p.sbuf("sc1pc", [128, nd], F32)
        p.op("vector", lambda e: e.tensor_scalar(sc1p_l[:, :], modl[:, 1, :], 1.0, None, ALU.add), reads=[modl], writes=[sc1p_l])
        p.op("vector", lambda e: e.tensor_scalar(sc1p_c[:, :], modc[:, 1, :], 1.0, None, ALU.add), reads=[modc], writes=[sc1p_c])
        meanB = p.sbuf("meanB", [128, Tc], F32)
        rstdB = p.sbuf("rstdB", [128, Tc], F32)

        def plane_tile(name, src, i):
            t = p.sbuf(name, [128, nd], F32)
            p.op("vector", lambda e: e.tensor_copy(t[:, :], src[:, i, :]), reads=[src], writes=[t])
            return t
        g1l, sh2l, g2l = plane_tile("g1l", modl, 0), plane_tile("sh2l", modl, 2), plane_tile("g2l", modl, 3)
        g1c, sh2c, g2c = plane_tile("g1c", modc, 0), plane_tile("sh2c", modc, 2), plane_tile("g2c", modc, 3)
        ln1g, ln1b, ln2g, ln2b = (plane_tile("ln%d" % i, lnp, i) for i in range(4))

        emit_resid_linear(p, zT, wmix_b, D, D, Tc, TL, xT, bo if has_bias else None, (g1l, g1c), V, meanB, rstdB, ones, eps, "r1", alpha)
        emit_ln_apply(p, V, D, Tc, TL, meanB, rstdB, ln1g, ln1b, X1, ((sc1p_l, sh2l), (sc1p_c, sh2c)), H2, mask, "l1")
        with phase(p):
            psums = Rot([p.psum("ups", [128, 512]) for _ in range(6)])
            gouts = Rot([p.sbuf("gout", [128, 512], F32) for _ in range(4)])
            cnt = [0]

            def epi_up(p, ps, cc, t0, n):
                ob = gouts.next()
                cnt[0] += 1
                if cnt[0] % 2 == 0:
                    p.op("scalar", lambda e: e.copy(ob[:, 0:n], ps[:, 0:n]), reads=[ps], writes=[ob])
                else:
                    p.op("vector", lambda e: e.tensor_copy(ob[:, 0:n], ps[:, 0:n]), reads=[ps], writes=[ob])
                p.dma("gpsimd", G.ap()[cc * 128:(cc + 1) * 128, t0:t0 + n], ob[:, 0:n], reads=[ob], writes=[G])
            emit_linear(p, H2, wup_b, D, 2 * F, Tc, epi_up, psums, TT=512, cb=256, tag="up")
        with phase(p):
            wdw = load_tile(p, "wdw", wdw_d, [128, nf, 9])
            bdw = load_tile(p, "bdw", bdw_d, [128, nf])
            mcl = load_tile(p, "mcl", mcl_d, [128, TL])
            mcr = load_tile(p, "mcr", mcr_d, [128, TL])
            emit_convglu(p, G, A, F, TO, CO, GW, wdw, bdw, mcl, mcr, has_ctx)
        emit_resid_linear(p, A, wdown_b, F, D, Tc, TL, X1, None, (g2l, g2c), V, meanB, rstdB, ones, eps, "r2", alpha)
        emit_ln_apply(p, V, D, Tc, TL, meanB, rstdB, ln2g, ln2b, out, None, None, None, "l2")
        p.emit()
    return nc


def build_ada(D, ncols, nlayers=2):
    nd = D // 128
    ncc = ncols // 128
    nc = bass.Bass("TRN2", target_bir_lowering=False)
    with contextlib.ExitStack() as st:
        p = Prog(nc, st)
        cond_d = p.dram("cond", [128, nd, 3], F32, kind="ExternalInput")
        ws = [p.dram("w%d" % l, [D, ncols], F32, kind="ExternalInput") for l in range(nlayers)]
        bs_d = p.dram("b", [128, nlayers, ncc], F32, kind="ExternalInput")
        out = p.dram("out", [128, nlayers, ncc, 3], F32, kind="ExternalOutput")
        cond = load_tile(p, "cond", cond_d, [128, nd, 3])
        bs = load_tile(p, "bs", bs_d, [128, nlayers, ncc])
        sT = p.sbuf("sT", [128, nd, 3], F32)
        p.op("scalar", lambda e: e.activation(out=sT[:, :, :], in_=cond[:, :, :], func=AF.Silu), reads=[cond], writes=[sT])
        ot = p.sbuf("ot", [128, nlayers, ncc, 3], F32)
        wts = Rot([p.sbuf("wt", [128, nd, 128], F32) for _ in range(3)])
        pss = Rot([p.psum("ps", [128, 512]) for _ in range(4)])
        for l in range(nlayers):
            wv = ws[l].ap().rearrange("(k p) c -> p k c", p=128)
            for cc in range(ncc):
                wt = wts.next()
                ps = pss.next()
                p.dma("sync", wt[:, :, :], wv[:, :, cc * 128:(cc + 1) * 128], reads=[ws[l]], writes=[wt])
                for k in range(nd):
                    p.op("tensor", lambda e, wt=wt, ps=ps, k=k: e.matmul(ps[:, 0:3], wt[:, k, :], sT[:, k, :], start=(k == 0), stop=(k == nd - 1)),
                         reads=[wt, sT], writes=[ps])
                p.op("vector", lambda e, ps=ps, l=l, cc=cc: e.tensor_scalar(ot[:, l, cc, :], ps[:, 0:3], bs[:, l, cc:cc + 1], None, ALU.add),
                     reads=[ps, bs], writes=[ot])
        p.dma("sync", out.ap()[:, :, :, :], ot[:, :, :, :], reads=[ot], writes=[out])
        p.emit()
    return nc


def dram_ap(buf, offset, pattern):
    return bass.AP(buf.t, offset, [list(x) for x in pattern])


def emit_sin_layer(p, ps, n, fp, li, out, tmp1, tmp2, pi2, K=64):
    hf = fp[0:K, li:li + 1]
    hfb = fp[0:K, 3 + li:4 + li]
    p.op("scalar", lambda e: e.activation(out=tmp1[0:K, 0:n], in_=ps[0:K, 0:n], func=AF.Sin, scale=hf, bias=hfb), reads=[ps, fp], writes=[tmp1])
    p.op("scalar", lambda e: e.activation(out=tmp2[0:K, 0:n], in_=ps[0:K, 0:n], func=AF.Abs, scale=hf, bias=hfb), reads=[ps, fp], writes=[tmp2])
    p.op("scalar", lambda e: e.activation(out=tmp2[0:K, 0:n], in_=tmp2[0:K, 0:n], func=AF.Sin, scale=-1.0, bias=pi2[0:K, 0:1]), reads=[tmp2, pi2], writes=[tmp2])
    p.op("vector", lambda e: e.scalar_tensor_tensor(out[0:K, 0:n], tmp1[0:K, 0:n], 2.0, tmp2[0:K, 0:n], ALU.mult, ALU.mult),
         reads=[tmp1, tmp2], writes=[out])


DBG = {}


def emit_hy_filters(p, Lx, zext, text, w1, w2, w3, fp, wout, negdelta, skip, Kf, CS, tag):
    W = 2 * Lx - 1
    ctr = Lx - 1
    ncc = CS // 128
    with phase(p):
        pi2 = p.sbuf(tag + "pi2", [128, 1], F32)
        p.op("vector", lambda e: e.memset(pi2[:, :], float(np.pi / 2)), writes=[pi2])
        zts = Rot([p.sbuf(tag + "z", [33, 512], F32) for _ in range(2)])
        tts = Rot([p.sbuf(tag + "t", [128, 512], F32) for _ in range(2)])
        hs = Rot([p.sbuf(tag + "h", [64, 512], F32) for _ in range(6)])
        t1s = Rot([p.sbuf(tag + "a", [64, 512], F32) for _ in range(2)])
        t2s = Rot([p.sbuf(tag + "b", [64, 512], F32) for _ in range(2)])
        pss = Rot([p.psum(tag + "ps", [128, 512]) for _ in range(4)])
        decs = Rot([p.sbuf(tag + "d", [128, 512], F32) for _ in range(2)])
        f32s = Rot([p.sbuf(tag + "f", [128, 512], F32) for _ in range(3)])
        fbs = Rot([p.sbuf(tag + "fb", [128, 512], BF16) for _ in range(3)])
        one1 = p.sbuf(tag + "one1", [1, 1], F32)
        p.op("vector", lambda e: e.memset(one1[:, :], 1.0), writes=[one1])
        c0 = 0
        while c0 < W:
            n = min(512, W - c0)
            zt, tt = zts.next(), tts.next()
            p.dma("sync", zt[:, 0:n], zext.ap()[:, c0:c0 + n], reads=[zext], writes=[zt])
            p.dma("sync", tt[:, 0:n], text.ap()[:, c0:c0 + n], reads=[text], writes=[tt])
            cur = zt
            Kdim = 33
            for li, wl in enumerate((w1, w2, w3)):
                ps = pss.next()
                p.op("tensor", lambda e, ps=ps, wl=wl, cur=cur, Kdim=Kdim, n=n: e.matmul(ps[0:64, 0:n], wl[0:Kdim, :], cur[0:Kdim, 0:n], start=True, stop=True),
                     reads=[wl, cur], writes=[ps])
                h = hs.next()
                emit_sin_layer(p, ps, n, fp, li, h, t1s.next(), t2s.next(), pi2)
                cur = h
                Kdim = 64
            h3 = cur
            has_c = (c0 <= ctr < c0 + n)
            for o in range(2):
                gl, gr = (0, 1) if o == 0 else (3, 2)
                for cc in range(ncc):
                    ps = pss.next()
                    a_end = min(n, max(0, ctr + 1 - c0))
                    if a_end > 0:
                        p.op("tensor", lambda e, ps=ps, gl=gl, cc=cc, a_end=a_end, h3=h3: e.matmul(ps[:, 0:a_end], wout[0:64, gl, cc * 128:(cc + 1) * 128], h3[0:64, 0:a_end], start=True, stop=True),
                             reads=[wout, h3], writes=[ps])
                    if a_end < n:
                        p.op("tensor", lambda e, ps=ps, gr=gr, cc=cc, a_end=a_end, h3=h3, n=n: e.matmul(ps[:, a_end:n], wout[0:64, gr, cc * 128:(cc + 1) * 128], h3[0:64, a_end:n], start=(a_end == 0), stop=True, skip_group_check=True),
                             reads=[wout, h3], writes=[ps])
                    if has_c:
                        ci = ctr - c0
                        p.op("tensor", lambda e, ps=ps, gr=gr, cc=cc, ci=ci, h3=h3: e.matmul(ps[:, ci:ci + 1], wout[0:64, gr, cc * 128:(cc + 1) * 128], h3[0:64, ci:ci + 1], start=False, stop=False, skip_group_check=True),
                             reads=[wout, h3], writes=[ps])
                        p.op("tensor", lambda e, ps=ps, o=o, cc=cc, ci=ci: e.matmul(ps[:, ci:ci + 1], skip[0:1, o, cc * 128:(cc + 1) * 128], one1[0:1, 0:1], start=False, stop=True, skip_group_check=True),
                             reads=[skip, one1], writes=[ps])
                    dec = decs.next()
                    p.op("scalar", lambda e, dec=dec, tt=tt, cc=cc, n=n: e.activation(out=dec[:, 0:n], in_=tt[:, 0:n], func=AF.Exp, scale=negdelta[:, cc:cc + 1]),
                         reads=[tt, negdelta], writes=[dec])
                    f32 = f32s.next()
                    p.op("vector", lambda e, f32=f32, ps=ps, dec=dec, n=n: e.tensor_tensor(f32[:, 0:n], ps[:, 0:n], dec[:, 0:n], ALU.mult), reads=[ps, dec], writes=[f32])
                    if DBG and tag == "fl" and c0 == 0 and o == 0 and cc == 0:
                        p.dma("sync", DBG["h3"].ap()[:, :], h3[:, :], reads=[h3], writes=[DBG["h3"]])
                        p.dma("sync", DBG["dec"].ap()[:, :], dec[:, :], reads=[dec], writes=[DBG["dec"]])
                        p.dma("sync", DBG["f32"].ap()[:, :], tt[:, :], reads=[tt], writes=[DBG["f32"]])
                    fb = fbs.next()
                    p.op("scalar", lambda e, fb=fb, f32=f32, n=n: e.copy(fb[:, 0:n], f32[:, 0:n]), reads=[f32], writes=[fb])
                    p.dma("gpsimd", Kf.ap()[o, cc * 128:(cc + 1) * 128, c0:c0 + n], fb[:, 0:n], reads=[fb], writes=[Kf])
            c0 += n


def emit_hy_conv(p, cfgh, U, wsh_d, Kf, Kfc, ident, antiid, zout, tag="hc"):
    B, L, Lc, CS = cfgh["B"], cfgh["L"], cfgh["Lc"], cfgh["CS"]
    WK, WKc = cfgh["WK"], cfgh["WKc"]
    nb, nbc = L // 128, Lc // 128
    NL = B * nb
    NCOL = NL + B * nbc
    Ttot = B * L + B * Lc
    WS = 2 * L - 1 - 127
    WSc = 2 * Lc - 1 - 127
    PIECE = min(2048, L)
    seqs = [(b * L, L) for b in range(B)] + [(B * L + b * Lc, Lc) for b in range(B)]
    with phase(p):
        VT = p.sbuf(tag + "VT", [128, 64, NCOL], BF16)
        XR = p.sbuf(tag + "XR", [128, 64, NCOL], BF16)
        X1T = p.sbuf(tag + "X1T", [128, 64, NCOL], BF16)
        X2T = p.sbuf(tag + "X2T", [128, 64, NCOL], BF16)
        Z1T = p.sbuf(tag + "Z1T", [128, 64, NCOL], BF16)
        Z2T = XR
        sks = Rot([p.sbuf(tag + "sk", [128, WS], BF16) for _ in range(2)])
        skcs = Rot([p.sbuf(tag + "skc", [128, WSc], BF16) for _ in range(2)])
        uts = Rot([p.sbuf(tag + "ut", [64, PIECE + 2], F32) for _ in range(2)])
        accs = Rot([p.sbuf(tag + "ac", [64, PIECE], F32) for _ in range(2)])
        cms = Rot([p.sbuf(tag + "cm", [64, PIECE], BF16) for _ in range(2)])
        wsh = Rot([p.sbuf(tag + "w", [64, 3, 4], F32) for _ in range(2)])
        pst = Rot([p.psum(tag + "pt", [128, 16, 64], BF16) for _ in range(2)])
        psf = Rot([p.psum(tag + "pf", [128, 512]) for _ in range(2)])
        psy = Rot([p.psum(tag + "py", [128, 512]) for _ in range(2)])
        pso = Rot([p.psum(tag + "po", [128, 512]) for _ in range(2)])
        obs = Rot([p.sbuf(tag + "ob", [64, 512], BF16) for _ in range(3)])
        for hg in range(CS // 64):
            wt = wsh.next()
            p.dma("sync", wt[:, :, :], wsh_d.ap()[hg, :, :, :], reads=[wsh_d], writes=[wt])
            for part, dst in ((2, VT), (0, XR), (1, X2T)):
                row0 = part * CS + hg * 64
                for (s0, Ls) in seqs:
                    for q0 in range(0, Ls, PIECE):
                        n = min(PIECE, Ls - q0)
                        ut, acc, cm = uts.next(), accs.next(), cms.next()
                        lo = q0 - 1
                        hi = q0 + n + 1
                        a, e_ = max(lo, 0), min(hi, Ls)
                        if lo < 0:
                            p.op("gpsimd", lambda e, ut=ut: e.memset(ut[:, 0:1], 0.0), writes=[ut])
                        if hi > Ls:
                            p.op("gpsimd", lambda e, ut=ut, n=n: e.memset(ut[:, n + 1:n + 2], 0.0), writes=[ut])
                        p.dma("sync", ut[:, a - lo:e_ - lo], U.ap()[row0:row0 + 64, s0 + a:s0 + e_], reads=[U], writes=[ut])
                        p.op("vector", lambda e, ut=ut, acc=acc, wt=wt, part=part, n=n: e.tensor_scalar(acc[:, 0:n], ut[:, 0:n], wt[:, part, 0:1], None, ALU.mult),
                             reads=[ut, wt], writes=[acc])
                        p.op("vector", lambda e, ut=ut, acc=acc, wt=wt, part=part, n=n: e.scalar_tensor_tensor(acc[:, 0:n], ut[:, 1:n + 1], wt[:, part, 1:2], acc[:, 0:n], ALU.mult, ALU.add),
                             reads=[ut, wt, acc], writes=[acc])
                        p.op("vector", lambda e, ut=ut, acc=acc, wt=wt, part=part, n=n: e.scalar_tensor_tensor(acc[:, 0:n], ut[:, 2:n + 2], wt[:, part, 2:3], acc[:, 0:n], ALU.mult, ALU.add),
                             reads=[ut, wt, acc], writes=[acc])
                        p.op("scalar", lambda e, acc=acc, cm=cm, wt=wt, part=part, n=n: e.activation(out=cm[:, 0:n], in_=acc[:, 0:n], func=AF.Identity, bias=wt[:, part, 3:4], scale=1.0),
                             reads=[acc, wt], writes=[cm])
                        col0 = (s0 + q0) // 128
                        nblk = n // 128
                        for g0 in range(0, nblk, 8):
                            ng = min(8, nblk - g0)
                            pt = pst.next()
                            for k in range(ng):
                                p.op("tensor", lambda e, pt=pt, cm=cm, k=k, g0=g0: e.transpose(pt[:, k, :], cm[:, (g0 + k) * 128:(g0 + k + 1) * 128], ident[0:64, 0:64]),
                                     reads=[cm, ident], writes=[pt])
                            c_a = col0 + g0
                            p.op("vector", lambda e, pt=pt, dst=dst, c_a=c_a, ng=ng: e.tensor_copy(dst[:, :, c_a:c_a + ng].rearrange("p c k -> p k c"), pt[:, 0:ng, :]),
                                 reads=[pt], writes=[dst])
            xrf = XR[:, :, :].rearrange("p c k -> p (c k)")
            x1f = X1T[:, :, :].rearrange("p c k -> p (c k)")
            tot = 64 * NCOL
            for f0 in range(0, tot, 512):
                n = min(512, tot - f0)
                ps = psf.next()
                p.op("tensor", lambda e, ps=ps, f0=f0, n=n: e.matmul(ps[:, 0:n], antiid[:, :], xrf[:, f0:f0 + n], start=True, stop=True), reads=[antiid, XR], writes=[ps])
                p.op("scalar", lambda e, ps=ps, f0=f0, n=n: e.copy(x1f[:, f0:f0 + n], ps[:, 0:n]), reads=[ps], writes=[X1T])
            for o, (src, gate, dstz) in enumerate(((VT, X1T, Z1T), (Z1T, X2T, Z2T))):
                sgn = -1 if o == 0 else 1
                for c in range(64):
                    ch = hg * 64 + c
                    sk, skc = sks.next(), skcs.next()
                    p.dma("sync", sk[:, :], dram_ap(Kf, (o * CS + ch) * WK, [[1, 128], [1, WS]]), reads=[Kf], writes=[sk])
                    p.dma("sync", skc[:, :], dram_ap(Kfc, (o * CS + ch) * WKc, [[1, 128], [1, WSc]]), reads=[Kfc], writes=[skc])
                    ps = psy.next()
                    dlist = [0] + [x for d in range(1, nb) for x in (d, -d)]
                    sv = src[:, c, 0:NL].rearrange("p (b j) -> p b j", b=B)
                    pv = ps[:, 0:NL].rearrange("p (b j) -> p b j", b=B)
                    for i_, d in enumerate(dlist):
                        j0, j1 = max(0, -d), min(nb, nb - d)
                        m0 = L - 128 + sgn * 128 * d
                        p.op("tensor", lambda e, sk=sk, m0=m0, sv=sv, pv=pv, j0=j0, j1=j1, d=d, i_=i_:
                             e.matmul(pv[:, :, j0 + d:j1 + d], sk[:, m0:m0 + 128], sv[:, :, j0:j1], start=(i_ == 0), stop=(i_ == len(dlist) - 1)),
                             reads=[sk, src], writes=[ps])
                    dlc = [0] + [x for d in range(1, nbc) for x in (d, -d)]
                    svc = src[:, c, NL:NCOL].rearrange("p (b j) -> p b j", b=B)
                    pvc = ps[:, NL:NCOL].rearrange("p (b j) -> p b j", b=B)
                    for i_, d in enumerate(dlc):
                        j0, j1 = max(0, -d), min(nbc, nbc - d)
                        m0 = Lc - 128 + sgn * 128 * d
                        p.op("tensor", lambda e, skc=skc, m0=m0, svc=svc, pvc=pvc, j0=j0, j1=j1, d=d, i_=i_:
                             e.matmul(pvc[:, :, j0 + d:j1 + d], skc[:, m0:m0 + 128], svc[:, :, j0:j1], start=(i_ == 0), stop=(i_ == len(dlc) - 1)),
                             reads=[skc, src], writes=[ps])
                    p.op("vector", lambda e, ps=ps, gate=gate, dstz=dstz, c=c: e.tensor_tensor(dstz[:, c, :], ps[:, 0:NCOL], gate[:, c, :], ALU.mult),
                         reads=[ps, gate], writes=[dstz])
            for g0 in range(0, NCOL, 4):
                ng = min(4, NCOL - g0)
                po = pso.next()
                for k in range(ng):
                    p.op("tensor", lambda e, po=po, k=k, g0=g0: e.matmul(po[0:64, k * 128:(k + 1) * 128], Z2T[:, :, g0 + k], ident[:, :], start=True, stop=True),
                         reads=[Z2T, ident], writes=[po])
                ob = obs.next()
                p.op("scalar", lambda e, po=po, ob=ob, ng=ng: e.copy(ob[:, 0:ng * 128], po[0:64, 0:ng * 128]), reads=[po], writes=[ob])
                p.dma("gpsimd", zout.ap()[hg * 64:(hg + 1) * 64, g0 * 128:(g0 + ng) * 128], ob[:, 0:ng * 128], reads=[ob], writes=[zout])


def build_hyfront(cfg):
    D, B, L, Lc, NC = cfg["D"], cfg["B"], cfg["L"], cfg["Lc"], cfg["NC"]
    CS = D // NC
    nd = D // 128
    Ttot = B * L + B * Lc
    WK = 2 * L
    WKc = 2 * Lc
    nc = bass.Bass("TRN2", target_bir_lowering=False)
    with contextlib.ExitStack() as st:
        p = Prog(nc, st)
        ext = lambda n, s, d: p.dram(n, s, d, kind="ExternalInput")
        xall = ext("xall", [D, Ttot], F32)
        modp_d = ext("modp", [128, 3, 2, nd], F32)
        win = ext("win", [D, 3 * CS], F32)
        bin_d = ext("bin", [128, 3 * CS // 128], F32)
        wsh_d = ext("wsh", [CS // 64, 64, 3, 4], F32)
        zext = ext("zext", [33, 2 * L], F32)
        zextc = ext("zextc", [33, 2 * Lc], F32)
        text = ext("text", [128, 2 * L], F32)
        textc = ext("textc", [128, 2 * Lc], F32)
        w1_d = ext("w1", [33, 64], F32)
        w2_d = ext("w2", [64, 64], F32)
        w3_d = ext("w3", [64, 64], F32)
        fp_d = ext("fp", [64, 8], F32)
        wout_d = ext("wout", [64, 4, CS], F32)
        ndl_d = ext("negdelta", [128, CS // 128], F32)
        skip_d = ext("skip", [1, 2, CS], F32)
        id_d = ext("ident", [128, 128], BF16)
        aid_d = ext("antiid", [128, 128], BF16)
        zout = p.dram("zout", [CS, Ttot], BF16, kind="ExternalOutput")
        H = p.dram("H", [D, Ttot], BF16)
        winb = p.dram("winb", [D, 3 * CS], BF16)
        U = p.dram("U", [3 * CS, Ttot], F32)
        dbg = cfg.get("dbg", False)
        Kf = p.dram("Kf", [2, CS, WK], BF16, kind="ExternalOutput" if dbg else "Internal")
        Kfc = p.dram("Kfc", [2, CS, WKc], BF16, kind="ExternalOutput" if dbg else "Internal")
        if dbg:
            DBG["h3"] = p.dram("dbg_h3", [64, 512], F32, kind="ExternalOutput")
            DBG["dec"] = p.dram("dbg_dec", [128, 512], F32, kind="ExternalOutput")
            DBG["f32"] = p.dram("dbg_f32", [128, 512], F32, kind="ExternalOutput")
        emit_cast(p, winb, win, D, 3 * CS)
        modp = load_tile(p, "modp", modp_d, [128, 3, 2, nd]) if False else None
        modp = p.sbuf("modp", [128, 3, 2, nd], F32)
        p.dma("sync", modp[:, :, :, :], modp_d.ap()[:, :, :, :], reads=[modp_d], writes=[modp])
        binT = load_tile(p, "bin", bin_d, [128, 3 * CS // 128])
        ident = load_tile(p, "ident", id_d, [128, 128], BF16)
        antiid = load_tile(p, "antiid", aid_d, [128, 128], BF16)
        with phase(p):
            xts = Rot([p.sbuf("b1x", [128, 512], F32) for _ in range(4)])
            hts = Rot([p.sbuf("b1h", [128, 512], BF16) for _ in range(4)])
            for t0 in range(0, Ttot, 512):
                n = min(512, Ttot - t0)
                cond = min(t0 // L, B) if t0 < B * L else B
                for k in range(nd):
                    xt, ht = xts.next(), hts.next()
                    p.dma("sync", xt[:, 0:n], xall.ap()[k * 128:(k + 1) * 128, t0:t0 + n], reads=[xall], writes=[xt])
                    p.op("scalar", lambda e, xt=xt, ht=ht, n=n, cond=cond, k=k: e.activation(out=ht[:, 0:n], in_=xt[:, 0:n], func=AF.Identity,
                                                                                             scale=modp[:, cond, 0, k:k + 1], bias=modp[:, cond, 1, k:k + 1]),
                         reads=[xt, modp], writes=[ht])
                    p.dma("gpsimd", H.ap()[k * 128:(k + 1) * 128, t0:t0 + n], ht[:, 0:n], reads=[ht], writes=[H])
        with phase(p):
            psums = Rot([p.psum("b2ps", [128, 512]) for _ in range(6)])
            outs = Rot([p.sbuf("b2o", [128, 512], F32) for _ in range(4)])

            def epi(p, ps, cc, t0, n):
                ob = outs.next()
                p.op("scalar", lambda e: e.activation(out=ob[:, 0:n], in_=ps[:, 0:n], func=AF.Identity, bias=binT[:, cc:cc + 1], scale=1.0),
                     reads=[ps, binT], writes=[ob])
                p.dma("gpsimd", U.ap()[cc * 128:(cc + 1) * 128, t0:t0 + n], ob[:, 0:n], reads=[ob], writes=[U])
            emit_linear(p, H, winb, D, 3 * CS, Ttot, epi, psums, TT=512, cb=256, tag="b2")
        with phase(p):
            w1 = load_tile(p, "w1", w1_d, [33, 64])
            w2 = load_tile(p, "w2", w2_d, [64, 64])
            w3 = load_tile(p, "w3", w3_d, [64, 64])
            fpr = load_tile(p, "fpr", fp_d, [64, 8])
            wout = load_tile(p, "wout", wout_d, [64, 4, CS])
            negdelta = load_tile(p, "ndl", ndl_d, [128, CS // 128])
            skip = load_tile(p, "skip", skip_d, [1, 2, CS])
            fp = p.sbuf("fp", [64, 8], F32)
            p.op("vector", lambda e: e.tensor_scalar(fp[:, 0:3], fpr[:, 0:3], 0.5, None, ALU.mult), reads=[fpr], writes=[fp])
            p.op("vector", lambda e: e.tensor_tensor(fp[:, 3:6], fp[:, 0:3], fpr[:, 3:6], ALU.mult), reads=[fp, fpr], writes=[fp])
            emit_hy_filters(p, L, zext, text, w1, w2, w3, fp, wout, negdelta, skip, Kf, CS, "fl")
            emit_hy_filters(p, Lc, zextc, textc, w1, w2, w3, fp, wout, negdelta, skip, Kfc, CS, "fc")
        cfgh = dict(B=B, L=L, Lc=Lc, CS=CS, WK=WK, WKc=WKc)
        emit_hy_conv(p, cfgh, U, wsh_d, Kf, Kfc, ident, antiid, zout)
        p.emit()
    return nc


def emit_modulate(p, xall, modp, Hout, D, Ttot, L, B, tag="md"):
    nd = D // 128
    with phase(p):
        xts = Rot([p.sbuf(tag + "x", [128, 512], F32) for _ in range(4)])
        hts = Rot([p.sbuf(tag + "h", [128, 512], BF16) for _ in range(4)])
        for t0 in range(0, Ttot, 512):
            n = min(512, Ttot - t0)
            cond = min(t0 // L, B) if t0 < B * L else B
            for k in range(nd):
                xt, ht = xts.next(), hts.next()
                p.dma("sync", xt[:, 0:n], xall.ap()[k * 128:(k + 1) * 128, t0:t0 + n], reads=[xall], writes=[xt])
                p.op("scalar", lambda e, xt=xt, ht=ht, n=n, cond=cond, k=k: e.activation(out=ht[:, 0:n], in_=xt[:, 0:n], func=AF.Identity,
                                                                                         scale=modp[:, cond, 0, k:k + 1], bias=modp[:, cond, 1, k:k + 1]),
                     reads=[xt, modp], writes=[ht])
                p.dma("gpsimd", Hout.ap()[k * 128:(k + 1) * 128, t0:t0 + n], ht[:, 0:n], reads=[ht], writes=[Hout])


def build_gdnfront(cfg):
    D, B, L, Lc, NC = cfg["D"], cfg["B"], cfg["L"], cfg["Lc"], cfg["NC"]
    rms_eps, l2_eps = cfg["rms_eps"], cfg["l2_eps"]
    H = D // 128
    HPC = H // NC
    CS = HPC * 128
    nd = D // 128
    Ttot = B * L + B * Lc
    NBLK = Ttot // 128
    NLB = B * L // 128
    nbl, nbc = L // 128, Lc // 128
    CO = 4 * CS + 128
    NAB = 4 * HPC
    nc = bass.Bass("TRN2", target_bir_lowering=False)
    with contextlib.ExitStack() as st:
        p = Prog(nc, st)
        ext = lambda n, s, d: p.dram(n, s, d, kind="ExternalInput")
        xall = ext("xall", [D, Ttot], F32)
        modp_d = ext("modp", [128, 3, 2, nd], F32)
        win = ext("win", [D, CO], F32)
        wcv_d = ext("wcv", [128, 3 * HPC, 3], F32)
        abc_d = ext("abc", [128, 2, 2 * HPC], F32)
        normw_d = ext("normw", [128, 128], F32)
        id_d = ext("ident", [128, 128], F32)
        msk_d = ext("masks", [128, 6, 128], F32)
        oout = p.dram("oout", [B * L, CS], BF16, kind="ExternalOutput")
        Hh = p.dram("H", [D, Ttot], BF16)
        winb = p.dram("winb", [D, CO], BF16)
        U = p.dram("U", [CO, Ttot], F32)
        QT = p.dram("QT", [CS, Ttot], F32)
        KT = p.dram("KT", [CS, Ttot], F32)
        KTM = p.dram("KTM", [Ttot, CS], F32)
        VTM = p.dram("VTM", [Ttot, CS], F32)
        OF = p.dram("OF", [B * L, CS], F32)
        OB = p.dram("OB", [B * L, CS], F32)
        emit_cast(p, winb, win, D, CO)
        modp = p.sbuf("modp", [128, 3, 2, nd], F32)
        p.dma("sync", modp[:, :, :, :], modp_d.ap()[:, :, :, :], reads=[modp_d], writes=[modp])
        ident = load_tile(p, "ident", id_d, [128, 128])
        masks = load_tile(p, "masks", msk_d, [128, 6, 128])
        ones = p.sbuf("ones", [128, 128], F32)
        p.op("vector", lambda e: e.memset(ones[:, :], 1.0), writes=[ones])
        Gtm = p.sbuf("Gtm", [128, NBLK, 2 * HPC], F32)
        Btm = p.sbuf("Btm", [128, NBLK, 2 * HPC], F32)
        emit_modulate(p, xall, modp, Hh, D, Ttot, L, B)
        with phase(p):
            psums = Rot([p.psum("g2ps", [128, 512]) for _ in range(6)])
            outs = Rot([p.sbuf("g2o", [128, 512], F32) for _ in range(4)])
            cnt = [0]

            def epi(p, ps, cc, t0, n):
                ob = outs.next()
                cnt[0] += 1
                if cnt[0] % 2 == 0:
                    p.op("scalar", lambda e: e.copy(ob[:, 0:n], ps[:, 0:n]), reads=[ps], writes=[ob])
                else:
                    p.op("vector", lambda e: e.tensor_copy(ob[:, 0:n], ps[:, 0:n]), reads=[ps], writes=[ob])
                p.dma("gpsimd", U.ap()[cc * 128:(cc + 1) * 128, t0:t0 + n], ob[:, 0:n], reads=[ob], writes=[U])
            emit_linear(p, Hh, winb, D, CO, Ttot, epi, psums, TT=512, cb=256, tag="g2")
        seqs = [(b * L, L) for b in range(B)] + [(B * L + b * Lc, Lc) for b in range(B)]
        with phase(p):
            wcv = load_tile(p, "wcv", wcv_d, [128, 3 * HPC, 3])
            PC = 512
            uts = Rot([p.sbuf("g3u", [128, PC + 2], F32) for _ in range(3)])
            accs = Rot([p.sbuf("g3a", [128, PC], F32) for _ in range(3)])
            sls = Rot([p.sbuf("g3s", [128, PC], F32) for _ in range(3)])
            sqs = Rot([p.sbuf("g3q", [128, PC], F32) for _ in range(2)])
            rss = Rot([p.sbuf("g3r", [128, PC], F32) for _ in range(2)])
            nrm = Rot([p.sbuf("g3n", [128, PC], F32) for _ in range(3)])
            tms = Rot([p.sbuf("g3t", [128, PC], F32) for _ in range(3)])
            pss = Rot([p.psum("g3ps", [128, 512]) for _ in range(3)])
            pst = Rot([p.psum("g3pt", [128, 512]) for _ in range(3)])
            for part in range(3):
                for hl in range(HPC):
                    ch = part * HPC + hl
                    row0 = part * CS + hl * 128
                    for (s0, Ls) in seqs:
                        for q0 in range(0, Ls, PC):
                            n = min(PC, Ls - q0)
                            ut, acc, sl = uts.next(), accs.next(), sls.next()
                            lo, hi = q0 - 1, q0 + n + 1
                            a, e_ = max(lo, 0), min(hi, Ls)
                            if lo < 0:
                                p.op("gpsimd", lambda e, ut=ut: e.memset(ut[:, 0:1], 0.0), writes=[ut])
                            if hi > Ls:
                                p.op("gpsimd", lambda e, ut=ut, n=n: e.memset(ut[:, n + 1:n + 2], 0.0), writes=[ut])
                            p.dma("sync", ut[:, a - lo:e_ - lo], U.ap()[row0:row0 + 128, s0 + a:s0 + e_], reads=[U], writes=[ut])
                            p.op("vector", lambda e, ut=ut, acc=acc, ch=ch, n=n: e.tensor_scalar(acc[:, 0:n], ut[:, 0:n], wcv[:, ch, 0:1], None, ALU.mult),
                                 reads=[ut, wcv], writes=[acc])
                            p.op("vector", lambda e, ut=ut, acc=acc, ch=ch, n=n: e.scalar_tensor_tensor(acc[:, 0:n], ut[:, 1:n + 1], wcv[:, ch, 1:2], acc[:, 0:n], ALU.mult, ALU.add),
                                 reads=[ut, wcv, acc], writes=[acc])
                            p.op("vector", lambda e, ut=ut, acc=acc, ch=ch, n=n: e.scalar_tensor_tensor(acc[:, 0:n], ut[:, 2:n + 2], wcv[:, ch, 2:3], acc[:, 0:n], ALU.mult, ALU.add),
                                 reads=[ut, wcv, acc], writes=[acc])
                            p.op("scalar", lambda e, acc=acc, sl=sl, n=n: e.activation(out=sl[:, 0:n], in_=acc[:, 0:n], func=AF.Silu), reads=[acc], writes=[sl])
                            cur = sl
                            if part < 2:
                                sq, rs, nr = sqs.next(), rss.next(), nrm.next()
                                ps = pss.next()
                                p.op("scalar", lambda e, sl=sl, sq=sq, n=n: e.activation(out=sq[:, 0:n], in_=sl[:, 0:n], func=AF.Square), reads=[sl], writes=[sq])
                                p.op("tensor", lambda e, ps=ps, sq=sq, n=n: e.matmul(ps[:, 0:n], ones[:, :], sq[:, 0:n], start=True, stop=True), reads=[ones, sq], writes=[ps])
                                p.op("vector", lambda e, ps=ps, rs=rs, n=n: e.tensor_scalar(rs[:, 0:n], ps[:, 0:n], float(l2_eps), None, ALU.add), reads=[ps], writes=[rs])
                                p.op("scalar", lambda e, rs=rs, n=n: e.activation(out=rs[:, 0:n], in_=rs[:, 0:n], func=AF.Sqrt), reads=[rs], writes=[rs])
                                p.op("vector", lambda e, rs=rs, n=n: e.reciprocal(rs[:, 0:n], rs[:, 0:n]), reads=[rs], writes=[rs])
                                sc = 128 ** -0.5 if part == 0 else 1.0
                                p.op("vector", lambda e, sl=sl, rs=rs, nr=nr, n=n, sc=sc: e.scalar_tensor_tensor(nr[:, 0:n], sl[:, 0:n], float(sc), rs[:, 0:n], ALU.mult, ALU.mult),
                                     reads=[sl, rs], writes=[nr])
                                dst = QT if part == 0 else KT
                                p.dma("gpsimd", dst.ap()[hl * 128:(hl + 1) * 128, s0 + q0:s0 + q0 + n], nr[:, 0:n], reads=[nr], writes=[dst])
                                cur = nr
                            if part >= 1:
                                pt = pst.next()
                                tm = tms.next()
                                nb_ = n // 128
                                for k in range(nb_):
                                    p.op("tensor", lambda e, pt=pt, cur=cur, k=k: e.transpose(pt[:, k * 128:(k + 1) * 128], cur[:, k * 128:(k + 1) * 128], ident[:, :]),
                                         reads=[cur, ident], writes=[pt])
                                p.op("scalar", lambda e, pt=pt, tm=tm, n=n: e.copy(tm[:, 0:n], pt[:, 0:n]), reads=[pt], writes=[tm])
                                dstm = KTM if part == 1 else VTM
                                blk0 = (s0 + q0) // 128
                                p.dma("gpsimd", dstm.ap()[blk0 * 128:(blk0 + nb_) * 128, hl * 128:(hl + 1) * 128].rearrange("(k t) c -> t k c", t=128),
                                      tm[:, 0:n].rearrange("t (k c) -> t k c", c=128), reads=[tm], writes=[dstm])
        with phase(p):
            abc = load_tile(p, "abc", abc_d, [128, 2, 2 * HPC])
            nexpA = p.sbuf("nexpA", [128, 2 * HPC], F32)
            p.op("scalar", lambda e: e.activation(out=nexpA[:, :], in_=abc[:, 1, :], func=AF.Exp), reads=[abc], writes=[nexpA])
            p.op("vector", lambda e: e.tensor_scalar(nexpA[:, :], nexpA[:, :], -1.0, None, ALU.mult), reads=[nexpA], writes=[nexpA])
            abt = Rot([p.sbuf("g4a", [NAB, 2048], F32) for _ in range(2)])
            pts = Rot([p.psum("g4p", [128, 512]) for _ in range(2)])
            tmp = Rot([p.sbuf("g4t", [128, 2 * HPC], F32) for _ in range(15)])
            for t0 in range(0, Ttot, 2048):
                n = min(2048, Ttot - t0)
                at = abt.next()
                p.dma("sync", at[:, 0:n], U.ap()[4 * CS:4 * CS + NAB, t0:t0 + n], reads=[U], writes=[at])
                for k in range(n // 128):
                    blk = t0 // 128 + k
                    pt = pts.next()
                    tt = tmp.next()
                    p.op("tensor", lambda e, pt=pt, at=at, k=k: e.transpose(pt[:, 0:NAB], at[:, k * 128:(k + 1) * 128], ident[0:NAB, 0:NAB]), reads=[at, ident], writes=[pt])
                    p.op("vector", lambda e, pt=pt, tt=tt: e.tensor_tensor(tt[:, :], pt[:, 0:2 * HPC], abc[:, 0, :], ALU.add), reads=[pt, abc], writes=[tt])
                    sa, sb, sc_, sd = tmp.next(), tmp.next(), tmp.next(), tmp.next()
                    p.op("scalar", lambda e, tt=tt: e.activation(out=tt[:, :], in_=tt[:, :], func=AF.Exp), reads=[tt], writes=[tt])
                    p.op("scalar", lambda e, tt=tt, sd=sd: e.activation(out=sd[:, :], in_=tt[:, :], func=AF.Ln, bias=1.0, scale=1.0), reads=[tt], writes=[sd])
                    p.op("vector", lambda e, tt=tt, sa=sa: e.tensor_scalar(sa[:, :], tt[:, :], 2.0, None, ALU.add), reads=[tt], writes=[sa])
                    p.op("vector", lambda e, sa=sa: e.reciprocal(sa[:, :], sa[:, :]), reads=[sa], writes=[sa])
                    p.op("vector", lambda e, tt=tt, sa=sa: e.tensor_tensor(sa[:, :], tt[:, :], sa[:, :], ALU.mult), reads=[tt, sa], writes=[sa])
                    p.op("vector", lambda e, sa=sa, sb=sb: e.tensor_tensor(sb[:, :], sa[:, :], sa[:, :], ALU.mult), reads=[sa], writes=[sb])
                    p.op("vector", lambda e, sb=sb, sc_=sc_: e.tensor_scalar(sc_[:, :], sb[:, :], 1.0 / 15, None, ALU.mult), reads=[sb], writes=[sc_])
                    for cf in (1.0 / 13, 1.0 / 11, 1.0 / 9, 1.0 / 7, 1.0 / 5, 1.0 / 3):
                        p.op("vector", lambda e, sb=sb, sc_=sc_, cf=cf: e.scalar_tensor_tensor(sc_[:, :], sc_[:, :], float(cf), sb[:, :], ALU.add, ALU.mult), reads=[sb, sc_], writes=[sc_])
                    p.op("vector", lambda e, sa=sa, sc_=sc_: e.scalar_tensor_tensor(sc_[:, :], sc_[:, :], 1.0, sa[:, :], ALU.add, ALU.mult), reads=[sa, sc_], writes=[sc_])
                    p.op("vector", lambda e, sc_=sc_: e.tensor_scalar(sc_[:, :], sc_[:, :], 2.0, None, ALU.mult), reads=[sc_], writes=[sc_])
                    p.op("vector", lambda e, tt=tt, sb=sb: e.tensor_scalar(sb[:, :], tt[:, :], 1.0, None, ALU.is_le), reads=[tt], writes=[sb])
                    p.op("vector", lambda e, sc_=sc_, sd=sd: e.tensor_tensor(sc_[:, :], sc_[:, :], sd[:, :], ALU.subtract), reads=[sc_, sd], writes=[sc_])
                    p.op("vector", lambda e, sc_=sc_, sb=sb: e.tensor_tensor(sc_[:, :], sc_[:, :], sb[:, :], ALU.mult), reads=[sc_, sb], writes=[sc_])
                    p.op("vector", lambda e, sc_=sc_, sd=sd: e.tensor_tensor(sc_[:, :], sc_[:, :], sd[:, :], ALU.add), reads=[sc_, sd], writes=[sc_])
                    p.op("vector", lambda e, sc_=sc_, blk=blk: e.tensor_tensor(Gtm[:, blk, :], sc_[:, :], nexpA[:, :], ALU.mult), reads=[sc_, nexpA], writes=[Gtm])
                    p.op("scalar", lambda e, pt=pt, blk=blk: e.activation(out=Btm[:, blk, :], in_=pt[:, 2 * HPC:4 * HPC], func=AF.Sigmoid), reads=[pt], writes=[Btm])
        emit_gdn_scan(p, cfg, QT, KT, KTM, VTM, Gtm, Btm, OF, OB, ident, masks, ones)
        emit_gdn_out(p, cfg, U, OF, OB, normw_d, ident, oout)
        p.emit()
    return nc


def emit_gdn_scan(p, cfg, QT, KT, KTM, VTM, Gtm, Btm, OF, OB, ident, masks, ones):
    D, B, L, Lc, NC = cfg["D"], cfg["B"], cfg["L"], cfg["Lc"], cfg["NC"]
    H = D // 128
    HPC = H // NC
    nbl, nbc = L // 128, Lc // 128
    T = lambda: [128, 128]
    with phase(p):
        def pool(name, k, shape=(128, 128)):
            return Rot([p.sbuf(name, list(shape), F32) for _ in range(k)])
        qTs, kTs, ktms, vtms = pool("sq", 8), pool("sk", 8), pool("skt", 8), pool("sv", 8)
        gcolBs, gcBs, colss = pool("sgb", 8), pool("sgc", 8), pool("scol", 8, (128, 8))
        Es, EMs, attns, bcolBs, EMSs, Ns, NTs, Ps = pool("sE", 8), pool("sEM", 8), pool("sat", 8), pool("sbb", 8), pool("sEMS", 8), pool("sN", 16), pool("sNT", 8), pool("sP", 24)
        Ms, Mts = pool("sM", 24), pool("sMt", 32)
        vbs, kbgs, us, wTs, egBs, qgTs, kgs, vnews, os_ = pool("svb", 8), pool("skbg", 8), pool("su", 8), pool("swT", 8), pool("seg", 8), pool("sqg", 8), pool("skg", 8), pool("svn", 8), pool("so", 8)
        banks = Rot([p.psum("sps", [128, 512]) for _ in range(8)])
        chains = [(b, hl, d) for b in range(B) for hl in range(HPC) for d in range(2)]
        S = {}
        for c in chains:
            S[c] = p.sbuf("S", [128, 128], F32)
            p.op("gpsimd", lambda e, s=S[c]: e.memset(s[:, :], 0.0), writes=[S[c]])
        nsteps = nbc + nbl
        cpy = [0]

        def evac(dst, ps, w=128):
            cpy[0] += 1
            if cpy[0] % 3 == 0:
                p.op("vector", lambda e: e.tensor_copy(dst[:, 0:w], ps[:, 0:w]), reads=[ps], writes=[dst])
            else:
                p.op("scalar", lambda e: e.copy(dst[:, 0:w], ps[:, 0:w]), reads=[ps], writes=[dst])

        def mm(ps, lhsT_buf, lhsT_ap, rhs_buf, rhs_ap, w=128, start=True, stop=True, off=0):
            p.op("tensor", lambda e: e.matmul(ps[:, off:off + w], lhsT_ap, rhs_ap, start=start, stop=stop), reads=[lhsT_buf, rhs_buf], writes=[ps])

        def chain_step(b, hl, d, step):
            if True:
                is_ctx = step < nbc
                if is_ctx:
                    k = step if d == 0 else nbc - 1 - step
                    blk = (B * L + b * Lc) // 128 + k
                else:
                    k = (step - nbc) if d == 0 else nbl - 1 - (step - nbc)
                    blk = (b * L) // 128 + k
                Mi = masks[:, 2 * d, :]
                nMs = masks[:, 2 * d + 1, :]
                last = 127 if d == 0 else 0
                Sb = S[(b, hl, d)]
                gi = d * HPC + hl
                qT, kT, ktm, vtm = qTs.next(), kTs.next(), ktms.next(), vtms.next()
                p.dma("sync", qT[:, :], QT.ap()[hl * 128:(hl + 1) * 128, blk * 128:(blk + 1) * 128], reads=[QT], writes=[qT])
                p.dma("sync", kT[:, :], KT.ap()[hl * 128:(hl + 1) * 128, blk * 128:(blk + 1) * 128], reads=[KT], writes=[kT])
                p.dma("sync", ktm[:, :], KTM.ap()[blk * 128:(blk + 1) * 128, hl * 128:(hl + 1) * 128], reads=[KTM], writes=[ktm])
                p.dma("sync", vtm[:, :], VTM.ap()[blk * 128:(blk + 1) * 128, hl * 128:(hl + 1) * 128], reads=[VTM], writes=[vtm])
                gcolB, gcB, cols = gcolBs.next(), gcBs.next(), colss.next()
                p.op("vector", lambda e, gcolB=gcolB, blk=blk, gi=gi: e.tensor_scalar(gcolB[:, :], ones[:, :], Gtm[:, blk, gi:gi + 1], None, ALU.mult), reads=[ones, Gtm], writes=[gcolB])
                ps1 = banks.next()
                mm(ps1, gcolB, gcolB[:, :], masks, Mi)
                mm(ps1, masks, Mi, Gtm, Gtm[:, blk, gi:gi + 1], w=1, off=128)
                p.op("scalar", lambda e, gcB=gcB, ps1=ps1: e.copy(gcB[:, :], ps1[:, 0:128]), reads=[ps1], writes=[gcB])
                p.op("scalar", lambda e, cols=cols, ps1=ps1: e.copy(cols[:, 0:1], ps1[:, 128:129]), reads=[ps1], writes=[cols])
                p.op("scalar", lambda e, cols=cols: e.activation(out=cols[:, 1:2], in_=cols[:, 0:1], func=AF.Exp), reads=[cols], writes=[cols])
                p.op("scalar", lambda e, cols=cols, gcB=gcB, last=last: e.activation(out=cols[:, 2:3], in_=cols[:, 0:1], func=AF.Exp, scale=-1.0, bias=gcB[:, last:last + 1]),
                     reads=[cols, gcB], writes=[cols])
                p.op("scalar", lambda e, cols=cols, gcB=gcB, last=last: e.activation(out=cols[:, 3:4], in_=gcB[:, last:last + 1], func=AF.Exp), reads=[gcB], writes=[cols])
                p.op("vector", lambda e, cols=cols, blk=blk, gi=gi: e.tensor_tensor(cols[:, 4:5], cols[:, 1:2], Btm[:, blk, gi:gi + 1], ALU.mult), reads=[cols, Btm], writes=[cols])
                yield
                E, EM = Es.next(), EMs.next()
                p.op("vector", lambda e, E=E, gcB=gcB, cols=cols: e.tensor_scalar(E[:, :], gcB[:, :], cols[:, 0:1], 0.0, ALU.subtract, ALU.min), reads=[gcB, cols], writes=[E])
                p.op("scalar", lambda e, E=E: e.activation(out=E[:, :], in_=E[:, :], func=AF.Exp), reads=[E], writes=[E])
                p.op("vector", lambda e, E=E, EM=EM, Mi=Mi: e.tensor_tensor(EM[:, :], E[:, :], Mi, ALU.mult), reads=[E, masks], writes=[EM])
                yield
                psA = banks.next()
                mm(psA, kT, kT[:, :], kT, kT[:, :])
                mm(psA, kT, kT[:, :], qT, qT[:, :], off=128)
                attn = attns.next()
                if not is_ctx:
                    p.op("vector", lambda e, attn=attn, psA=psA, EM=EM: e.tensor_tensor(attn[:, :], psA[:, 128:256], EM[:, :], ALU.mult), reads=[psA, EM], writes=[attn])
                bcolB = bcolBs.next()
                p.op("vector", lambda e, bcolB=bcolB, blk=blk, gi=gi: e.tensor_scalar(bcolB[:, :], ones[:, :], Btm[:, blk, gi:gi + 1], None, ALU.mult), reads=[ones, Btm], writes=[bcolB])
                ps3 = banks.next()
                mm(ps3, bcolB, bcolB[:, :], ident, ident[:, :])
                EMS, Nn = EMSs.next(), Ns.next()
                p.op("gpsimd", lambda e, EMS=EMS, EM=EM, nMs=nMs: e.tensor_tensor(EMS[:, :], EM[:, :], nMs, ALU.mult), reads=[EM, masks], writes=[EMS])
                p.op("vector", lambda e, EMS=EMS, ps3=ps3: e.tensor_tensor(EMS[:, :], EMS[:, :], ps3[:, 0:128], ALU.mult), reads=[EMS, ps3], writes=[EMS])
                p.op("vector", lambda e, Nn=Nn, psA=psA, EMS=EMS: e.tensor_tensor(Nn[:, :], psA[:, 0:128], EMS[:, :], ALU.mult), reads=[psA, EMS], writes=[Nn])
                yield
                ps4 = banks.next()
                NT = NTs.next()
                p.op("tensor", lambda e, ps4=ps4, Nn=Nn: e.transpose(ps4[:, 0:128], Nn[:, :], ident[:, :]), reads=[Nn, ident], writes=[ps4])
                evac(NT, ps4)
                yield
                BD = masks[:, 4, :]
                ODm = masks[:, 5, :]
                Nd, NTd, No = Ms.next(), Mts.next(), Ns.next()
                p.op("gpsimd", lambda e, Nd=Nd, Nn=Nn, BD=BD: e.tensor_tensor(Nd[:, :], Nn[:, :], BD, ALU.mult), reads=[Nn, masks], writes=[Nd])
                p.op("gpsimd", lambda e, NTd=NTd, NT=NT, BD=BD: e.tensor_tensor(NTd[:, :], NT[:, :], BD, ALU.mult), reads=[NT, masks], writes=[NTd])
                p.op("gpsimd", lambda e, No=No, Nn=Nn, ODm=ODm: e.tensor_tensor(No[:, :], Nn[:, :], ODm, ALU.mult), reads=[Nn, masks], writes=[No])
                P = Ps.next()
                p.op("gpsimd", lambda e, P=P, Nd=Nd: e.tensor_tensor(P[:, :], Nd[:, :], ident[:, :], ALU.add), reads=[Nd, ident], writes=[P])
                M, Mt = Nd, NTd
                for lev in range(1, 5):
                    lastlev = (lev == 4)
                    M2 = None
                    if not lastlev:
                        psM = banks.next()
                        mm(psM, Mt, Mt[:, :], M, M[:, :])
                        M2 = Ms.next()
                        evac(M2, psM)
                        yield
                    psMt = banks.next()
                    mm(psMt, M, M[:, :], Mt, Mt[:, :])
                    Mt2 = Mts.next()
                    evac(Mt2, psMt)
                    yield
                    psP = banks.next()
                    mm(psP, Mt2, Mt2[:, :], P, P[:, :])
                    p.op("vector", lambda e, P=P, psP=psP: e.tensor_tensor(P[:, :], P[:, :], psP[:, 0:128], ALU.add), reads=[P, psP], writes=[P])
                    yield
                    M, Mt = M2, Mt2
                Dinv = P
                psD = banks.next()
                DinvT = Mts.next()
                p.op("tensor", lambda e, psD=psD, Dinv=Dinv: e.transpose(psD[:, 0:128], Dinv[:, :], ident[:, :]), reads=[Dinv, ident], writes=[psD])
                evac(DinvT, psD)
                yield
                psG = banks.next()
                GT = Ms.next()
                mm(psG, No, No[:, :], DinvT, DinvT[:, :])
                evac(GT, psG)
                yield
                psY = banks.next()
                Y1 = Mts.next()
                mm(psY, GT, GT[:, :], Dinv, Dinv[:, :])
                evac(Y1, psY)
                yield
                psY2 = banks.next()
                mm(psY2, GT, GT[:, :], Y1, Y1[:, :])
                X1 = Ps.next()
                p.op("vector", lambda e, X1=X1, Dinv=Dinv, psY2=psY2: e.tensor_tensor(X1[:, :], Dinv[:, :], psY2[:, 0:128], ALU.add), reads=[Dinv, psY2], writes=[X1])
                yield
                psY3 = banks.next()
                mm(psY3, GT, GT[:, :], X1, X1[:, :])
                P = Ps.next()
                p.op("vector", lambda e, P=P, X1=X1, psY3=psY3: e.tensor_tensor(P[:, :], X1[:, :], psY3[:, 0:128], ALU.add), reads=[X1, psY3], writes=[P])
                yield
                vb, kbg = vbs.next(), kbgs.next()
                p.op("scalar", lambda e, vb=vb, vtm=vtm, blk=blk, gi=gi: e.activation(out=vb[:, :], in_=vtm[:, :], func=AF.Identity, scale=Btm[:, blk, gi:gi + 1], bias=0.0), reads=[vtm, Btm], writes=[vb])
                p.op("scalar", lambda e, kbg=kbg, ktm=ktm, cols=cols: e.activation(out=kbg[:, :], in_=ktm[:, :], func=AF.Identity, scale=cols[:, 4:5], bias=0.0), reads=[ktm, cols], writes=[kbg])
                psU = banks.next()
                mm(psU, P, P[:, :], vb, vb[:, :])
                mm(psU, kbg, kbg[:, :], P, P[:, :], off=128)
                u, wT = us.next(), wTs.next()
                evac(u, psU)
                p.op("scalar", lambda e, wT=wT, psU=psU: e.copy(wT[:, :], psU[:, 128:256]), reads=[psU], writes=[wT])
                yield
                kg = kgs.next()
                p.op("scalar", lambda e, kg=kg, ktm=ktm, cols=cols: e.activation(out=kg[:, :], in_=ktm[:, :], func=AF.Identity, scale=cols[:, 2:3], bias=0.0), reads=[ktm, cols], writes=[kg])
                psS = banks.next()
                mm(psS, wT, wT[:, :], Sb, Sb[:, :])
                vnew = vnews.next()
                p.op("vector", lambda e, vnew=vnew, u=u, psS=psS: e.tensor_tensor(vnew[:, :], u[:, :], psS[:, 0:128], ALU.subtract), reads=[u, psS], writes=[vnew])
                yield
                if not is_ctx:
                    egB, qgT = egBs.next(), qgTs.next()
                    p.op("scalar", lambda e, egB=egB, gcB=gcB: e.activation(out=egB[:, :], in_=gcB[:, :], func=AF.Exp), reads=[gcB], writes=[egB])
                    p.op("gpsimd", lambda e, qgT=qgT, qT=qT, egB=egB: e.tensor_tensor(qgT[:, :], qT[:, :], egB[:, :], ALU.mult), reads=[qT, egB], writes=[qgT])
                    psO = banks.next()
                    mm(psO, qgT, qgT[:, :], Sb, Sb[:, :], start=True, stop=False)
                    mm(psO, attn, attn[:, :], vnew, vnew[:, :], start=False, stop=True)
                    o = os_.next()
                    evac(o, psO)
                    yield
                    dst = OF if d == 0 else OB
                    r0 = blk * 128
                    p.dma("gpsimd", dst.ap()[r0:r0 + 128, hl * 128:(hl + 1) * 128], o[:, :], reads=[o], writes=[dst])
                psK = banks.next()
                mm(psK, kg, kg[:, :], vnew, vnew[:, :])
                p.op("vector", lambda e, Sb=Sb, cols=cols, psK=psK: e.scalar_tensor_tensor(Sb[:, :], Sb[:, :], cols[:, 3:4], psK[:, 0:128], ALU.mult, ALU.add),
                     reads=[Sb, cols, psK], writes=[Sb])
        WAVE = 8
        for step in range(nsteps):
            for w0 in range(0, len(chains), WAVE):
                gens = [chain_step(b, hl, d, step) for (b, hl, d) in chains[w0:w0 + WAVE]]
                while gens:
                    for g_ in list(gens):
                        try:
                            next(g_)
                        except StopIteration:
                            gens.remove(g_)


def emit_gdn_out(p, cfg, U, OF, OB, normw_d, ident, oout):
    D, B, L, Lc, NC = cfg["D"], cfg["B"], cfg["L"], cfg["Lc"], cfg["NC"]
    rms_eps = cfg["rms_eps"]
    H = D // 128
    HPC = H // NC
    CS = HPC * 128
    with phase(p):
        normw = load_tile(p, "normw", normw_d, [128, 128])
        pool = lambda name, k, shape=(128, 128): Rot([p.sbuf(name, list(shape), F32) for _ in range(k)])
        ofs, obs, zs, szs, sqs, ys, sss = pool("oof", 3), pool("oob", 3), pool("oz", 3), pool("osz", 3), pool("osq", 2), pool("oy", 3), pool("oss", 3, (128, 2))
        pts = Rot([p.psum("opt", [128, 512]) for _ in range(3)])
        ybs = Rot([p.sbuf("oyb", [128, 128], BF16) for _ in range(3)])
        for blk in range(B * L // 128):
            for hl in range(HPC):
                of, ob, z, sz, sq, y, ss = ofs.next(), obs.next(), zs.next(), szs.next(), sqs.next(), ys.next(), sss.next()
                r0 = blk * 128
                p.dma("sync", of[:, :], OF.ap()[r0:r0 + 128, hl * 128:(hl + 1) * 128], reads=[OF], writes=[of])
                p.dma("sync", ob[:, :], OB.ap()[r0:r0 + 128, hl * 128:(hl + 1) * 128], reads=[OB], writes=[ob])
                p.dma("sync", z[:, :], U.ap()[3 * CS + hl * 128:3 * CS + (hl + 1) * 128, r0:r0 + 128], reads=[U], writes=[z])
                pt = pts.next()
                p.op("tensor", lambda e, pt=pt, z=z: e.transpose(pt[:, 0:128], z[:, :], ident[:, :]), reads=[z, ident], writes=[pt])
                p.op("scalar", lambda e, pt=pt, sz=sz: e.activation(out=sz[:, :], in_=pt[:, 0:128], func=AF.Silu), reads=[pt], writes=[sz])
                p.op("vector", lambda e, of=of, ob=ob: e.tensor_tensor(of[:, :], of[:, :], ob[:, :], ALU.add), reads=[of, ob], writes=[of])
                p.op("scalar", lambda e, of=of, sq=sq, ss=ss: e.activation(out=sq[:, :], in_=of[:, :], func=AF.Square, accum_out=ss[:, 0:1]), reads=[of], writes=[sq, ss])
                p.op("vector", lambda e, ss=ss: e.tensor_scalar(ss[:, 1:2], ss[:, 0:1], 1.0 / 128, float(rms_eps), ALU.mult, ALU.add), reads=[ss], writes=[ss])
                p.op("scalar", lambda e, ss=ss: e.activation(out=ss[:, 1:2], in_=ss[:, 1:2], func=AF.Sqrt), reads=[ss], writes=[ss])
                p.op("vector", lambda e, ss=ss: e.reciprocal(ss[:, 1:2], ss[:, 1:2]), reads=[ss], writes=[ss])
                p.op("vector", lambda e, y=y, of=of, ss=ss: e.scalar_tensor_tensor(y[:, :], of[:, :], ss[:, 1:2], normw[:, :], ALU.mult, ALU.mult), reads=[of, ss, normw], writes=[y])
                yb = ybs.next()
                p.op("gpsimd", lambda e, y=y, sz=sz, yb=yb: e.tensor_tensor(yb[:, :], y[:, :], sz[:, :], ALU.mult), reads=[y, sz], writes=[yb])
                p.dma("gpsimd", oout.ap()[r0:r0 + 128, hl * 128:(hl + 1) * 128], yb[:, :], reads=[yb], writes=[oout])


import ml_dtypes
BF = ml_dtypes.bfloat16

def ptile(v, nd=None):
    v = np.asarray(v, np.float32)
    return np.ascontiguousarray(v.reshape(-1, 128).T)

def back_inputs(cfg, j, has_ctx, zT_lat, xT_lat, zT_ctx, xT_ctx, mods, lnp, b_out, wmix, wup, wdw, bdw, wdown):
    D, F, TO, CO, GW, B, L, Lc, NC = (cfg[k] for k in ("D", "F", "TO", "CO", "GW", "B", "L", "Lc", "NC"))
    HL = GW + 1
    TL = TO + 2 * HL
    Tc = TL + (CO + 2 if has_ctx else 0)
    cpb = NC // B
    b, r = j // cpb, j % cpb
    zT = np.zeros((D, Tc), BF)
    xT = np.zeros((D, Tc), np.float32)
    cmask = np.zeros((128, Tc), np.float32)
    lo = r * TO - HL
    hi = r * TO + TO + HL
    a, e = max(lo, 0), min(hi, L)
    zT[:, a - lo:e - lo] = zT_lat[:, b * L + a:b * L + e]
    xT[:, a - lo:e - lo] = xT_lat[:, b * L + a:b * L + e]
    cmask[:, a - lo:e - lo] = 1.0
    if has_ctx:
        lo = r * CO - 1
        hi = r * CO + CO + 1
        a, e = max(lo, 0), min(hi, Lc)
        zT[:, TL + a - lo:TL + e - lo] = zT_ctx[:, b * Lc + a:b * Lc + e]
        xT[:, TL + a - lo:TL + e - lo] = xT_ctx[:, b * Lc + a:b * Lc + e]
        cmask[:, TL + a - lo:TL + e - lo] = 1.0
    tok = (r * TO - HL) + np.arange(TL)
    gc = tok % GW
    mcl = np.broadcast_to((gc != GW - 1).astype(np.float32), (128, TL)).copy()
    mcr = np.broadcast_to((gc != 0).astype(np.float32), (128, TL)).copy()
    def modpack(c):
        return np.ascontiguousarray(np.stack([ptile(mods[c, k]) for k in (2, 4, 3, 5)], axis=1))
    nf = F // 128
    im = {
        "zT": zT, "xT": xT, "cmask": cmask, "mcl": mcl, "mcr": mcr,
        "modl": modpack(b), "modc": modpack(2),
        "lnp": np.ascontiguousarray(np.stack([ptile(v) for v in lnp], axis=1)),
        "bo": ptile(b_out) if b_out is not None else np.zeros((128, D // 128), np.float32),
        "wmix": np.ascontiguousarray(wmix, np.float32), "wup": np.ascontiguousarray(wup, np.float32),
        "wdw": np.ascontiguousarray(wdw.reshape(9, nf, 128).transpose(2, 1, 0), np.float32),
        "bdw": ptile(bdw), "wdown": np.ascontiguousarray(wdown, np.float32),
    }
    return im

def back_gather(cfg, outs, has_ctx):
    D, TO, CO, GW, B, L, Lc, NC = (cfg[k] for k in ("D", "TO", "CO", "GW", "B", "L", "Lc", "NC"))
    HL = GW + 1
    TL = TO + 2 * HL
    cpb = NC // B
    xl = np.zeros((D, B * L), np.float32)
    xc = np.zeros((D, B * Lc), np.float32) if has_ctx else None
    for j, o in enumerate(outs):
        b, r = j // cpb, j % cpb
        xl[:, b * L + r * TO:b * L + (r + 1) * TO] = o[:, HL:HL + TO]
        if has_ctx:
            xc[:, b * Lc + r * CO:b * Lc + (r + 1) * CO] = o[:, TL + 1:TL + 1 + CO]
    return xl, xc

def ada_inputs(D, NC, j, conds, w_adas, b_adas):
    ncols = 6 * D // NC
    cond = np.ascontiguousarray(np.stack([ptile(c) for c in conds], axis=2))
    im = {"cond": cond}
    bl = []
    for l, (w, b) in enumerate(zip(w_adas, b_adas)):
        im["w%d" % l] = np.ascontiguousarray(w[:, j * ncols:(j + 1) * ncols], np.float32)
        bl.append(ptile(b[j * ncols:(j + 1) * ncols]))
    im["b"] = np.ascontiguousarray(np.stack(bl, axis=1))
    return im

def ada_gather(D, NC, outs, nlayers):
    ncols = 6 * D // NC
    res = []
    for l in range(nlayers):
        m = np.zeros((3, 6 * D), np.float32)
        for j, o in enumerate(outs):
            blk = o[:, l].transpose(2, 1, 0).reshape(3, ncols)
            m[:, j * ncols:(j + 1) * ncols] = blk
        res.append(m.reshape(3, 6, D))
    return res

def hy_posfeat(Lx, emb=33):
    f32 = np.float32
    t = np.linspace(0.0, 1.0, Lx, dtype=f32)
    bands = (emb - 1) // 2
    ang = (f32(2.0 * np.pi) * np.arange(Lx, dtype=f32) / f32(Lx)).astype(f32)
    f = np.linspace(1e-4, bands - 1, bands, dtype=f32)
    fa = (f[None, :] * ang[:, None]).astype(f32)
    z = np.concatenate([t[:, None], np.cos(fa), -np.sin(fa)], axis=-1).astype(f32)
    pos = np.abs(np.arange(2 * Lx - 1) - (Lx - 1))
    pos = np.concatenate([pos, [0]])
    return np.ascontiguousarray(z[pos].T), t[pos]

def hy_deltas(D):
    import math
    max_decay = math.log(1e-2) / 0.3
    min_decay = math.log(1e-2) / 1.5
    return np.abs(np.linspace(min_decay, max_decay, D, dtype=np.float32))

def hyfront_inputs(cfg, j, xall, mods, P):
    D, B, L, Lc, NC = cfg["D"], cfg["B"], cfg["L"], cfg["Lc"], cfg["NC"]
    CS = D // NC
    sl = lambda part: slice(part * D + j * CS, part * D + (j + 1) * CS)
    cols = np.concatenate([np.arange(part * D + j * CS, part * D + (j + 1) * CS) for part in range(3)])
    modp = np.stack([np.stack([ptile(1.0 + mods[c, 1]), ptile(mods[c, 0])], axis=1) for c in range(3)], axis=1)
    wsh = np.zeros((CS // 64, 64, 3, 4), np.float32)
    for part in range(3):
        wpart = P["hy_w_short"][:, sl(part)]
        bpart = P["hy_b_short"][sl(part)]
        wsh[:, :, part, 0:3] = wpart.T.reshape(CS // 64, 64, 3)
        wsh[:, :, part, 3] = bpart.reshape(CS // 64, 64)
    zext, tt = hy_posfeat(L)
    zextc, ttc = hy_posfeat(Lc)
    fpv = np.zeros((64, 8), np.float32)
    for i in range(3):
        fpv[:, i] = P["hy_f_freq"]
    fpv[:, 3] = P["hy_f_b1"]; fpv[:, 4] = P["hy_f_b2"]; fpv[:, 5] = P["hy_f_b3"]
    wout = P["hy_f_wout"].reshape(64, 4, D)[:, :, j * CS:(j + 1) * CS]
    delt = hy_deltas(D)[j * CS:(j + 1) * CS]
    eye = np.eye(128, dtype=np.float32)
    im = {
        "xall": xall, "modp": np.ascontiguousarray(modp, np.float32),
        "win": np.ascontiguousarray(P["hy_w_in"][:, cols], np.float32), "bin": ptile(P["hy_b_in"][cols]),
        "wsh": wsh, "zext": zext, "zextc": zextc,
        "text": np.ascontiguousarray(np.broadcast_to(tt, (128, tt.size)), np.float32),
        "textc": np.ascontiguousarray(np.broadcast_to(ttc, (128, ttc.size)), np.float32),
        "w1": np.ascontiguousarray(P["hy_f_w1"], np.float32), "w2": np.ascontiguousarray(P["hy_f_w2"], np.float32),
        "w3": np.ascontiguousarray(P["hy_f_w3"], np.float32), "fp": fpv,
        "wout": np.ascontiguousarray(wout, np.float32), "negdelta": ptile(-delt),
        "skip": np.ascontiguousarray(P["hy_skip"][None, :, j * CS:(j + 1) * CS], np.float32),
        "ident": eye.astype(BF), "antiid": np.ascontiguousarray(eye[::-1]).astype(BF),
    }
    return im

def gdn_masks():
    j = np.arange(128)[:, None]; i = np.arange(128)[None, :]
    m = np.zeros((128, 6, 128), np.float32)
    m[:, 4] = ((i // 32) == (j // 32)); m[:, 5] = 1.0 - m[:, 4]
    m[:, 0] = (i >= j); m[:, 1] = -(i > j).astype(np.float32)
    m[:, 2] = (i <= j); m[:, 3] = -(i < j).astype(np.float32)
    return m

def gdnfront_inputs(cfg, j, xall, mods, P):
    D, B, L, Lc, NC = cfg["D"], cfg["B"], cfg["L"], cfg["Lc"], cfg["NC"]
    H = D // 128; HPC = H // NC; CS = HPC * 128
    heads = np.arange(j * HPC, (j + 1) * HPC)
    ch = np.concatenate([np.arange(h * 128, (h + 1) * 128) for h in heads])
    cols = np.concatenate([part * D + ch for part in range(4)])
    abcols = np.array([4 * D + ab * 2 * H + d * H + h for ab in range(2) for d in range(2) for h in heads])
    win = np.zeros((D, 4 * CS + 128), np.float32)
    win[:, :4 * CS] = P["gdn_w_in"][:, cols]
    win[:, 4 * CS:4 * CS + abcols.size] = P["gdn_w_in"][:, abcols]
    modp = np.stack([np.stack([ptile(1.0 + mods[c, 1]), ptile(mods[c, 0])], axis=1) for c in range(3)], axis=1)
    wcv = np.zeros((128, 3 * HPC, 3), np.float32)
    for part in range(3):
        for hl, h in enumerate(heads):
            wcv[:, part * HPC + hl, :] = P["gdn_w_conv"][:, part * D + h * 128: part * D + (h + 1) * 128].T
    abc = np.zeros((128, 2, 2 * HPC), np.float32)
    for d in range(2):
        for hl, h in enumerate(heads):
            abc[:, 0, d * HPC + hl] = P["gdn_dt_bias"][d, h]
            abc[:, 1, d * HPC + hl] = P["gdn_a_log"][d, h]
    return {"xall": xall, "modp": np.ascontiguousarray(modp, np.float32), "win": win, "wcv": wcv, "abc": abc,
            "normw": np.ascontiguousarray(np.broadcast_to(P["gdn_norm_w"].astype(np.float32), (128, 128))),
            "ident": np.eye(128, dtype=np.float32), "masks": gdn_masks()}

def fused_inputs(cfg, j, xall, conds, lay):
    D, F, B, L, Lc, NC, GW = cfg["D"], cfg["F"], cfg["B"], cfg["L"], cfg["Lc"], cfg["NC"], cfg["GW"]
    CS = D // NC
    ncs = CS // 128
    nfc = -(-(F // 128) // NC)
    FS = nfc * 128
    FP = FS * NC
    own = slice(j * CS, (j + 1) * CS)
    im = {"xall": xall, "xown": np.ascontiguousarray(xall[own])}
    sel = np.zeros((128, NC), np.float32); sel[:, j] = 1.0
    im["sel"] = sel
    a = ada_inputs(D, NC, j, conds, [lay[0]["w_ada"], lay[1]["w_ada"]], [lay[0]["b_ada"], lay[1]["b_ada"]])
    im["cond"] = a["cond"]; im["wada0"] = a["w0"]; im["wada1"] = a["w1"]; im["bada"] = a["b"]
    fp = np.arange(j * FS, (j + 1) * FS)
    valid = fp < F
    fpc = np.minimum(fp, F - 1)
    for l in range(2):
        P = lay[l]
        im["lnp%d" % l] = np.ascontiguousarray(np.stack([ptile(np.asarray(P[k])[own]) for k in ("ln1_g", "ln1_b", "ln2_g", "ln2_b")], axis=1))
        wm = P["hy_w_out"] if l == 0 else P["gdn_w_out"]
        im["wmix%d" % l] = np.ascontiguousarray(wm[:, own], np.float32)
        wu = np.zeros((D, 2 * FS), np.float32)
        wu[:, :FS][:, valid] = P["ffn_w_up"][:, fp[valid]]
        wu[:, FS:][:, valid] = P["ffn_w_up"][:, F + fp[valid]]
        im["wup%d" % l] = wu
        wd = np.zeros((9, FS), np.float32)
        wd[:, valid] = P["ffn_w_dw"].reshape(9, F)[:, fp[valid]]
        im["wdw%d" % l] = np.ascontiguousarray(wd.reshape(9, nfc, 128).transpose(2, 1, 0))
        bd = np.zeros((FS,), np.float32)
        bd[valid] = P["ffn_b_dw"][fp[valid]]
        im["bdw%d" % l] = ptile(bd)
        wdn = np.zeros((FP, CS), np.float32)
        wdn[:F] = P["ffn_w_down"][:, own]
        im["wdown%d" % l] = wdn
    im["bo0"] = ptile(np.asarray(lay[0]["hy_b_out"])[own])
    W_ = 2048 + 2 * (GW + 1)
    gc = (np.arange(W_) - (GW + 1)) % GW
    im["mcl"] = np.ascontiguousarray(np.broadcast_to((gc != GW - 1).astype(np.float32), (128, W_)))
    im["mcr"] = np.ascontiguousarray(np.broadcast_to((gc != 0).astype(np.float32), (128, W_)))
    dummy_mods = np.zeros((3, 6, D), np.float32)
    h = hyfront_inputs(cfg, j, xall, dummy_mods, lay[0])
    for k, v in h.items():
        if k not in ("xall", "modp"):
            im["hy_" + k] = v
    g = gdnfront_inputs(cfg, j, xall, dummy_mods, lay[1])
    for k, v in g.items():
        if k not in ("xall", "modp"):
            im["gd_" + k] = v
    return im


D_MODEL, BATCH, SEQ, DEPTH = 4096, 2, 8192, 2
CTX_LEN, D_FF, GRID_W, NCORES = 256, 11008, 64, 8
CFG = dict(D=D_MODEL, F=D_FF, TO=SEQ * BATCH // NCORES, CO=CTX_LEN * BATCH // NCORES, GW=GRID_W, B=BATCH, L=SEQ, Lc=CTX_LEN,
           NC=NCORES, alpha=(2 * DEPTH) ** 0.25, ln_eps=1e-5, rms_eps=1e-6, l2_eps=1e-6)
_PROGS = {}


def _prog(key, fn):
    if key not in _PROGS:
        _PROGS[key] = fn()
    return _PROGS[key]


def _run(nc, ims):
    res = run_bass_kernel_spmd(nc, ims, core_ids=list(range(NCORES)))
    return res.results


def kernel(x, c, ctx, c_ctx,
           l0_w_ada, l0_b_ada, l0_ln1_g, l0_ln1_b, l0_ln2_g, l0_ln2_b,
           l0_hy_w_in, l0_hy_b_in, l0_hy_w_short, l0_hy_b_short,
           l0_hy_f_w1, l0_hy_f_b1, l0_hy_f_w2, l0_hy_f_b2, l0_hy_f_w3, l0_hy_f_b3,
           l0_hy_f_wout, l0_hy_f_freq, l0_hy_skip, l0_hy_w_out, l0_hy_b_out,
           l0_ffn_w_up, l0_ffn_w_dw, l0_ffn_b_dw, l0_ffn_w_down,
           l1_w_ada, l1_b_ada, l1_ln1_g, l1_ln1_b, l1_ln2_g, l1_ln2_b,
           l1_gdn_w_in, l1_gdn_w_conv, l1_gdn_a_log, l1_gdn_dt_bias, l1_gdn_norm_w, l1_gdn_w_out,
           l1_ffn_w_up, l1_ffn_w_dw, l1_ffn_b_dw, l1_ffn_w_down):
    cfg = CFG
    D, B, L, Lc, NC = cfg["D"], cfg["B"], cfg["L"], cfg["Lc"], cfg["NC"]
    f32 = np.float32
    A = lambda v: np.asarray(v, f32)
    P0 = dict(w_ada=A(l0_w_ada), b_ada=A(l0_b_ada), ln1_g=A(l0_ln1_g), ln1_b=A(l0_ln1_b), ln2_g=A(l0_ln2_g), ln2_b=A(l0_ln2_b),
              hy_w_in=A(l0_hy_w_in), hy_b_in=A(l0_hy_b_in), hy_w_short=A(l0_hy_w_short), hy_b_short=A(l0_hy_b_short),
              hy_f_w1=A(l0_hy_f_w1), hy_f_b1=A(l0_hy_f_b1), hy_f_w2=A(l0_hy_f_w2), hy_f_b2=A(l0_hy_f_b2),
              hy_f_w3=A(l0_hy_f_w3), hy_f_b3=A(l0_hy_f_b3), hy_f_wout=A(l0_hy_f_wout), hy_f_freq=A(l0_hy_f_freq),
              hy_skip=A(l0_hy_skip), hy_w_out=A(l0_hy_w_out), hy_b_out=A(l0_hy_b_out),
              ffn_w_up=A(l0_ffn_w_up), ffn_w_dw=A(l0_ffn_w_dw), ffn_b_dw=A(l0_ffn_b_dw), ffn_w_down=A(l0_ffn_w_down))
    P1 = dict(w_ada=A(l1_w_ada), b_ada=A(l1_b_ada), ln1_g=A(l1_ln1_g), ln1_b=A(l1_ln1_b), ln2_g=A(l1_ln2_g), ln2_b=A(l1_ln2_b),
              gdn_w_in=A(l1_gdn_w_in), gdn_w_conv=A(l1_gdn_w_conv), gdn_a_log=A(l1_gdn_a_log),
              gdn_dt_bias=A(l1_gdn_dt_bias), gdn_norm_w=A(l1_gdn_norm_w), gdn_w_out=A(l1_gdn_w_out),
              ffn_w_up=A(l1_ffn_w_up), ffn_w_dw=A(l1_ffn_w_dw), ffn_b_dw=A(l1_ffn_b_dw), ffn_w_down=A(l1_ffn_w_down))
    conds = np.concatenate([A(c), A(c_ctx)[None, :]], axis=0)
    nc_a = _prog("ada", lambda: build_ada(D, 6 * D // NC, DEPTH))
    outs = _run(nc_a, [ada_inputs(D, NC, j, conds, [P0["w_ada"], P1["w_ada"]], [P0["b_ada"], P1["b_ada"]]) for j in range(NC)])
    mods = ada_gather(D, NC, [o["out"] for o in outs], DEPTH)
    xT = np.ascontiguousarray(A(x).reshape(B * L, D).T)
    cT = np.ascontiguousarray(A(ctx).reshape(B * Lc, D).T)
    xall = np.ascontiguousarray(np.concatenate([xT, cT], axis=1))
    nc_b = _prog("hy", lambda: build_hyfront(cfg))
    outs = _run(nc_b, [hyfront_inputs(cfg, j, xall, mods[0], P0) for j in range(NC)])
    zT = np.concatenate([o["zout"] for o in outs], axis=0)
    del outs
    lnp0 = [P0["ln1_g"], P0["ln1_b"], P0["ln2_g"], P0["ln2_b"]]
    nc_c0 = _prog("back0", lambda: build_back(cfg, True, True))
    outs = _run(nc_c0, [back_inputs(cfg, j, True, zT[:, :B * L], xT, zT[:, B * L:], cT, mods[0], lnp0, P0["hy_b_out"], P0["hy_w_out"],
                                    P0["ffn_w_up"], P0["ffn_w_dw"], P0["ffn_b_dw"], P0["ffn_w_down"]) for j in range(NC)])
    x1T, c1T = back_gather(cfg, [o["out"] for o in outs], True)
    del outs, zT, xall
    xall = np.ascontiguousarray(np.concatenate([x1T, c1T], axis=1))
    nc_d = _prog("gdn", lambda: build_gdnfront(cfg))
    outs = _run(nc_d, [gdnfront_inputs(cfg, j, xall, mods[1], P1) for j in range(NC)])
    oT = np.ascontiguousarray(np.concatenate([o["oout"] for o in outs], axis=1).T)
    del outs, xall
    lnp1 = [P1["ln1_g"], P1["ln1_b"], P1["ln2_g"], P1["ln2_b"]]
    nc_c1 = _prog("back1", lambda: build_back(cfg, False, False))
    outs = _run(nc_c1, [back_inputs(cfg, j, False, oT, x1T, None, None, mods[1], lnp1, None, P1["gdn_w_out"],
                                    P1["ffn_w_up"], P1["ffn_w_dw"], P1["ffn_b_dw"], P1["ffn_w_down"]) for j in range(NC)])
    x2T, _ = back_gather(cfg, [o["out"] for o in outs], False)
    return np.ascontiguousarray(x2T.T).reshape(B, L, D).astype(f32)
```
